# Optimizing a Trainium2 kernel written in Bass

```python
import math
import jax, jax.numpy as jnp
from jax import lax
import numpy as np

D_MODEL = 2048
BATCH = 4
SEQ = 4096
DEPTH = 1

CTX_LEN = 256
GRID_W = 64

HY_WIDTH = 1024
HY_ORDER = 2
HY_SHORT = 3
HY_BANDS = 16
HY_EMB_DIM = 1 + 2 * HY_BANDS
HY_FILTER_WIDTH = 64
HY_DECAY_TARGET = 1e-2
HY_DECAY_PCT_SLOW = 1.5
HY_DECAY_PCT_QUICK = 0.3

GDN_HEADS = 8
GDN_HEAD_DIM = 128
GDN_WIDTH = GDN_HEADS * GDN_HEAD_DIM
GDN_SHORT = 3
CHUNK = 64

D_FF = 5632
N_MOD = 9
EPS = 1e-6

HY_COLS = 3 * HY_WIDTH
Q_OFF = HY_COLS
KV_OFF = Q_OFF + GDN_WIDTH
Z_OFF = KV_OFF + 2 * GDN_WIDTH
AB_OFF = Z_OFF + GDN_WIDTH
GATE_OFF = AB_OFF + 4 * GDN_HEADS
IN_COLS = GATE_OFF + 2 * D_MODEL

kernel_name = "hyena_gdn_macaron_prefix_block"


def rms_norm(x, gain):
    xf = x.astype(jnp.float32)
    y = xf * lax.rsqrt(jnp.mean(xf * xf, -1, keepdims=True) + EPS)
    return (y * gain).astype(x.dtype)


def ada(xn, m, j):
    return xn * (1 + m[:, :, 3 * j + 1]) + m[:, :, 3 * j], m[:, :, 3 * j + 2]


def swiglu(x, w_gu, w_down):
    g, u = jnp.split(x @ w_gu, 2, -1)
    return (jax.nn.silu(g) * u) @ w_down


def ffn_half(h, m, j, gain, w_gu, w_down):
    xn, gate = ada(rms_norm(h, gain), m, j)
    return h + 0.5 * gate * swiglu(xn, w_gu, w_down)


def short_conv(x, w, rows):
    b_, L, C = x.shape
    K = w.shape[0]
    pad = K // 2
    n = L // rows
    xp = jnp.pad(x.reshape(b_, rows, n, C), ((0, 0), (0, 0), (pad, pad), (0, 0)))
    y = sum(xp[:, :, j:j + n] * w[j] for j in range(K))
    return y.reshape(b_, L, C)


def hyena_filters(L, w1, b1, w2, b2, w3, sin_freq):
    pos = jnp.arange(L, dtype=jnp.float32)
    t = (pos / L)[:, None]
    f = jnp.linspace(1e-4, HY_BANDS - 1, HY_BANDS, dtype=jnp.float32)
    ang = 2 * math.pi * t * f
    z = jnp.concatenate([t, jnp.cos(ang), -jnp.sin(ang)], -1)
    h = jnp.sin(sin_freq * (z @ w1 + b1))
    h = jnp.sin(sin_freq * (h @ w2 + b2))
    h = (h @ w3).reshape(L, HY_ORDER, 2, HY_WIDTH)
    deltas = jnp.abs(jnp.linspace(math.log(HY_DECAY_TARGET) / HY_DECAY_PCT_SLOW,
                                  math.log(HY_DECAY_TARGET) / HY_DECAY_PCT_QUICK,
                                  HY_WIDTH, dtype=jnp.float32))
    h = h * jnp.exp(-t[:, :, None, None] * deltas)
    return h / jnp.sum(jnp.abs(h), axis=(0, 2), keepdims=True)


def long_conv(z, h_fwd, h_bwd, bias):
    _, L, C = z.shape
    k = jnp.concatenate([h_fwd, jnp.zeros((1, C), h_fwd.dtype), h_bwd[:0:-1]], 0)
    zf = jnp.fft.rfft(z.astype(jnp.float32), n=2 * L, axis=1)
    kf = jnp.fft.rfft(k.astype(jnp.float32), n=2 * L, axis=0)
    y = jnp.fft.irfft(zf * kf, n=2 * L, axis=1)[:, :L]
    return (y + z * bias).astype(z.dtype)


def hyena_branch(v, x1, x2, filt, bias):
    z = v
    for o, gate in enumerate((x1, x2)):
        z = gate * long_conv(z, filt[:, o, 0], filt[:, o, 1], bias[o])
    return z


def split_heads(t):
    return t.reshape(*t.shape[:-1], GDN_HEADS, GDN_HEAD_DIM)


def l2norm(x):
    x = x.astype(jnp.float32)
    return x * lax.rsqrt(jnp.sum(x * x, -1, keepdims=True) + 1e-6)


def gdn_query(q_proj, conv_w, rows):
    q = jax.nn.silu(short_conv(q_proj, conv_w, rows))
    return l2norm(split_heads(q)) * (GDN_HEAD_DIM ** -0.5)


def gdn_keys_values(kv_proj, conv_w, rows):
    kv = jax.nn.silu(short_conv(kv_proj, conv_w, rows))
    k, v = jnp.split(kv, 2, -1)
    return l2norm(split_heads(k)), split_heads(v).astype(jnp.float32)


def gdn_decay_beta(ab_proj, a_log, dt_bias):
    ab = ab_proj.astype(jnp.float32).reshape(*ab_proj.shape[:-1], 2, 2, GDN_HEADS)
    g = -jnp.exp(a_log) * jax.nn.softplus(ab[..., 0, :, :] + dt_bias)
    beta = jax.nn.sigmoid(ab[..., 1, :, :])
    return g, beta


def gated_delta_chunked(k, v, g, beta, s0, q=None):
    b_, L, H, _ = k.shape
    n = L // CHUNK

    def chunks(t):
        t = t.astype(jnp.float32).reshape(b_, n, CHUNK, H, *t.shape[3:])
        return jnp.moveaxis(t, (1, 3), (0, 2))

    kc, vc, gc, bc = chunks(k), chunks(v), chunks(g), chunks(beta)
    gcum = jnp.cumsum(gc, -1)
    idx = jnp.arange(CHUNK)
    lower = idx[:, None] >= idx[None, :]
    decay = jnp.exp(jnp.where(lower, gcum[..., :, None] - gcum[..., None, :], -jnp.inf))
    kb = kc * bc[..., None]
    a = jnp.where(idx[:, None] > idx[None, :], jnp.einsum('nbhcd,nbhed->nbhce', kb, kc) * decay, 0.0)
    eye = jnp.eye(CHUNK, dtype=jnp.float32)
    t_inv = lax.linalg.triangular_solve(eye + a, jnp.broadcast_to(eye, a.shape),
                                        left_side=True, lower=True, unit_diagonal=True)
    u = t_inv @ (vc * bc[..., None])
    w = t_inv @ (kb * jnp.exp(gcum)[..., None])
    g_last = gcum[..., -1]
    k_dec = kc * jnp.exp(g_last[..., None] - gcum)[..., None]
    xs = (u, w, k_dec, g_last)
    if q is not None:
        qc = chunks(q)
        xs = xs + (qc * jnp.exp(gcum)[..., None], jnp.einsum('nbhcd,nbhed->nbhce', qc, kc) * decay)

    def step(S, inp):
        u_i, w_i, kd_i, gl_i = inp[:4]
        v_new = u_i - w_i @ S
        S_new = S * jnp.exp(gl_i)[..., None, None] + jnp.swapaxes(kd_i, -1, -2) @ v_new
        if q is None:
            return S_new, None
        qd_i, qk_i = inp[4:]
        return S_new, qd_i @ S + qk_i @ v_new

    s_fin, o = lax.scan(step, s0.astype(jnp.float32), xs)
    if q is None:
        return s_fin, None
    return s_fin, jnp.moveaxis(o, (0, 2), (1, 3)).reshape(b_, L, H, -1)


def gdn_bidir(k, v, g, beta, s0, q=None):
    rev = lambda t: None if t is None else jnp.flip(t, 1)
    s_f, o_f = gated_delta_chunked(k, v, g[:, :, 0], beta[:, :, 0], s0[0], q)
    s_b, o_b = gated_delta_chunked(rev(k), rev(v), rev(g[:, :, 1]), rev(beta[:, :, 1]), s0[1], rev(q))
    states = jnp.stack([s_f, s_b])
    return states, (None if q is None else o_f + rev(o_b))


def zero_states(b_):
    return jnp.zeros((2, b_, GDN_HEADS, GDN_HEAD_DIM, GDN_HEAD_DIM), jnp.float32)


def context_states(uc, lp):
    w, cw = lp['w_in'], lp['gdn_conv_w']
    k, v = gdn_keys_values(uc @ w[:, KV_OFF:Z_OFF], cw[:, GDN_WIDTH:], 1)
    g, beta = gdn_decay_beta(uc @ w[:, AB_OFF:GATE_OFF], lp['gdn_a_log'], lp['gdn_dt_bias'])
    states, _ = gdn_bidir(k, v, g, beta, zero_states(uc.shape[0]))
    return states


def token_mixer(u, rows, s0, lp):
    proj = u @ lp['w_in']
    hy = short_conv(proj[..., :HY_COLS], lp['hy_conv_w'], rows) + lp['hy_conv_b']
    hv, hx1, hx2 = jnp.split(hy, 3, -1)
    filt = hyena_filters(u.shape[1], lp['hy_f_w1'], lp['hy_f_b1'], lp['hy_f_w2'], lp['hy_f_b2'],
                         lp['hy_f_w3'], lp['hy_sin_freq'])
    y_hy = hyena_branch(hv, hx1, hx2, filt, lp['hy_bias']) @ lp['hy_out']
    cw = lp['gdn_conv_w']
    q = gdn_query(proj[..., Q_OFF:KV_OFF], cw[:, :GDN_WIDTH], rows)
    k, v = gdn_keys_values(proj[..., KV_OFF:Z_OFF], cw[:, GDN_WIDTH:], rows)
    g, beta = gdn_decay_beta(proj[..., AB_OFF:GATE_OFF], lp['gdn_a_log'], lp['gdn_dt_bias'])
    states, o = gdn_bidir(k, v, g, beta, s0, q)
    zg = split_heads(proj[..., Z_OFF:AB_OFF]).astype(jnp.float32)
    o = rms_norm(o, lp['gdn_norm_g']) * jax.nn.silu(zg)
    y_gdn = o.reshape(*u.shape[:2], GDN_WIDTH) @ lp['gdn_out']
    gates = jax.nn.sigmoid(proj[..., GATE_OFF:].astype(jnp.float32))
    merged = gates[..., :D_MODEL] * y_hy + gates[..., D_MODEL:] * y_gdn
    return merged @ lp['w_o'], states


def setup_inputs(seed: int = 0) -> dict:
    key = jax.random.key(seed)
    ks = iter(jax.random.split(key, 40))
    nrm = lambda shape, scale: jax.random.normal(next(ks), shape, jnp.float32) * scale
    H = GDN_HEADS
    dt = jnp.exp(jax.random.uniform(next(ks), (DEPTH, 2, H), jnp.float32,
                                    minval=math.log(1e-3), maxval=math.log(1e-1)))
    dt_bias = dt + jnp.log(-jnp.expm1(-dt))
    a_log = jnp.log(jax.random.uniform(next(ks), (DEPTH, 2, H), jnp.float32, minval=1.0, maxval=16.0))
    return {
        "x": nrm((BATCH, SEQ, D_MODEL), 1.0),
        "c": nrm((BATCH, D_MODEL), 1.0),
        "ctx": nrm((BATCH, CTX_LEN, D_MODEL), 1.0),
        "c_ctx": nrm((D_MODEL,), 1.0),
        "w_mod": nrm((DEPTH, D_MODEL, N_MOD * D_MODEL), 0.5 * D_MODEL ** -0.5),
        "b_mod": nrm((DEPTH, N_MOD * D_MODEL), 0.01),
        "norm_g": 1.0 + nrm((DEPTH, 3, D_MODEL), 0.02),
        "ffn1_wgu": nrm((DEPTH, D_MODEL, 2 * D_FF), D_MODEL ** -0.5),
        "ffn1_wd": nrm((DEPTH, D_FF, D_MODEL), D_FF ** -0.5),
        "ffn2_wgu": nrm((DEPTH, D_MODEL, 2 * D_FF), D_MODEL ** -0.5),
        "ffn2_wd": nrm((DEPTH, D_FF, D_MODEL), D_FF ** -0.5),
        "w_in": nrm((DEPTH, D_MODEL, IN_COLS), D_MODEL ** -0.5),
        "hy_conv_w": nrm((DEPTH, HY_SHORT, HY_COLS), HY_SHORT ** -0.5),
        "hy_conv_b": nrm((DEPTH, HY_COLS), 0.02),
        "hy_f_w1": nrm((DEPTH, HY_EMB_DIM, HY_FILTER_WIDTH), HY_EMB_DIM ** -0.5),
        "hy_f_b1": nrm((DEPTH, HY_FILTER_WIDTH), 0.1),
        "hy_f_w2": nrm((DEPTH, HY_FILTER_WIDTH, HY_FILTER_WIDTH), HY_FILTER_WIDTH ** -0.5),
        "hy_f_b2": nrm((DEPTH, HY_FILTER_WIDTH), 0.1),
        "hy_f_w3": nrm((DEPTH, HY_FILTER_WIDTH, HY_ORDER * 2 * HY_WIDTH), HY_FILTER_WIDTH ** -0.5),
        "hy_sin_freq": 1.0 + nrm((DEPTH, HY_FILTER_WIDTH), 0.1),
        "hy_bias": nrm((DEPTH, HY_ORDER, HY_WIDTH), 1.0),
        "hy_out": nrm((DEPTH, HY_WIDTH, D_MODEL), HY_WIDTH ** -0.5),
        "gdn_conv_w": nrm((DEPTH, GDN_SHORT, 3 * GDN_WIDTH), GDN_SHORT ** -0.5),
        "gdn_a_log": a_log,
        "gdn_dt_bias": dt_bias,
        "gdn_norm_g": 1.0 + nrm((DEPTH, GDN_HEAD_DIM), 0.02),
        "gdn_out": nrm((DEPTH, GDN_WIDTH, D_MODEL), GDN_WIDTH ** -0.5),
        "w_o": nrm((DEPTH, D_MODEL, D_MODEL), D_MODEL ** -0.5),
        "final_norm_g": 1.0 + nrm((D_MODEL,), 0.02),
    }


def reference(x, c, ctx, c_ctx, w_mod, b_mod, norm_g, ffn1_wgu, ffn1_wd, ffn2_wgu, ffn2_wd, w_in,
              hy_conv_w, hy_conv_b, hy_f_w1, hy_f_b1, hy_f_w2, hy_f_b2, hy_f_w3, hy_sin_freq, hy_bias,
              hy_out, gdn_conv_w, gdn_a_log, gdn_dt_bias, gdn_norm_g, gdn_out, w_o, final_norm_g):
    rows = x.shape[1] // GRID_W
    h, hc = x, ctx
    s_c = jax.nn.silu(c.astype(jnp.float32))
    s_cc = jax.nn.silu(c_ctx.astype(jnp.float32))
    for i in range(DEPTH):
        last = i == DEPTH - 1
        lp = {
            'w_in': w_in[i], 'hy_conv_w': hy_conv_w[i], 'hy_conv_b': hy_conv_b[i],
            'hy_f_w1': hy_f_w1[i], 'hy_f_b1': hy_f_b1[i], 'hy_f_w2': hy_f_w2[i], 'hy_f_b2': hy_f_b2[i],
            'hy_f_w3': hy_f_w3[i], 'hy_sin_freq': hy_sin_freq[i], 'hy_bias': hy_bias[i],
            'hy_out': hy_out[i], 'gdn_conv_w': gdn_conv_w[i], 'gdn_a_log': gdn_a_log[i],
            'gdn_dt_bias': gdn_dt_bias[i], 'gdn_norm_g': gdn_norm_g[i], 'gdn_out': gdn_out[i],
            'w_o': w_o[i],
        }
        mod = (s_c @ w_mod[i] + b_mod[i]).reshape(-1, 1, N_MOD, D_MODEL)
        mod_c = (s_cc @ w_mod[i] + b_mod[i]).reshape(1, 1, N_MOD, D_MODEL)
        h = ffn_half(h, mod, 0, norm_g[i, 0], ffn1_wgu[i], ffn1_wd[i])
        hc = ffn_half(hc, mod_c, 0, norm_g[i, 0], ffn1_wgu[i], ffn1_wd[i])
        uc, gate_c = ada(rms_norm(hc, norm_g[i, 1]), mod_c, 1)
        if last:
            s_ctx = context_states(uc, lp)
        else:
            y_c, s_ctx = token_mixer(uc, 1, zero_states(uc.shape[0]), lp)
            hc = hc + gate_c * y_c
        u, gate = ada(rms_norm(h, norm_g[i, 1]), mod, 1)
        y, _ = token_mixer(u, rows, s_ctx, lp)
        h = h + gate * y
        h = ffn_half(h, mod, 2, norm_g[i, 2], ffn2_wgu[i], ffn2_wd[i])
        if not last:
            hc = ffn_half(hc, mod_c, 2, norm_g[i, 2], ffn2_wgu[i], ffn2_wd[i])
    return rms_norm(h, final_norm_g)
```

```python
import math
import contextlib
import numpy as np
import concourse.bass as bass
import concourse.mybir as mybir
from concourse.bass_utils import run_bass_kernel_spmd

F32 = mybir.dt.float32
BF16 = mybir.dt.bfloat16
AF = mybir.ActivationFunctionType
ALU = mybir.AluOpType
AX = mybir.AxisListType

EPOCH = 4096
ENGS = ("tensor", "vector", "scalar", "gpsimd", "sync")
N_DMA_SEMS = 12


class Op:
    __slots__ = ("eng", "fn", "deps", "idx", "needs_inc", "dma", "dma_tok", "dma_prev", "inc_no")

    def __init__(self, eng, fn):
        self.eng = eng
        self.fn = fn
        self.deps = []
        self.idx = None
        self.needs_inc = False
        self.dma = False
        self.dma_tok = None
        self.dma_prev = None


class Sched:
    def __init__(self, nc):
        self.nc = nc
        self.ops = {e: [] for e in ENGS}
        self.last_w = {}
        self.readers = {}
        self.dma_rr = {e: 0 for e in ENGS}
        self.dma_cnt = {}
        self.dma_last = {}
        self.n_ops = 0
        self.bar_ops = []
        self.bar_gen = 0
        self.bar_applied = {e: 0 for e in ENGS}

    def barrier(self):
        self.bar_gen += 1
        b = []
        for e in ENGS:
            for op in reversed(self.ops[e]):
                if not op.dma:
                    b.append(op)
                    break
        b.extend(self.dma_last.values())
        self.bar_ops = b

    def _add(self, eng, fn, reads, writes, dma=False, pe_acc=False):
        op = Op(eng, fn)
        op.dma = dma
        deps = []
        for k in reads:
            w = self.last_w.get(k)
            if w is not None:
                deps.append(w)
        for k in writes:
            w = self.last_w.get(k)
            if w is not None:
                deps.append(w)
            for r in self.readers.get(k, ()):
                deps.append(r)
        if self.bar_applied[eng] != self.bar_gen:
            deps.extend(self.bar_ops)
            self.bar_applied[eng] = self.bar_gen
        seen = set()
        for d in deps:
            if d is op or id(d) in seen:
                continue
            seen.add(id(d))
            if (not d.dma) and (not dma) and d.eng == "tensor" and eng == "tensor":
                continue
            op.deps.append(d)
        for k in reads:
            self.readers.setdefault(k, []).append(op)
        for k in writes:
            self.last_w[k] = op
            self.readers[k] = []
        op.idx = len(self.ops[eng])
        self.ops[eng].append(op)
        if dma:
            slot = self.dma_rr[eng] % N_DMA_SEMS
            self.dma_rr[eng] += 1
            key = (eng, slot)
            cnt = self.dma_cnt.get(key, 0) + 1
            self.dma_cnt[key] = cnt
            op.dma_tok = (key, cnt * 16)
            op.dma_prev = self.dma_last.get(key)
            self.dma_last[key] = op
        self.n_ops += 1
        return op

    def pe(self, fn, reads=(), writes=()):
        return self._add("tensor", fn, reads, writes)

    def dve(self, fn, reads=(), writes=()):
        return self._add("vector", fn, reads, writes)

    def act(self, fn, reads=(), writes=()):
        return self._add("scalar", fn, reads, writes)

    def pool(self, fn, reads=(), writes=()):
        return self._add("gpsimd", fn, reads, writes)

    def dma(self, out, in_, reads=(), writes=(), eng="sync"):
        return self._add(eng, lambda e: e.dma_start(out=out, in_=in_), reads, writes, dma=True)

    def dma_cast(self, out, in_, reads=(), writes=()):
        return self.dma(out, in_, reads, writes, eng="gpsimd")

    def emit(self, final_keys=()):
        nc = self.nc
        fin = Op("sync", None)
        for k in final_keys:
            w = self.last_w.get(k)
            if w is not None:
                fin.deps.append(w)
        for e in ENGS:
            for op in self.ops[e]:
                for d in op.deps:
                    if not d.dma:
                        d.needs_inc = True
        for d in fin.deps:
            if not d.dma:
                d.needs_inc = True
        n_inc = {}
        for e in ENGS:
            c = 0
            for op in self.ops[e]:
                if (not op.dma) and op.needs_inc:
                    op.inc_no = c
                    c += 1
            n_inc[e] = c
        import contextlib
        stack = contextlib.ExitStack()
        sems = {}
        with stack:
            for e in ENGS:
                n = n_inc[e]
                for ep in range((n + EPOCH - 1) // EPOCH + 1):
                    sems[(e, ep)] = stack.enter_context(nc.semaphore(f"p_{e}_{ep}"))
            for key in self.dma_cnt:
                sems[("dma",) + key] = stack.enter_context(nc.semaphore(f"d_{key[0]}_{key[1]}"))
            block = stack.enter_context(nc.Block())

            def tok(d):
                if d.dma:
                    return (sems[("dma",) + d.dma_tok[0]], d.dma_tok[1], ("dma",) + d.dma_tok[0])
                ep, r = divmod(d.inc_no, EPOCH)
                return (sems[(d.eng, ep)], r + 1, (d.eng, ep))

            def run(engname):
                def body(eng):
                    waited = {}
                    def do_wait(d):
                        s, v, key = tok(d)
                        if waited.get(key, 0) >= v:
                            return
                        waited[key] = v
                        eng.wait_ge(s, v)
                    for op in self.ops[engname]:
                        for d in op.deps:
                            if (not d.dma) and d.eng == engname and d.idx >= op.idx:
                                continue
                            do_wait(d)
                        if op.dma and op.dma_prev is not None:
                            do_wait(op.dma_prev)
                        ins = op.fn(eng)
                        if op.dma:
                            s, v, _ = tok(op)
                            ins.then_inc(s, 16)
                        elif op.needs_inc:
                            ep, _r = divmod(op.inc_no, EPOCH)
                            ins.then_inc(sems[(engname, ep)], 1)
                    if engname == "sync":
                        for d in fin.deps:
                            do_wait(d)
                return body

            block.tensor(run("tensor"))
            block.vector(run("vector"))
            block.scalar(run("scalar"))
            block.gpsimd(run("gpsimd"))
            block.sync(run("sync"))

D = 2048; KC = 16; DFF = 5632; HC = 44; L = 4096; LC = 256; NT = 512
HYW = 1024; GW = 1024; NH = 8
HY_COLS = 3072; Q_OFF = 3072; KV_OFF = 4096; Z_OFF = 6144; AB_OFF = 7168; GATE_OFF = 7200; IN_COLS = 11296
EPS = 1e-6
LTOT = L + LC


class Ctx:
    pass


def build_program(nc, dbg=False, stages=("A", "F", "H", "G", "T"), ext=()):
    S = Sched(nc)
    C = Ctx()
    C.nc = nc; C.S = S
    din = {}
    def inp(name, shape, dt=F32):
        din[name] = nc.dram_tensor(name, list(shape), dt, kind="ExternalInput").ap()
        return din[name]
    def scratch(name, shape, dt=F32):
        if name in ext:
            return inp(name, shape, dt)
        return nc.dram_tensor(name, list(shape), dt, kind="Internal").ap()
    xT = inp("xT", [D, L]); cT = inp("ctxT", [D, LC]); sT = inp("sT", [128, KC, 2])
    w_mod = inp("w_mod", [D, 9 * D]); b_mod = inp("b_mod", [128, 9 * KC])
    norm_g = inp("norm_g", [128, 3 * KC]); fin_g = inp("fin_g", [128, KC])
    wgu = [inp("ffn1_wgu", [D, 2 * DFF]), inp("ffn2_wgu", [D, 2 * DFF])]
    wd = [inp("ffn1_wd", [DFF, D]), inp("ffn2_wd", [DFF, D])]
    w_in = inp("w_in", [D, IN_COLS])
    hy_cw = inp("hy_cw", [128, 24, 3]); hy_cb = inp("hy_cb", [128, 24]); g_cw = inp("g_cw", [128, 24, 3])
    hy_out = inp("hy_out", [HYW, D]); gdn_out = inp("gdn_out", [GW, D]); w_o = inp("w_o", [D, D])
    zT_d = inp("zT", [33, L]); fw1 = inp("f_w1", [33, 64]); fw2 = inp("f_w2", [64, 64]); fw3 = inp("f_w3", [64, 4096])
    fb1 = inp("f_b1", [64, 1]); fb2 = inp("f_b2", [64, 1]); ffr = inp("f_freq", [64, 1])
    deltas_d = inp("deltas", [1, 1024]); that_d = inp("that", [128, 32]); mb0_d = inp("mb0", [128, 1])
    CFt = inp("CFt", [32, 128, 32, 128], BF16); SFt = inp("SFt", [32, 128, 32, 128], BF16)
    CIt = inp("CIt", [32, 128, 32, 128], BF16); SIt = inp("SIt", [32, 128, 32, 128], BF16)
    hyb_d = inp("hy_bias", [2, 1024])
    tri_d = inp("tri", [128, 4, 128]); alog_d = inp("a_log", [1, 16]); dtb_d = inp("dt_bias", [1, 16]); gng_d = inp("gdn_ng", [128, 1])
    out_d = nc.dram_tensor("outT", [D, L], F32, kind="ExternalOutput").ap()
    C.din = din
    H1T = scratch("H1T", [D, L]); HYT = scratch("HYT", [L, HY_COLS], BF16)
    QT = scratch("QT", [GW, LTOT]); KT = scratch("KT", [GW, LTOT]); VT = scratch("VT", [GW, LTOT])
    ZG = scratch("ZG", [GW, L], BF16); ABs = scratch("ABs", [LTOT, 32])
    ZH = scratch("ZH", [HYW, L], BF16); OG = scratch("OG", [GW, L], BF16)
    OFs = scratch("OFs", [L, 8, 128])
    KCs = scratch("KCs", [L, 2048]); KSs = scratch("KSs", [L, 2048])
    Z2T = scratch("Z2T", [L, 1024], BF16); Ps = scratch("Ps", [L, 1024], BF16); Qs = scratch("Qs", [L, 1024], BF16)
    C.scr = dict(KCs=KCs, KSs=KSs, Z2T=Z2T, H1T=H1T, HYT=HYT, QT=QT, KT=KT, VT=VT, ZG=ZG, ABs=ABs, ZH=ZH, OG=OG)
    dbg_outs = {}
    C.dbg_outs = dbg_outs

    stack = contextlib.ExitStack()
    with stack:
        def sb(name, shape, dt=F32):
            return stack.enter_context(nc.sbuf_tensor(name, list(shape), dt)).ap()
        PS = [nc.alloc_psum_tensor(f"psb{i}", [128, 512], F32).ap() for i in range(8)]
        C.PS = PS
        C.ps_rr = 0
        def next_ps():
            i = C.ps_rr % 6
            C.ps_rr += 1
            return PS[i], f"PS{i}"
        C.next_ps = next_ps
        ident = sb("ident", [128, 128]); identb = sb("identb", [128, 128], BF16)
        onesb = sb("onesb", [128, 128], BF16); onesf = sb("onesf", [128, 128])
        S.pool(lambda e: e.memset(ident, 1.0), writes=["ident"])
        S.pool(lambda e: e.affine_select(ident, ident, pattern=[[-1, 128]], compare_op=ALU.is_equal, fill=0.0, base=0, channel_multiplier=1), reads=["ident"], writes=["ident"])
        S.dve(lambda e: e.tensor_copy(identb, ident), reads=["ident"], writes=["identb"])
        S.pool(lambda e: e.memset(onesf, 1.0), writes=["onesf"])
        S.dve(lambda e: e.memset(onesb, 1.0), writes=["onesb"])
        C.ident = ident; C.identb = identb; C.onesb = onesb; C.onesf = onesf
        modv = sb("modv", [128, 9 * KC, 2]); sTs = sb("sTs", [128, KC, 2]); sTb = sb("sTb", [128, KC, 2], BF16)
        bmod = sb("bmod", [128, 9 * KC]); ng = sb("ng", [128, 3 * KC]); fg = sb("fg", [128, KC])
        Av = sb("Av", [128, 3, KC, 2]); Bv = sb("Bv", [128, 3, KC, 2]); Gv = sb("Gv", [128, 3, KC, 2])
        S.dma(sTs, sT, writes=["sTs"]); S.dma(bmod, b_mod, writes=["bmod"]); S.dma(ng, norm_g, writes=["ng"]); S.dma(fg, fin_g, writes=["fg"])
        S.act(lambda e: e.activation(sTb, sTs, AF.Silu), reads=["sTs"], writes=["sTb"])
        C.wb_rr = 0
        def next_wb():
            i = C.wb_rr % 3
            C.wb_rr += 1
            return C.WB[i], f"WB{i}"
        C.next_wb = next_wb
        def alloc_main(st):
            def sbl(name, shape, dt=F32):
                return st.enter_context(nc.sbuf_tensor(name, list(shape), dt)).ap()
            C.gen = getattr(C, "gen", 0) + 1
            g = C.gen
            C.WB = [sbl(f"wb{i}_{g}", [128, 8192], BF16) for i in range(3)]
            C.H = sbl(f"H_{g}", [128, KC, NT]); C.XM = sbl(f"XM_{g}", [128, KC, NT], BF16); C.ACT = sbl(f"ACT_{g}", [128, HC, NT], BF16)
            C.TMP = [sbl(f"tmp{i}_{g}", [128, NT]) for i in range(6)]
            return sbl
        st_mod = contextlib.ExitStack()
        st_mod.__enter__()
        C.WB = [st_mod.enter_context(nc.sbuf_tensor(f"wbm{i}", [128, 8192], BF16)).ap() for i in range(3)]
        for cb in range(9 * D // 512):
            wbuf, wkey = next_wb()
            wv = wbuf.rearrange("p (k n) -> p k n", k=KC)
            S.dma_cast(wv, w_mod[:, cb * 512:(cb + 1) * 512].rearrange("(k p) n -> p k n", p=128), writes=[wkey])
            for j in range(4):
                ps, pk = next_ps()
                for k in range(KC):
                    S.pe(lambda e, ps=ps, wv=wv, k=k, j=j: e.matmul(ps[:, 0:2], lhsT=wv[:, k, j * 128:(j + 1) * 128], rhs=sTb[:, k, :], start=(k == 0), stop=(k == KC - 1)),
                         reads=[wkey, "sTb"], writes=[pk])
                col = cb * 4 + j
                S.dve(lambda e, ps=ps, col=col: e.tensor_scalar(modv[:, col, :], ps[:, 0:2], bmod[:, col:col + 1], None, op0=ALU.add),
                      reads=[pk, "bmod"], writes=["modv"])
        for j in range(3):
            for r in range(2):
                sh = modv[:, (3 * j) * KC:(3 * j + 1) * KC, r]; sc = modv[:, (3 * j + 1) * KC:(3 * j + 2) * KC, r]; gt = modv[:, (3 * j + 2) * KC:(3 * j + 3) * KC, r]
                S.dve(lambda e, sc=sc, j=j, r=r: e.scalar_tensor_tensor(Av[:, j, :, r], in0=sc, scalar=1.0, in1=ng[:, j * KC:(j + 1) * KC], op0=ALU.add, op1=ALU.mult),
                      reads=["modv", "ng"], writes=["Av"])
                S.dve(lambda e, sh=sh, j=j, r=r: e.tensor_copy(Bv[:, j, :, r], sh), reads=["modv"], writes=["Bv"])
                S.dve(lambda e, gt=gt, j=j, r=r: e.tensor_scalar(Gv[:, j, :, r], gt, (1.0 if j == 1 else 0.5), None, op0=ALU.mult), reads=["modv"], writes=["Gv"])
        C.Av = Av; C.Bv = Bv; C.Gv = Gv; C.fg = fg
        if dbg:
            dbg_outs["modv"] = nc.dram_tensor("dbg_modv", [128, 9 * KC, 2], F32, kind="ExternalOutput").ap()
            S.dma(dbg_outs["modv"], modv, reads=["modv"], writes=["dbg_modv"])
        st_mod.__exit__(None, None, None)
        S.barrier()

        def norm_mod(T, j, r, src_key="H"):
            H, XM, ACT, TMP = C.H, C.XM, C.ACT, C.TMP
            pss, psk = PS[6], "PS6"
            sq = ACT
            for k in range(KC):
                S.act(lambda e, k=k: e.activation(sq[:, k, :T], H[:, k, :T], AF.Square), reads=["H"], writes=["ACT"])
            for k in range(KC):
                S.pe(lambda e, k=k: e.matmul(pss[:, :T], lhsT=onesb, rhs=sq[:, k, :T], start=(k == 0), stop=(k == KC - 1)), reads=["ACT", "onesb"], writes=[psk])
            rs = TMP[5]
            S.dve(lambda e: e.tensor_scalar(rs[:, :T], pss[:, :T], 1.0 / D, EPS, op0=ALU.mult, op1=ALU.add), reads=[psk], writes=["tmp5"])
            S.act(lambda e: e.activation(rs[:, :T], rs[:, :T], AF.Ln), reads=["tmp5"], writes=["tmp5"])
            S.act(lambda e: e.activation(rs[:, :T], rs[:, :T], AF.Exp, scale=-0.5), reads=["tmp5"], writes=["tmp5"])
            for k in range(KC):
                t = TMP[k % 2]; tk = f"tmp{k % 2}"
                S.dve(lambda e, k=k, t=t: e.tensor_tensor(t[:, :T], H[:, k, :T], rs[:, :T], op=ALU.mult), reads=["H", "tmp5"], writes=[tk])
                S.act(lambda e, k=k, t=t: e.activation(XM[:, k, :T], t[:, :T], AF.Identity, bias=Bv[:, j, k, r:r + 1], scale=Av[:, j, k, r:r + 1]),
                      reads=[tk, "Av", "Bv"], writes=["XM"])
        C.norm_mod = norm_mod

        def linear(T, w_ap, kc, col0, ncols, evac, cw=512, xsrc=None, xkey="XM"):
            xs = C.XM if xsrc is None else xsrc
            cw = min(cw, 8192 // kc)
            nblk = (ncols + cw - 1) // cw
            ci = 0
            for b in range(nblk):
                c0 = col0 + b * cw
                w = min(cw, col0 + ncols - c0)
                wbuf, wkey = next_wb()
                wv = wbuf[:, :kc * w].rearrange("p (k n) -> p k n", k=kc)
                S.dma_cast(wv, w_ap[:, c0:c0 + w].rearrange("(k p) n -> p k n", p=128), writes=[wkey])
                for jj in range((w + 127) // 128):
                    m = min(128, w - jj * 128)
                    ps, pk = next_ps()
                    for k in range(kc):
                        S.pe(lambda e, ps=ps, wv=wv, k=k, jj=jj, m=m: e.matmul(ps[:m, :T], lhsT=wv[:, k, jj * 128:jj * 128 + m], rhs=xs[:, k, :T], start=(k == 0), stop=(k == kc - 1)),
                             reads=[wkey, xkey], writes=[pk])
                    evac(ci, ps, pk)
                    ci += 1
        C.linear = linear

        def ffn(T, fi, j, r):
            H, XM, ACT, TMP = C.H, C.XM, C.ACT, C.TMP
            norm_mod(T, j, r)
            for hb in range(HC // 4):
                wg_b, wgk = next_wb(); wu_b, wuk = next_wb()
                wgv = wg_b.rearrange("p (k n) -> p k n", k=KC); wuv = wu_b.rearrange("p (k n) -> p k n", k=KC)
                S.dma_cast(wgv, wgu[fi][:, hb * 512:(hb + 1) * 512].rearrange("(k p) n -> p k n", p=128), writes=[wgk])
                S.dma_cast(wuv, wgu[fi][:, DFF + hb * 512:DFF + (hb + 1) * 512].rearrange("(k p) n -> p k n", p=128), writes=[wuk])
                for jj in range(4):
                    hc = hb * 4 + jj
                    psg, pgk = next_ps(); psu, puk = next_ps()
                    for k in range(KC):
                        S.pe(lambda e, psg=psg, wgv=wgv, k=k, jj=jj: e.matmul(psg[:, :T], lhsT=wgv[:, k, jj * 128:(jj + 1) * 128], rhs=XM[:, k, :T], start=(k == 0), stop=(k == KC - 1)), reads=[wgk, "XM"], writes=[pgk])
                    for k in range(KC):
                        S.pe(lambda e, psu=psu, wuv=wuv, k=k, jj=jj: e.matmul(psu[:, :T], lhsT=wuv[:, k, jj * 128:(jj + 1) * 128], rhs=XM[:, k, :T], start=(k == 0), stop=(k == KC - 1)), reads=[wuk, "XM"], writes=[puk])
                    t = TMP[2 + (hc % 2)]; tk = f"tmp{2 + (hc % 2)}"
                    S.act(lambda e, psg=psg, t=t: e.activation(t[:, :T], psg[:, :T], AF.Silu), reads=[pgk], writes=[tk])
                    S.dve(lambda e, psu=psu, t=t, hc=hc: e.tensor_tensor(ACT[:, hc, :T], t[:, :T], psu[:, :T], op=ALU.mult), reads=[tk, puk], writes=["ACT"])
            def ev(ci, ps, pk):
                S.dve(lambda e, ps=ps, ci=ci: e.scalar_tensor_tensor(H[:, ci, :T], in0=ps[:, :T], scalar=Gv[:, j, ci, r:r + 1], in1=H[:, ci, :T], op0=ALU.mult, op1=ALU.add),
                      reads=[pk, "H", "Gv"], writes=["H"])
            linear(T, wd[fi], HC, 0, D, ev, cw=128, xsrc=ACT, xkey="ACT")
        C.ffn = ffn

        if "A" in stages:
            st_a = contextlib.ExitStack(); st_a.__enter__()
            sb = alloc_main(st_a)
            H, XM, ACT, TMP = C.H, C.XM, C.ACT, C.TMP
            cwh = sb("cwh", [128, 24, 3]); cbh = sb("cbh", [128, 24]); cwg = sb("cwg", [128, 24, 3]); zb = sb("zb", [128, 24])
            S.dma(cwh, hy_cw, writes=["cwh"]); S.dma(cbh, hy_cb, writes=["cbh"]); S.dma(cwg, g_cw, writes=["cwg"])
            S.pool(lambda e: e.memset(zb, 0.0), writes=["zb"])
            wab = sb("wab", [128, KC, 32], BF16)
            S.dma_cast(wab, w_in[:, AB_OFF:AB_OFF + 32].rearrange("(k p) n -> p k n", p=128), writes=["wab"])
            STG = sb("STG", [128, 4, 1024], BF16)
            tiles = [(0, t * NT, NT, 0) for t in range(L // NT)] + [(1, 0, LC, 1)]
            if dbg == "short":
                tiles = [tiles[0], tiles[-1]]
            def do_tile(isctx, t0, T, r):
                src = cT if isctx else xT
                S.dma(H[:, :, :T], src[:, t0:t0 + T].rearrange("(k p) t -> p k t", p=128), writes=["H"])
                ffn(T, 0, 0, r)
                if not isctx:
                    S.dma(H1T[:, t0:t0 + T].rearrange("(k p) t -> p k t", p=128), H[:, :, :T], reads=["H"], writes=["H1T"])
                norm_mod(T, 1, r)
                seg = 64 if not isctx else LC
                nseg = T // seg
                tcol = (L + t0) if isctx else t0

                def conv(ci, ps, pk, cw_t, cb_t, widx, outk):
                    pr = TMP[0]; y = TMP[1]
                    S.act(lambda e, ps=ps: e.copy(pr[:, :T], ps[:, :T]), reads=[pk], writes=["tmp0"])
                    S.dve(lambda e: e.tensor_scalar(y[:, :T], pr[:, :T], cw_t[:, widx, 1:2], cb_t[:, widx:widx + 1], op0=ALU.mult, op1=ALU.add), reads=["tmp0"], writes=["tmp1"])
                    prv = pr[:, :T].rearrange("p (s n) -> p s n", n=seg); yv = y[:, :T].rearrange("p (s n) -> p s n", n=seg)
                    S.dve(lambda e: e.scalar_tensor_tensor(yv[:, :, 1:], in0=prv[:, :, :seg - 1], scalar=cw_t[:, widx, 0:1], in1=yv[:, :, 1:], op0=ALU.mult, op1=ALU.add), reads=["tmp0", "tmp1"], writes=["tmp1"])
                    S.dve(lambda e: e.scalar_tensor_tensor(yv[:, :, :seg - 1], in0=prv[:, :, 1:], scalar=cw_t[:, widx, 2:3], in1=yv[:, :, :seg - 1], op0=ALU.mult, op1=ALU.add), reads=["tmp0", "tmp1"], writes=["tmp1"])
                    return y

                if not isctx:
                    def ev_hy(ci, ps, pk):
                        y = conv(ci, ps, pk, cwh, cbh, ci, None)
                        yb = TMP[2].bitcast(BF16)
                        S.act(lambda e: e.copy(yb[:, :T], y[:, :T]), reads=["tmp1"], writes=["tmp2"])
                        pt = PS[7].bitcast(BF16)
                        for bl in range(T // 128):
                            S.pe(lambda e, bl=bl: e.transpose(pt[:, bl * 128:(bl + 1) * 128], yb[:, bl * 128:(bl + 1) * 128], identb), reads=["tmp2", "identb"], writes=["PS7"])
                        c8 = ci % 8
                        S.dve(lambda e, c8=c8: e.tensor_copy(STG[:, :T // 128, c8 * 128:(c8 + 1) * 128], pt[:, :T].rearrange("p (b c) -> p b c", c=128)), reads=["PS7"], writes=["STG"])
                        if c8 == 7:
                            g8 = ci // 8
                            S.dma(HYT[t0:t0 + T, g8 * 1024:(g8 + 1) * 1024].rearrange("(b p) c -> p b c", p=128), STG[:, :T // 128, :], reads=["STG"], writes=["HYT"])
                    linear(T, w_in, KC, 0, HY_COLS, ev_hy)
                def ev_qkv(ci, ps, pk, base):
                    gi = base + ci
                    y = conv(ci, ps, pk, cwg, zb, gi, None)
                    a = TMP[2]
                    S.act(lambda e: e.activation(a[:, :T], y[:, :T], AF.Silu), reads=["tmp1"], writes=["tmp2"])
                    which = gi // 8; h = gi % 8
                    dst = (QT, KT, VT)[which]
                    if which < 2:
                        sq = TMP[3]
                        S.dve(lambda e: e.tensor_tensor(sq[:, :T], a[:, :T], a[:, :T], op=ALU.mult), reads=["tmp2"], writes=["tmp3"])
                        pss = PS[6]
                        S.pe(lambda e: e.matmul(pss[:, :T], lhsT=onesf, rhs=sq[:, :T], start=True, stop=True), reads=["tmp3", "onesf"], writes=["PS6"])
                        rn = TMP[4]
                        S.dve(lambda e: e.tensor_scalar(rn[:, :T], pss[:, :T], 1e-6, None, op0=ALU.add), reads=["PS6"], writes=["tmp4"])
                        S.act(lambda e: e.activation(rn[:, :T], rn[:, :T], AF.Ln), reads=["tmp4"], writes=["tmp4"])
                        S.act(lambda e: e.activation(rn[:, :T], rn[:, :T], AF.Exp, scale=-0.5), reads=["tmp4"], writes=["tmp4"])
                        scl = (128.0 ** -0.5) if which == 0 else 1.0
                        S.dve(lambda e: e.scalar_tensor_tensor(sq[:, :T], in0=a[:, :T], scalar=scl, in1=rn[:, :T], op0=ALU.mult, op1=ALU.mult), reads=["tmp2", "tmp4"], writes=["tmp3"])
                        res = sq; ak = "tmp3"
                    else:
                        res = a; ak = "tmp2"
                    S.dma(dst[h * 128:(h + 1) * 128, tcol:tcol + T], res[:, :T], reads=[ak], writes=[("QT", "KT", "VT")[which]])
                if not isctx:
                    linear(T, w_in, KC, Q_OFF, 1024, lambda ci, ps, pk: ev_qkv(ci, ps, pk, 0))
                linear(T, w_in, KC, KV_OFF, 2048, lambda ci, ps, pk: ev_qkv(ci, ps, pk, 8))
                if not isctx:
                    def ev_z(ci, ps, pk):
                        zt = TMP[2].bitcast(BF16)
                        S.act(lambda e, ps=ps: e.activation(zt[:, :T], ps[:, :T], AF.Silu), reads=[pk], writes=["tmp2"])
                        S.dma(ZG[ci * 128:(ci + 1) * 128, t0:t0 + T], zt[:, :T], reads=["tmp2"], writes=["ZG"])
                    linear(T, w_in, KC, Z_OFF, 1024, ev_z)
                for bl in range(T // 128):
                    ps, pk = next_ps()
                    for k in range(KC):
                        S.pe(lambda e, ps=ps, k=k, bl=bl: e.matmul(ps[:, :32], lhsT=XM[:, k, bl * 128:(bl + 1) * 128], rhs=wab[:, k, :], start=(k == 0), stop=(k == KC - 1)), reads=["XM", "wab"], writes=[pk])
                    a = TMP[3]
                    S.dve(lambda e, ps=ps: e.tensor_copy(a[:, :32], ps[:, :32]), reads=[pk], writes=["tmp3"])
                    S.dma(ABs[tcol + bl * 128:tcol + (bl + 1) * 128, :], a[:, :32], reads=["tmp3"], writes=["ABs"])
            for tl in tiles:
                do_tile(*tl)
            st_a.__exit__(None, None, None)
            S.barrier()


        if "F" in stages:
            st_f = contextlib.ExitStack(); st_f.__enter__()
            def sbf(name, shape, dt=F32):
                return st_f.enter_context(nc.sbuf_tensor("sF_" + name, list(shape), dt)).ap()
            TWO_PI = 2.0 * math.pi
            w3s = sbf("w3s", [64, 4096]); h2T = sbf("h2T", [64, 4096]); deltab = sbf("deltab", [128, 1024]); thn = sbf("thn", [128, 32])
            mb0 = sbf("mb0", [128, 1]); negpi = sbf("negpi", [128, 1]); frs = sbf("frs", [64, 1]); s1 = sbf("s1", [64, 1]); s2a = sbf("s2a", [64, 1]); s2b = sbf("s2b", [64, 1])
            b1s = sbf("b1s", [64, 1]); b2s = sbf("b2s", [64, 1])
            S.dma(w3s, fw3, writes=["w3s"]); S.dma(deltab, deltas_d.partition_broadcast(128), writes=["deltab"]); S.dma(thn, that_d, writes=["thn"])
            S.dma(mb0, mb0_d, writes=["mb0"]); S.dma(frs, ffr, writes=["frs"]); S.dma(b1s, fb1, writes=["b1s"]); S.dma(b2s, fb2, writes=["b2s"])
            S.pool(lambda e: e.memset(negpi, -math.pi), writes=["negpi"])
            S.dve(lambda e: e.tensor_scalar(thn, thn, -1.0, None, op0=ALU.mult), reads=["thn"], writes=["thn"])
            S.dve(lambda e: e.tensor_scalar(s1, frs, 1.0 / TWO_PI, None, op0=ALU.mult), reads=["frs"], writes=["s1"])
            S.dve(lambda e: e.tensor_tensor(s2a, frs, b1s, op=ALU.mult), reads=["frs", "b1s"], writes=["s2a"])
            S.dve(lambda e: e.tensor_scalar(s2a, s2a, 1.0 / TWO_PI, 16.5, op0=ALU.mult, op1=ALU.add), reads=["s2a"], writes=["s2a"])
            S.dve(lambda e: e.tensor_tensor(s2b, frs, b2s, op=ALU.mult), reads=["frs", "b2s"], writes=["s2b"])
            S.dve(lambda e: e.tensor_scalar(s2b, s2b, 1.0 / TWO_PI, 16.5, op0=ALU.mult, op1=ALU.add), reads=["s2b"], writes=["s2b"])
            st_f1 = contextlib.ExitStack(); st_f1.__enter__()
            def sbf1(name, shape, dt=F32):
                return st_f1.enter_context(nc.sbuf_tensor("sF1_" + name, list(shape), dt)).ap()
            zTs = sbf1("zTs", [33, 4096]); w1s = sbf1("w1s", [33, 64]); w2s = sbf1("w2s", [64, 64]); h1T = sbf1("h1T", [64, 4096])
            yt = sbf1("yt", [64, 512]); kit = sbf1("kit", [64, 512], mybir.dt.int32); kft = sbf1("kft", [64, 512])
            S.dma(zTs, zT_d, writes=["zTs"]); S.dma(w1s, fw1, writes=["w1s"]); S.dma(w2s, fw2, writes=["w2s"])
            def sin_layer(srcT, skey, kdim, wS, wkey, s2, s2key, dstT, dkey):
                for tt in range(8):
                    ps, pk = next_ps()
                    S.pe(lambda e, ps=ps, tt=tt: e.matmul(ps[:64, :512], lhsT=wS[:kdim, :], rhs=srcT[:kdim, tt * 512:(tt + 1) * 512], start=True, stop=True), reads=[skey, wkey], writes=[pk])
                    S.dve(lambda e, ps=ps: e.tensor_scalar(yt, ps[:64, :512], s1[:, 0:1], s2[:, 0:1], op0=ALU.mult, op1=ALU.add), reads=[pk, "s1", s2key], writes=["yt"])
                    S.dve(lambda e: e.tensor_copy(kit, yt), reads=["yt"], writes=["kit"])
                    S.dve(lambda e: e.tensor_copy(kft, kit), reads=["kit"], writes=["kft"])
                    S.dve(lambda e: e.tensor_tensor(yt, yt, kft, op=ALU.subtract), reads=["yt", "kft"], writes=["yt"])
                    S.dve(lambda e: e.tensor_single_scalar(kft, yt, 0.0, op=ALU.is_lt), reads=["yt"], writes=["kft"])
                    S.dve(lambda e: e.tensor_tensor(yt, yt, kft, op=ALU.add), reads=["yt", "kft"], writes=["yt"])
                    S.act(lambda e, tt=tt: e.activation(dstT[:, tt * 512:(tt + 1) * 512], yt, AF.Sin, bias=negpi[:64, 0:1], scale=TWO_PI), reads=["yt", "negpi"], writes=[dkey])
            sin_layer(zTs, "zTs", 33, w1s, "w1s", s2a, "s2a", h1T, "h1T")
            sin_layer(h1T, "h1T", 64, w2s, "w2s", s2b, "s2b", h2T, "h2T")
            st_f1.__exit__(None, None, None)
            HS = sbf("HS", [128, 32, 512], BF16); HD = sbf("HD", [128, 32, 512], BF16)
            CB = [sbf(f"CBf{i}", [128, 32, 128], BF16) for i in range(2)]; SBk = [sbf(f"SBf{i}", [128, 32, 128], BF16) for i in range(2)]
            FT = [[sbf(f"ft{i}_{j}", [128, 512]) for j in range(6)] for i in range(2)]
            RN = sbf("RN", [128, 512])
            for o in range(2):
                for hh in range(2):
                    colf = o * 2048 + hh * 512; colb = o * 2048 + 1024 + hh * 512
                    for tc in range(32):
                        tw, thf, thb, ta1, ta2, thm = FT[tc % 2]; fk = [f"ft{tc % 2}_{j}" for j in range(6)]
                        S.act(lambda e, tw=tw, tc=tc, hh=hh: e.activation(tw, deltab[:, hh * 512:(hh + 1) * 512], AF.Exp, scale=thn[:, tc:tc + 1]), reads=["deltab", "thn"], writes=[fk[0]])
                        psf, pkf = next_ps(); psb, pkb = next_ps()
                        S.pe(lambda e, psf=psf, tc=tc, colf=colf: e.matmul(psf, lhsT=h2T[:, tc * 128:(tc + 1) * 128], rhs=w3s[:, colf:colf + 512], start=True, stop=True), reads=["h2T", "w3s"], writes=[pkf])
                        S.pe(lambda e, psb=psb, tc=tc, colb=colb: e.matmul(psb, lhsT=h2T[:, tc * 128:(tc + 1) * 128], rhs=w3s[:, colb:colb + 512], start=True, stop=True), reads=["h2T", "w3s"], writes=[pkb])
                        S.dve(lambda e, psf=psf, thf=thf, tw=tw: e.tensor_tensor(thf, psf, tw, op=ALU.mult), reads=[pkf, fk[0]], writes=[fk[1]])
                        S.dve(lambda e, psb=psb, thb=thb, tw=tw: e.tensor_tensor(thb, psb, tw, op=ALU.mult), reads=[pkb, fk[0]], writes=[fk[2]])
                        S.act(lambda e, ta1=ta1, thf=thf: e.activation(ta1, thf, AF.Abs), reads=[fk[1]], writes=[fk[3]])
                        S.act(lambda e, ta2=ta2, thb=thb: e.activation(ta2, thb, AF.Abs), reads=[fk[2]], writes=[fk[4]])
                        S.pool(lambda e, ta1=ta1, ta2=ta2: e.tensor_tensor(ta1, ta1, ta2, op=ALU.add), reads=[fk[3], fk[4]], writes=[fk[3]])
                        S.pe(lambda e, ta1=ta1, tc=tc: e.matmul(PS[6], lhsT=onesf, rhs=ta1, start=(tc == 0), stop=(tc == 31)), reads=[fk[3], "onesf"], writes=["PS6"])
                        if tc == 0:
                            S.dve(lambda e, thm=thm, thb=thb: e.tensor_scalar(thm, thb, mb0[:, 0:1], None, op0=ALU.mult), reads=[fk[2], "mb0"], writes=[fk[5]])
                            hbm = thm; hbk = fk[5]
                        else:
                            hbm = thb; hbk = fk[2]
                        S.pool(lambda e, thf=thf, hbm=hbm, tc=tc: e.tensor_tensor(HS[:, tc, :], thf, hbm, op=ALU.add), reads=[fk[1], hbk], writes=["HS"])
                        S.dve(lambda e, thf=thf, thb=thb, tc=tc: e.tensor_tensor(HD[:, tc, :], thf, thb, op=ALU.subtract), reads=[fk[1], fk[2]], writes=["HD"])
                    S.dve(lambda e: e.reciprocal(RN, PS[6]), reads=["PS6"], writes=["RN"])
                    S.dve(lambda e: e.tensor_scalar(RN, RN, 2.0 / 8192.0, None, op0=ALU.mult), reads=["RN"], writes=["RN"])
                    for fc in range(32):
                        cb = CB[fc % 2]; sk = SBk[fc % 2]; cbk = f"CB{fc % 2}"; skk = f"SBk{fc % 2}"
                        S.dma(cb, CFt[fc], writes=[cbk]); S.dma(sk, SFt[fc], writes=[skk])
                        psC, pkC = next_ps(); psS, pkS = next_ps()
                        for tc in range(32):
                            S.pe(lambda e, psC=psC, cb=cb, tc=tc: e.matmul(psC, lhsT=cb[:, tc, :], rhs=HS[:, tc, :], start=(tc == 0), stop=(tc == 31)), reads=[cbk, "HS"], writes=[pkC])
                        for tc in range(32):
                            S.pe(lambda e, psS=psS, sk=sk, tc=tc: e.matmul(psS, lhsT=sk[:, tc, :], rhs=HD[:, tc, :], start=(tc == 0), stop=(tc == 31)), reads=[skk, "HD"], writes=[pkS])
                        ta, tb = FT[fc % 2][0], FT[fc % 2][1]; tak, tbk = f"ft{fc % 2}_0", f"ft{fc % 2}_1"
                        S.dve(lambda e, psC=psC, ta=ta: e.tensor_tensor(ta, psC, RN, op=ALU.mult), reads=[pkC, "RN"], writes=[tak])
                        S.dve(lambda e, psS=psS, tb=tb: e.tensor_tensor(tb, psS, RN, op=ALU.mult), reads=[pkS, "RN"], writes=[tbk])
                        cc = o * 1024 + hh * 512
                        S.dma(KCs[fc * 128:(fc + 1) * 128, cc:cc + 512], ta, reads=[tak], writes=["KCs"])
                        S.dma(KSs[fc * 128:(fc + 1) * 128, cc:cc + 512], tb, reads=[tbk], writes=["KSs"])
            st_f.__exit__(None, None, None)
            S.barrier()

        if "H" in stages:
            st_h = contextlib.ExitStack(); st_h.__enter__()
            def sbh(name, shape, dt=F32):
                return st_h.enter_context(nc.sbuf_tensor("sH_" + name, list(shape), dt)).ap()
            ZIN = sbh("ZIN", [128, 32, 1024], BF16)
            CBh = [sbh(f"CB{i}", [128, 32, 128], BF16) for i in range(2)]; SBh = [sbh(f"SB{i}", [128, 32, 128], BF16) for i in range(2)]
            HT = [[sbh(f"ht{i}_{j}", [128, 512]) for j in range(8)] for i in range(2)]
            PQ = [[sbh(f"pq{i}_{j}", [128, 512], BF16) for j in range(4)] for i in range(2)]
            biasb = sbh("biasb", [128, 2, 1024]); ZST = sbh("ZST", [128, 4, 512], BF16)
            for o in range(2):
                S.dma(biasb[:, o, :], hyb_d[o:o + 1, :].partition_broadcast(128), writes=["biasb"])
            for o in range(2):
                src = HYT[:, 0:1024] if o == 0 else Z2T
                skey = "HYT" if o == 0 else "Z2T"
                gsrc = HYT[:, 1024:2048] if o == 0 else HYT[:, 2048:3072]
                for q4 in range(4):
                    S.dma(ZIN[:, q4 * 8:(q4 + 1) * 8, :], src[q4 * 1024:(q4 + 1) * 1024, :].rearrange("(tc p) c -> p tc c", p=128), reads=[skey], writes=["ZIN"])
                for fc in range(32):
                    cb = CBh[fc % 2]; sk = SBh[fc % 2]; cbk = f"hCB{fc % 2}"; skk = f"hSB{fc % 2}"
                    S.dma(cb, CFt[fc], writes=[cbk]); S.dma(sk, SFt[fc], writes=[skk])
                    for ct in range(2):
                        par = (fc * 2 + ct) % 2
                        ht = HT[par]; hk = [f"ht{par}_{j}" for j in range(8)]; pq = PQ[par]; pk_ = [f"pq{par}_{j}" for j in range(4)]
                        psA, pkA = next_ps(); psB, pkB = next_ps()
                        for tc in range(32):
                            S.pe(lambda e, psA=psA, cb=cb, tc=tc, ct=ct: e.matmul(psA, lhsT=cb[:, tc, :], rhs=ZIN[:, tc, ct * 512:(ct + 1) * 512], start=(tc == 0), stop=(tc == 31)), reads=[cbk, "ZIN"], writes=[pkA])
                        for tc in range(32):
                            S.pe(lambda e, psB=psB, sk=sk, tc=tc, ct=ct: e.matmul(psB, lhsT=sk[:, tc, :], rhs=ZIN[:, tc, ct * 512:(ct + 1) * 512], start=(tc == 0), stop=(tc == 31)), reads=[skk, "ZIN"], writes=[pkB])
                        kc, ks, A, B, t1, t2, t3, t4 = ht
                        cc = o * 1024 + ct * 512
                        S.dma(kc, KCs[fc * 128:(fc + 1) * 128, cc:cc + 512], reads=["KCs"], writes=[hk[0]])
                        S.dma(ks, KSs[fc * 128:(fc + 1) * 128, cc:cc + 512], reads=["KSs"], writes=[hk[1]])
                        S.act(lambda e, A=A, psA=psA: e.copy(A, psA), reads=[pkA], writes=[hk[2]])
                        S.act(lambda e, B=B, psB=psB: e.copy(B, psB), reads=[pkB], writes=[hk[3]])
                        S.dve(lambda e, t1=t1, A=A, kc=kc: e.tensor_tensor(t1, A, kc, op=ALU.mult), reads=[hk[2], hk[0]], writes=[hk[4]])
                        S.pool(lambda e, t2=t2, B=B, ks=ks: e.tensor_tensor(t2, B, ks, op=ALU.mult), reads=[hk[3], hk[1]], writes=[hk[5]])
                        S.dve(lambda e, t1=t1, t2=t2, p=pq[0]: e.tensor_tensor(p, t1, t2, op=ALU.subtract), reads=[hk[4], hk[5]], writes=[pk_[0]])
                        S.pool(lambda e, t3=t3, A=A, ks=ks: e.tensor_tensor(t3, A, ks, op=ALU.mult), reads=[hk[2], hk[1]], writes=[hk[6]])
                        S.dve(lambda e, t4=t4, B=B, kc=kc: e.tensor_tensor(t4, B, kc, op=ALU.mult), reads=[hk[3], hk[0]], writes=[hk[7]])
                        S.pool(lambda e, t3=t3, t4=t4, q=pq[1]: e.tensor_tensor(q, t3, t4, op=ALU.add), reads=[hk[6], hk[7]], writes=[pk_[1]])
                        S.dma(Ps[fc * 128:(fc + 1) * 128, ct * 512:(ct + 1) * 512], pq[0], reads=[pk_[0]], writes=["Ps"])
                        S.dma(Qs[fc * 128:(fc + 1) * 128, ct * 512:(ct + 1) * 512], pq[1], reads=[pk_[1]], writes=["Qs"])
                for ct in range(2):
                    PB = ZIN[:, :, 0:512]; QB = ZIN[:, :, 512:1024]
                    S.dma(PB, Ps[:, ct * 512:(ct + 1) * 512].rearrange("(fc p) c -> p fc c", p=128), reads=["Ps"], writes=["ZIN"])
                    S.dma(QB, Qs[:, ct * 512:(ct + 1) * 512].rearrange("(fc p) c -> p fc c", p=128), reads=["Qs"], writes=["ZIN"])
                    for tch in range(32):
                        ci_ = CBh[tch % 2]; si_ = SBh[tch % 2]; cbk = f"hCB{tch % 2}"; skk = f"hSB{tch % 2}"
                        S.dma(ci_, CIt[tch], writes=[cbk]); S.dma(si_, SIt[tch], writes=[skk])
                        par = tch % 2
                        ht = HT[par]; hk = [f"ht{par}_{j}" for j in range(8)]; pq = PQ[par]; pk_ = [f"pq{par}_{j}" for j in range(4)]
                        psY, pkY = next_ps()
                        for fc in range(32):
                            S.pe(lambda e, psY=psY, ci_=ci_, fc=fc: e.matmul(psY, lhsT=ci_[:, fc, :], rhs=PB[:, fc, :], start=(fc == 0), stop=False), reads=[cbk, "ZIN"], writes=[pkY])
                        for fc in range(32):
                            S.pe(lambda e, psY=psY, si_=si_, fc=fc: e.matmul(psY, lhsT=si_[:, fc, :], rhs=QB[:, fc, :], start=False, stop=(fc == 31)), reads=[skk, "ZIN"], writes=[pkY])
                        zin_t = pq[2]; gate_t = pq[3]; res = pq[0]
                        S.dma(zin_t, src[tch * 128:(tch + 1) * 128, ct * 512:(ct + 1) * 512], reads=[skey], writes=[pk_[2]])
                        S.dma(gate_t, gsrc[tch * 128:(tch + 1) * 128, ct * 512:(ct + 1) * 512], reads=["HYT"], writes=[pk_[3]])
                        t1, t2 = ht[4], ht[5]
                        S.dve(lambda e, t1=t1, zin_t=zin_t, o=o, ct=ct: e.tensor_tensor(t1, zin_t, biasb[:, o, ct * 512:(ct + 1) * 512], op=ALU.mult), reads=[pk_[2], "biasb"], writes=[hk[4]])
                        S.dve(lambda e, t2=t2, t1=t1, psY=psY: e.tensor_tensor(t2, psY, t1, op=ALU.add), reads=[pkY, hk[4]], writes=[hk[5]])
                        S.pool(lambda e, res=res, t2=t2, gate_t=gate_t: e.tensor_tensor(res, t2, gate_t, op=ALU.mult), reads=[hk[5], pk_[3]], writes=[pk_[0]])
                        if o == 0:
                            S.dma(Z2T[tch * 128:(tch + 1) * 128, ct * 512:(ct + 1) * 512], res, reads=[pk_[0]], writes=["Z2T"])
                        else:
                            pt = PS[7].bitcast(BF16)
                            for j in range(4):
                                S.pe(lambda e, res=res, j=j: e.transpose(pt[:, j * 128:(j + 1) * 128], res[:, j * 128:(j + 1) * 128], identb), reads=[pk_[0], "identb"], writes=["PS7"])
                            t4_ = tch % 4
                            S.act(lambda e, t4_=t4_: e.copy(ZST[:, :, t4_ * 128:(t4_ + 1) * 128], pt[:, 0:512].rearrange("p (j t) -> p j t", j=4)), reads=["PS7"], writes=["ZST"])
                            if t4_ == 3:
                                tg = tch // 4
                                S.dma(ZH[ct * 512:(ct + 1) * 512, tg * 512:(tg + 1) * 512].rearrange("(j p) t -> p j t", p=128), ZST, reads=["ZST"], writes=["ZH"])
            st_h.__exit__(None, None, None)
            S.barrier()

        if "G" in stages:
            st_g = contextlib.ExitStack(); st_g.__enter__()
            def sbg(name, shape, dt=F32):
                return st_g.enter_context(nc.sbuf_tensor("sG_" + name, list(shape), dt)).ap()
            NCH = LTOT // 128
            tri = sbg("tri", [128, 4, 128]); S.dma(tri, tri_d, writes=["tri"])
            TRI_LE, TRI_GE, TRI_GT, TRI_LT = tri[:, 0, :], tri[:, 1, :], tri[:, 2, :], tri[:, 3, :]
            gng = sbg("gng", [128, 1]); S.dma(gng, gng_d, writes=["gng"])
            alb = sbg("alb", [128, 16]); dtbb = sbg("dtbb", [128, 16]); negea = sbg("negea", [128, 16])
            S.dma(alb, alog_d.partition_broadcast(128), writes=["alb"]); S.dma(dtbb, dtb_d.partition_broadcast(128), writes=["dtbb"])
            S.act(lambda e: e.activation(negea, alb, AF.Exp), reads=["alb"], writes=["negea"])
            S.dve(lambda e: e.tensor_scalar(negea, negea, -1.0, None, op0=ALU.mult), reads=["negea"], writes=["negea"])
            ABt = sbg("ABt", [128, NCH, 32]); S.dma(ABt, ABs.rearrange("(c p) n -> p c n", p=128), reads=["ABs"], writes=["ABt"])
            X = sbg("X", [128, NCH, 16])
            for col in range(16):
                S.dve(lambda e, col=col: e.tensor_scalar(X[:, :, col], ABt[:, :, col], dtbb[:, col:col + 1], None, op0=ALU.add), reads=["ABt", "dtbb"], writes=["X"])
            S.act(lambda e: e.activation(X, X, AF.Exp), reads=["X"], writes=["X"])
            S.act(lambda e: e.activation(X, X, AF.Ln, bias=1.0), reads=["X"], writes=["X"])
            Gg = sbg("Gg", [128, 2, NCH, 8]); BETA = sbg("BETA", [128, 2, NCH, 8]); GC = sbg("GC", [128, 2, NCH, 8]); GLt = sbg("GLt", [128, 2, NCH, 8])
            EG = sbg("EG", [128, 2, NCH, 8]); NEG = sbg("NEG", [128, 2, NCH, 8]); EKD = sbg("EKD", [128, 2, NCH, 8]); EGL = sbg("EGL", [128, 2, NCH, 8])
            for col in range(16):
                d_, h_ = col // 8, col % 8
                S.dve(lambda e, col=col, d_=d_, h_=h_: e.tensor_scalar(Gg[:, d_, :, h_], X[:, :, col], negea[:, col:col + 1], None, op0=ALU.mult), reads=["X", "negea"], writes=["Gg"])
            for d_ in range(2):
                S.act(lambda e, d_=d_: e.activation(BETA[:, d_, :, :], ABt[:, :, 16 + 8 * d_:24 + 8 * d_], AF.Sigmoid), reads=["ABt"], writes=["BETA"])
                gflat = Gg[:, d_, :, :].rearrange("p c h -> p (c h)")
                S.pe(lambda e, d_=d_, gflat=gflat: e.matmul(PS[0][:, :NCH * 8], lhsT=(TRI_LE if d_ == 0 else TRI_GE), rhs=gflat, start=True, stop=True), reads=["Gg", "tri"], writes=["PS0"])
                S.dve(lambda e, d_=d_: e.tensor_copy(GC[:, d_, :, :].rearrange("p c h -> p (c h)"), PS[0][:, :NCH * 8]), reads=["PS0"], writes=["GC"])
                S.pe(lambda e, d_=d_, gflat=gflat: e.matmul(PS[1][:, :NCH * 8], lhsT=onesf, rhs=gflat, start=True, stop=True), reads=["Gg", "onesf"], writes=["PS1"])
                S.dve(lambda e, d_=d_: e.tensor_copy(GLt[:, d_, :, :].rearrange("p c h -> p (c h)"), PS[1][:, :NCH * 8]), reads=["PS1"], writes=["GLt"])
            S.act(lambda e: e.activation(EG, GC, AF.Exp), reads=["GC"], writes=["EG"])
            S.dve(lambda e: e.tensor_scalar(NEG, EG, -1.0, None, op0=ALU.mult), reads=["EG"], writes=["NEG"])
            S.dve(lambda e: e.tensor_tensor(EKD, GLt, GC, op=ALU.subtract), reads=["GLt", "GC"], writes=["EKD"])
            S.act(lambda e: e.activation(EKD, EKD, AF.Exp), reads=["EKD"], writes=["EKD"])
            S.act(lambda e: e.activation(EGL, GLt, AF.Exp), reads=["GLt"], writes=["EGL"])
            if dbg:
                for nm, t_ in (("Gg", Gg), ("BETA", BETA)):
                    dbg_outs[nm] = nc.dram_tensor("dbg_" + nm, [128, 2, NCH, 8], F32, kind="ExternalOutput").ap()
                    S.dma(dbg_outs[nm], t_, reads=[nm], writes=["dbg_" + nm])
            S8 = sbg("S8", [128, 8, 128])
            KB = [sbg(f"kT8_{i}", [128, 8, 128]) for i in range(2)]; VB = [sbg(f"vT8_{i}", [128, 8, 128]) for i in range(2)]; QB_ = [sbg(f"qT8_{i}", [128, 8, 128]) for i in range(2)]
            O8 = [sbg(f"O8_{i}", [128, 8, 128]) for i in range(2)]; OFt = [sbg(f"OFt_{i}", [128, 8, 128]) for i in range(2)]
            ZGt = [sbg(f"ZGt_{i}", [128, 8, 128], BF16) for i in range(2)]; OGc = [sbg(f"OGc_{i}", [128, 8, 128], BF16) for i in range(2)]
            ONn = sbg("ONn", [128, 8, 128]); SQn = sbg("SQn", [128, 8, 128]); ssn = sbg("ssn", [128, 8])
            NSET = 3
            names = ["gU", "ET", "ETs", "ETi", "Pa", "Pb", "PTa", "PTb", "TT", "kd", "vtok", "R", "vnew", "qkT", "o2s"]
            TS = [{n: sbg(f"{n}_{i}", [128, 128]) for n in names} for i in range(NSET)]
            C.pg_rr = 0
            def next_pg():
                i = C.pg_rr % 8
                C.pg_rr += 1
                return PS[i][:, 0:128], f"PS{i}"

            GSTOP = 99; GPROB = 10 ** 9
            def prob(d, c, h, pi, par):
                if pi >= GPROB: return
                isctx = c >= 32
                ts = TS[pi % NSET]; tk = {n: f"{n}_{pi % NSET}" for n in names}
                gcol = Gg[:, d, c, h:h + 1]; bcol = BETA[:, d, c, h:h + 1]; negeg = NEG[:, d, c, h:h + 1]; eg = EG[:, d, c, h:h + 1]
                ekd = EKD[:, d, c, h:h + 1]; egl = EGL[:, d, c, h:h + 1]
                U, Lm, MsT, MiT = (TRI_LE, TRI_GT, TRI_LT, TRI_LE) if d == 0 else (TRI_GE, TRI_LT, TRI_GT, TRI_GE)
                kT = KB[par][:, h, :]; vT = VB[par][:, h, :]; qT = QB_[par][:, h, :]
                kk, vk, qk_ = f"kT8_{par}", f"vT8_{par}", f"qT8_{par}"
                gU, ET, ETs, ETi, TT = ts["gU"], ts["ET"], ts["ETs"], ts["ETi"], ts["TT"]
                GSTEP = 99
                if GSTEP <= 0: return
                S.dve(lambda e: e.tensor_scalar(gU, U, gcol, None, op0=ALU.mult), reads=["tri", "Gg"], writes=[tk["gU"]])
                if GSTEP <= 1: return
                p1, k1 = next_pg()
                S.pe(lambda e: e.matmul(p1, lhsT=Lm, rhs=gU, start=True, stop=True), reads=["tri", tk["gU"]], writes=[k1])
                if GSTEP <= 2: return
                S.act(lambda e: e.activation(ET, p1, AF.Exp), reads=[k1], writes=[tk["ET"]])
                if GSTEP <= 3: return
                S.pool(lambda e: e.tensor_tensor(ETs, ET, MsT, op=ALU.mult), reads=[tk["ET"], "tri"], writes=[tk["ETs"]])
                if GSTEP <= 4: return
                p2, k2 = next_pg()
                S.pe(lambda e: e.matmul(p2, lhsT=kT, rhs=kT, start=True, stop=True), reads=[kk], writes=[k2])
                P0 = ts["Pa"]; PT0 = ts["PTa"]
                if GSTEP <= 5: return
                S.dve(lambda e: e.scalar_tensor_tensor(P0, in0=p2, scalar=bcol, in1=ETs, op0=ALU.mult, op1=ALU.mult), reads=[k2, "BETA", tk["ETs"]], writes=[tk["Pa"]])
                if not isctx:
                    S.pool(lambda e: e.tensor_tensor(ETi, ET, MiT, op=ALU.mult), reads=[tk["ET"], "tri"], writes=[tk["ETi"]])
                    p3, k3 = next_pg()
                    S.pe(lambda e: e.matmul(p3, lhsT=kT, rhs=qT, start=True, stop=True), reads=[kk, qk_], writes=[k3])
                    qkT = ts["qkT"]
                    S.dve(lambda e: e.tensor_tensor(qkT, p3, ETi, op=ALU.mult), reads=[k3, tk["ETi"]], writes=[tk["qkT"]])
                if GSTOP <= 1: return
                p4, k4 = next_pg()
                S.pe(lambda e: e.transpose(p4, P0, ident), reads=[tk["Pa"], "ident"], writes=[k4])
                S.act(lambda e: e.copy(PT0, p4), reads=[k4], writes=[tk["PTa"]])
                S.pool(lambda e: e.tensor_tensor(TT, ident, P0, op=ALU.subtract), reads=["ident", tk["Pa"]], writes=[tk["TT"]])
                Pc, PTc, Pck, PTck = P0, PT0, tk["Pa"], tk["PTa"]
                for l in range(1, 7):
                    Pn, PTn = (ts["Pb"], ts["PTb"]) if l % 2 == 1 else (ts["Pa"], ts["PTa"])
                    Pnk, PTnk = (tk["Pb"], tk["PTb"]) if l % 2 == 1 else (tk["Pa"], tk["PTa"])
                    if l < 6:
                        pa, ka = next_pg()
                        S.pe(lambda e, pa=pa, PTc=PTc, Pc=Pc: e.matmul(pa, lhsT=PTc, rhs=Pc, start=True, stop=True), reads=[PTck, Pck], writes=[ka])
                        S.act(lambda e, pa=pa, Pn=Pn: e.copy(Pn, pa), reads=[ka], writes=[Pnk])
                    pb, kb = next_pg()
                    S.pe(lambda e, pb=pb, PTc=PTc, Pc=Pc: e.matmul(pb, lhsT=Pc, rhs=PTc, start=True, stop=True), reads=[PTck, Pck], writes=[kb])
                    S.dve(lambda e, pb=pb, PTn=PTn: e.tensor_copy(PTn, pb), reads=[kb], writes=[PTnk])
                    pc, kc_ = next_pg()
                    S.pe(lambda e, pc=pc, PTn=PTn: e.matmul(pc, lhsT=PTn, rhs=TT, start=True, stop=True), reads=[PTnk, tk["TT"]], writes=[kc_])
                    S.dve(lambda e, pc=pc: e.tensor_tensor(TT, TT, pc, op=ALU.add), reads=[tk["TT"], kc_], writes=[tk["TT"]])
                    Pc, PTc, Pck, PTck = Pn, PTn, Pnk, PTnk
                if GSTOP <= 2: return
                kd, vtok, R_, vnew = ts["kd"], ts["vtok"], ts["R"], ts["vnew"]
                p5, k5 = next_pg()
                S.pe(lambda e: e.transpose(p5, kT, ident), reads=[kk, "ident"], writes=[k5])
                S.dve(lambda e: e.tensor_scalar(kd, p5, ekd, None, op0=ALU.mult), reads=[k5, "EKD"], writes=[tk["kd"]])
                p6, k6 = next_pg()
                S.pe(lambda e: e.transpose(p6, vT, ident), reads=[vk, "ident"], writes=[k6])
                S.act(lambda e: e.copy(vtok, p6), reads=[k6], writes=[tk["vtok"]])
                if GSTOP <= 3: return
                Sh = S8[:, h, :]; sk_ = f"S8_{h}"
                p7, k7 = next_pg()
                S.pe(lambda e: e.matmul(p7, lhsT=kT, rhs=Sh, start=True, stop=True), reads=[kk, sk_], writes=[k7])
                S.dve(lambda e: e.scalar_tensor_tensor(R_, in0=p7, scalar=negeg, in1=vtok, op0=ALU.mult, op1=ALU.add), reads=[k7, "NEG", tk["vtok"]], writes=[tk["R"]])
                p8, k8 = next_pg()
                S.pe(lambda e: e.matmul(p8, lhsT=TT, rhs=R_, start=True, stop=True), reads=[tk["TT"], tk["R"]], writes=[k8])
                S.act(lambda e: e.activation(vnew, p8, AF.Identity, scale=bcol), reads=[k8, "BETA"], writes=[tk["vnew"]])
                if not isctx:
                    o2s = ts["o2s"]
                    p9, k9 = next_pg(); p10, k10 = next_pg()
                    S.pe(lambda e: e.matmul(p9, lhsT=qT, rhs=Sh, start=True, stop=True), reads=[qk_, sk_], writes=[k9])
                    S.pe(lambda e: e.matmul(p10, lhsT=qkT, rhs=vnew, start=True, stop=True), reads=[tk["qkT"], tk["vnew"]], writes=[k10])
                    S.act(lambda e: e.copy(o2s, p10), reads=[k10], writes=[tk["o2s"]])
                    S.dve(lambda e: e.scalar_tensor_tensor(O8[par][:, h, :], in0=p9, scalar=eg, in1=o2s, op0=ALU.mult, op1=ALU.add), reads=[k9, "EG", tk["o2s"]], writes=[f"O8_{par}"])
                p11, k11 = next_pg()
                S.pe(lambda e: e.matmul(p11, lhsT=kd, rhs=vnew, start=True, stop=True), reads=[tk["kd"], tk["vnew"]], writes=[k11])
                S.dve(lambda e: e.scalar_tensor_tensor(Sh, in0=Sh, scalar=egl, in1=p11, op0=ALU.mult, op1=ALU.add), reads=[sk_, "EGL", k11], writes=[sk_])

            pi = 0
            lat = list(range(32))
            if dbg == "short":
                lat = [0, 1]

            for d in range(2 if GSTOP > 0 else 0):
                S.pool(lambda e: e.memset(S8, 0.0), reads=[f"S8_{h}" for h in range(8)], writes=[f"S8_{h}" for h in range(8)])
                order = ([32, 33] + lat) if d == 0 else ([33, 32] + lat[::-1])
                for n_, c in enumerate(order):
                    par = n_ % 2
                    isctx = c >= 32
                    S.dma(KB[par], KT[:, c * 128:(c + 1) * 128].rearrange("(h p) t -> p h t", p=128), reads=["KT"], writes=[f"kT8_{par}"])
                    S.dma(VB[par], VT[:, c * 128:(c + 1) * 128].rearrange("(h p) t -> p h t", p=128), reads=["VT"], writes=[f"vT8_{par}"])
                    if not isctx:
                        S.dma(QB_[par], QT[:, c * 128:(c + 1) * 128].rearrange("(h p) t -> p h t", p=128), reads=["QT"], writes=[f"qT8_{par}"])
                    for h in range(8):
                        prob(d, c, h, pi, par)
                        pi += 1
                    if isctx:
                        continue
                    if GSTOP < 4: continue
                    if d == 0:
                        S.dma(OFs[c * 128:(c + 1) * 128], O8[par], reads=[f"O8_{par}"], writes=["OFs"])
                    else:
                        def epilogue(c=c, par=par):
                            oft = OFt[par]; zg = ZGt[par]; ogc = OGc[par]; o8 = O8[par]
                            S.dma(oft, OFs[c * 128:(c + 1) * 128], reads=["OFs"], writes=[f"OFt_{par}"])
                            S.dma(zg, ZG[:, c * 128:(c + 1) * 128].rearrange("(h p) t -> p h t", p=128), reads=["ZG"], writes=[f"ZGt_{par}"])
                            S.pool(lambda e: e.tensor_tensor(o8, o8, oft, op=ALU.add), reads=[f"O8_{par}", f"OFt_{par}"], writes=[f"O8_{par}"])
                            S.dve(lambda e: e.tensor_tensor(SQn, o8, o8, op=ALU.mult), reads=[f"O8_{par}"], writes=["SQn"])
                            S.dve(lambda e: e.tensor_reduce(ssn, SQn, axis=AX.X, op=ALU.add), reads=["SQn"], writes=["ssn"])
                            S.dve(lambda e: e.tensor_scalar(ssn, ssn, 1.0 / 128.0, EPS, op0=ALU.mult, op1=ALU.add), reads=["ssn"], writes=["ssn"])
                            S.act(lambda e: e.activation(ssn, ssn, AF.Ln), reads=["ssn"], writes=["ssn"])
                            S.act(lambda e: e.activation(ssn, ssn, AF.Exp, scale=-0.5), reads=["ssn"], writes=["ssn"])
                            for h in range(8):
                                S.dve(lambda e, h=h: e.tensor_scalar(ONn[:, h, :], o8[:, h, :], ssn[:, h:h + 1], None, op0=ALU.mult), reads=[f"O8_{par}", "ssn"], writes=["ONn"])
                                pt_, kt_ = next_pg()
                                S.pe(lambda e, h=h, pt_=pt_: e.transpose(pt_, ONn[:, h, :], ident), reads=["ONn", "ident"], writes=[kt_])
                                S.dve(lambda e, h=h, pt_=pt_: e.scalar_tensor_tensor(ogc[:, h, :], in0=pt_, scalar=gng[:, 0:1], in1=zg[:, h, :], op0=ALU.mult, op1=ALU.mult), reads=[kt_, "gng", f"ZGt_{par}"], writes=[f"OGc_{par}"])
                            S.dma(OG[:, c * 128:(c + 1) * 128].rearrange("(h p) t -> p h t", p=128), ogc, reads=[f"OGc_{par}"], writes=["OG"])
                        if GSTOP >= 5: epilogue()
            st_g.__exit__(None, None, None)
            S.barrier()

        if "T" in stages:
            st_t = contextlib.ExitStack(); st_t.__enter__()
            sb = alloc_main(st_t)
            H, XM, ACT, TMP = C.H, C.XM, C.ACT, C.TMP
            ZHt = sb("ZHt", [128, 8, NT], BF16); OGt = sb("OGt", [128, 8, NT], BF16); MRG = sb("MRG", [128, KC, NT], BF16)
            GATES = ACT
            ttiles = [t * NT for t in range(L // NT)]
            if dbg == "short":
                ttiles = ttiles[:1]
            def do_tail(t0):
                T = NT
                S.dma(H, H1T[:, t0:t0 + T].rearrange("(k p) t -> p k t", p=128), reads=["H1T"], writes=["H"])
                S.dma(ZHt, ZH[:, t0:t0 + T].rearrange("(k p) t -> p k t", p=128), reads=["ZH"], writes=["ZHt"])
                S.dma(OGt, OG[:, t0:t0 + T].rearrange("(k p) t -> p k t", p=128), reads=["OG"], writes=["OGt"])
                norm_mod(T, 1, 0)
                def ev_gate(ci, ps, pk):
                    S.act(lambda e, ps=ps, ci=ci: e.activation(GATES[:, ci, :T], ps[:, :T], AF.Sigmoid), reads=[pk], writes=["ACT"])
                linear(T, w_in, KC, GATE_OFF, 2 * D, ev_gate)
                for fb in range(4):
                    wbuf, wkey = next_wb()
                    wa = wbuf[:, 0:4096].rearrange("p (k n) -> p k n", k=8); wb_ = wbuf[:, 4096:8192].rearrange("p (k n) -> p k n", k=8)
                    S.dma_cast(wa, hy_out[:, fb * 512:(fb + 1) * 512].rearrange("(k p) n -> p k n", p=128), writes=[wkey])
                    S.dma_cast(wb_, gdn_out[:, fb * 512:(fb + 1) * 512].rearrange("(k p) n -> p k n", p=128), writes=[wkey])
                    for jj in range(4):
                        ci = fb * 4 + jj
                        psA, pkA = next_ps(); psB, pkB = next_ps()
                        for k in range(8):
                            S.pe(lambda e, psA=psA, wa=wa, k=k, jj=jj: e.matmul(psA[:, :T], lhsT=wa[:, k, jj * 128:(jj + 1) * 128], rhs=ZHt[:, k, :T], start=(k == 0), stop=(k == 7)), reads=[wkey, "ZHt"], writes=[pkA])
                        for k in range(8):
                            S.pe(lambda e, psB=psB, wb_=wb_, k=k, jj=jj: e.matmul(psB[:, :T], lhsT=wb_[:, k, jj * 128:(jj + 1) * 128], rhs=OGt[:, k, :T], start=(k == 0), stop=(k == 7)), reads=[wkey, "OGt"], writes=[pkB])
                        t1 = TMP[0]; t2 = TMP[1]
                        S.dve(lambda e, psA=psA, ci=ci: e.tensor_tensor(t1[:, :T], GATES[:, ci, :T], psA[:, :T], op=ALU.mult), reads=["ACT", pkA], writes=["tmp0"])
                        S.dve(lambda e, psB=psB, ci=ci: e.tensor_tensor(t2[:, :T], GATES[:, KC + ci, :T], psB[:, :T], op=ALU.mult), reads=["ACT", pkB], writes=["tmp1"])
                        S.pool(lambda e, ci=ci: e.tensor_tensor(MRG[:, ci, :T], t1[:, :T], t2[:, :T], op=ALU.add), reads=["tmp0", "tmp1"], writes=["MRG"])
                def ev_o(ci, ps, pk):
                    S.dve(lambda e, ps=ps, ci=ci: e.scalar_tensor_tensor(H[:, ci, :T], in0=ps[:, :T], scalar=Gv[:, 1, ci, 0:1], in1=H[:, ci, :T], op0=ALU.mult, op1=ALU.add),
                          reads=[pk, "H", "Gv"], writes=["H"])
                linear(T, w_o, KC, 0, D, ev_o, xsrc=MRG, xkey="MRG")
                if dbg:
                    S.dma(dbg_outs["h2"][:, t0:t0 + T].rearrange("(k p) t -> p k t", p=128), H, reads=["H"], writes=["dbg_h2"])
                ffn(T, 1, 2, 0)
                pss = PS[6]
                for k in range(KC):
                    S.act(lambda e, k=k: e.activation(ACT[:, k, :T], H[:, k, :T], AF.Square), reads=["H"], writes=["ACT"])
                for k in range(KC):
                    S.pe(lambda e, k=k: e.matmul(pss[:, :T], lhsT=onesb, rhs=ACT[:, k, :T], start=(k == 0), stop=(k == KC - 1)), reads=["ACT", "onesb"], writes=["PS6"])
                rs = TMP[5]
                S.dve(lambda e: e.tensor_scalar(rs[:, :T], pss[:, :T], 1.0 / D, EPS, op0=ALU.mult, op1=ALU.add), reads=["PS6"], writes=["tmp5"])
                S.act(lambda e: e.activation(rs[:, :T], rs[:, :T], AF.Ln), reads=["tmp5"], writes=["tmp5"])
                S.act(lambda e: e.activation(rs[:, :T], rs[:, :T], AF.Exp, scale=-0.5), reads=["tmp5"], writes=["tmp5"])
                for k in range(KC):
                    S.dve(lambda e, k=k: e.scalar_tensor_tensor(H[:, k, :T], in0=H[:, k, :T], scalar=fg[:, k:k + 1], in1=rs[:, :T], op0=ALU.mult, op1=ALU.mult), reads=["H", "tmp5", "fg"], writes=["H"])
                S.dma(out_d[:, t0:t0 + T].rearrange("(k p) t -> p k t", p=128), H, reads=["H"], writes=["outT"])
            if dbg:
                dbg_outs["h2"] = nc.dram_tensor("dbg_h2", [D, L], F32, kind="ExternalOutput").ap()
            for t0 in ttiles:
                do_tail(t0)
            st_t.__exit__(None, None, None)
            S.barrier()
        C.final_keys = []
        if dbg and "G" in stages:
            dbg_outs["OG"] = nc.dram_tensor("dbg_OG", [GW, L], BF16, kind="ExternalOutput").ap()
            S.dma(dbg_outs["OG"], OG, reads=["OG"], writes=["dbg_OG"])
            dbg_outs["OFs"] = nc.dram_tensor("dbg_OFs", [L, 8, 128], F32, kind="ExternalOutput").ap()
            S.dma(dbg_outs["OFs"], OFs, reads=["OFs"], writes=["dbg_OFs"])
        if dbg and "H" in stages:
            for nm in ("Z2T", "ZH"):
                a = C.scr[nm]
                dbg_outs[nm] = nc.dram_tensor("dbg_" + nm, list(a.shape), BF16, kind="ExternalOutput").ap()
                S.dma(dbg_outs[nm], a, reads=[nm], writes=["dbg_" + nm])
        if dbg and "F" in stages:
            for nm in ("KCs", "KSs"):
                a = C.scr[nm]
                dbg_outs[nm] = nc.dram_tensor("dbg_" + nm, list(a.shape), F32, kind="ExternalOutput").ap()
                S.dma(dbg_outs[nm], a, reads=[nm], writes=["dbg_" + nm])
        if dbg and "A" in stages:
            for nm in ("H1T", "QT", "KT", "VT", "ABs"):
                a = C.scr[nm]
                dbg_outs[nm] = nc.dram_tensor("dbg_" + nm, list(a.shape), F32, kind="ExternalOutput").ap()
                S.dma(dbg_outs[nm], a, reads=[nm], writes=["dbg_" + nm])
            for nm in ("HYT", "ZG"):
                a = C.scr[nm]
                if nm in ext: continue
                dbg_outs[nm] = nc.dram_tensor("dbg_" + nm, list(a.shape), BF16, kind="ExternalOutput").ap()
                S.dma(dbg_outs[nm], a, reads=[nm], writes=["dbg_" + nm])
        S.emit(final_keys=[k for k in S.last_w if isinstance(k, str) and (k.startswith("dbg_") or k == "outT")])
    return nc, C


def _prep_core(inputs, b):
    f = np.float32
    g = lambda k: np.asarray(inputs[k])
    m = {}
    m["xT"] = np.ascontiguousarray(g("x")[b].T, dtype=f)
    m["ctxT"] = np.ascontiguousarray(g("ctx")[b].T, dtype=f)
    s = np.stack([g("c")[b], g("c_ctx")], 0)
    m["sT"] = np.ascontiguousarray(s.reshape(2, KC, 128).transpose(2, 1, 0), dtype=f)
    m["w_mod"] = np.ascontiguousarray(g("w_mod")[0], dtype=f)
    m["b_mod"] = np.ascontiguousarray(g("b_mod")[0].reshape(9 * KC, 128).T, dtype=f)
    m["norm_g"] = np.ascontiguousarray(g("norm_g")[0].reshape(3 * KC, 128).T, dtype=f)
    m["fin_g"] = np.ascontiguousarray(g("final_norm_g").reshape(KC, 128).T, dtype=f)
    for k in ("ffn1_wgu", "ffn1_wd", "ffn2_wgu", "ffn2_wd", "w_in", "hy_out", "gdn_out", "w_o"):
        m[k] = np.ascontiguousarray(g(k)[0], dtype=f)
    m["hy_cw"] = np.ascontiguousarray(g("hy_conv_w")[0].reshape(3, 24, 128).transpose(2, 1, 0), dtype=f)
    m["hy_cb"] = np.ascontiguousarray(g("hy_conv_b")[0].reshape(24, 128).T, dtype=f)
    m["g_cw"] = np.ascontiguousarray(g("gdn_conv_w")[0].reshape(3, 24, 128).transpose(2, 1, 0), dtype=f)
    return m


_CONST = {}
def _consts():
    if _CONST:
        return _CONST
    import ml_dtypes
    f = np.float32
    Lq = 4096; N = 8192
    pos = np.arange(Lq, dtype=f)
    t = (pos / f(Lq))[:, None].astype(f)
    fb = np.linspace(1e-4, 15, 16, dtype=f)
    ang = (f(2 * math.pi) * t * fb).astype(f)
    z = np.concatenate([t, np.cos(ang), -np.sin(ang)], -1).astype(f)
    _CONST["zT"] = np.ascontiguousarray(z.T)
    _CONST["deltas"] = np.abs(np.linspace(math.log(1e-2) / 1.5, math.log(1e-2) / 0.3, 1024, dtype=f)).reshape(1, 1024).astype(f)
    _CONST["that"] = np.ascontiguousarray((pos / f(Lq)).reshape(32, 128).T)
    mb0 = np.ones((128, 1), f); mb0[0, 0] = 0.0
    _CONST["mb0"] = mb0
    tt = np.arange(Lq, dtype=np.int64)[:, None]; ff = np.arange(Lq, dtype=np.int64)[None, :]
    m = ((2 * ff + 1) * tt) % (2 * N)
    th = (2.0 * np.pi / (2 * N)) * m.astype(np.float64)
    Cm = np.cos(th).astype(ml_dtypes.bfloat16); Sm = np.sin(th).astype(ml_dtypes.bfloat16)
    del th, m
    def fwd_tile(M):
        return np.ascontiguousarray(M.reshape(32, 128, 32, 128).transpose(2, 1, 0, 3))
    def inv_tile(M):
        return np.ascontiguousarray(M.reshape(32, 128, 32, 128).transpose(0, 3, 2, 1))
    _CONST["CFt"] = fwd_tile(Cm); _CONST["SFt"] = fwd_tile(Sm); _CONST["CIt"] = inv_tile(Cm); _CONST["SIt"] = inv_tile(Sm)
    return _CONST


def _prep_core2(inputs, b, m):
    f = np.float32
    g = lambda k: np.asarray(inputs[k])
    m.update(_consts())
    m["f_w1"] = np.ascontiguousarray(g("hy_f_w1")[0], dtype=f); m["f_w2"] = np.ascontiguousarray(g("hy_f_w2")[0], dtype=f)
    m["f_w3"] = np.ascontiguousarray(g("hy_f_w3")[0], dtype=f)
    m["f_b1"] = np.ascontiguousarray(g("hy_f_b1")[0].reshape(64, 1), dtype=f); m["f_b2"] = np.ascontiguousarray(g("hy_f_b2")[0].reshape(64, 1), dtype=f)
    m["f_freq"] = np.ascontiguousarray(g("hy_sin_freq")[0].reshape(64, 1), dtype=f)
    m["hy_bias"] = np.ascontiguousarray(g("hy_bias")[0], dtype=f)
    a = np.arange(128)
    tri = np.stack([a[:, None] <= a[None, :], a[:, None] >= a[None, :], a[:, None] > a[None, :], a[:, None] < a[None, :]], 1).astype(f)
    m["tri"] = np.ascontiguousarray(tri)
    m["a_log"] = np.ascontiguousarray(g("gdn_a_log")[0].reshape(1, 16), dtype=f)
    m["dt_bias"] = np.ascontiguousarray(g("gdn_dt_bias")[0].reshape(1, 16), dtype=f)
    m["gdn_ng"] = np.ascontiguousarray(g("gdn_norm_g")[0].reshape(128, 1), dtype=f)
    return m


_PROG = {}

def kernel(**inputs):
    if "nc" not in _PROG:
        nc = bass.Bass("TRN2", target_bir_lowering=False)
        nc, C = build_program(nc, dbg=False)
        _PROG["nc"] = nc; _PROG["din"] = set(C.din)
    nc = _PROG["nc"]
    B = np.asarray(inputs["x"]).shape[0]
    in_maps = []
    percore = {}
    for r in range(8):
        b = r // 2
        if b not in percore:
            m = _prep_core2(inputs, b, _prep_core(inputs, b))
            percore[b] = {k: v for k, v in m.items() if k in _PROG["din"]}
        in_maps.append(percore[b])
    res = run_bass_kernel_spmd(nc, in_maps, core_ids=list(range(8)))
    out = np.empty((B, L, D), np.float32)
    for b in range(B):
        out[b] = res.results[2 * b]["outT"].T
    return out
```

```python
import math
import contextlib
import numpy as np
import concourse.bass as bass
import concourse.mybir as mybir
from concourse.bass_utils import run_bass_kernel_spmd

F32 = mybir.dt.float32
BF16 = mybir.dt.bfloat16
AF = mybir.ActivationFunctionType
ALU = mybir.AluOpType
AX = mybir.AxisListType

EPOCH = 4096
ENGS = ("tensor", "vector", "scalar", "gpsimd", "sync")
N_DMA_SEMS = 12


class Op:
    __slots__ = ("eng", "fn", "deps", "idx", "needs_inc", "dma", "dma_tok", "dma_prev", "inc_no")

    def __init__(self, eng, fn):
        self.eng = eng
        self.fn = fn
        self.deps = []
        self.idx = None
        self.needs_inc = False
        self.dma = False
        self.dma_tok = None
        self.dma_prev = None


class Sched:
    def __init__(self, nc):
        self.nc = nc
        self.ops = {e: [] for e in ENGS}
        self.last_w = {}
        self.readers = {}
        self.dma_rr = {e: 0 for e in ENGS}
        self.dma_cnt = {}
        self.dma_last = {}
        self.n_ops = 0
        self.bar_ops = []
        self.bar_gen = 0
        self.bar_applied = {e: 0 for e in ENGS}

    def barrier(self):
        self.bar_gen += 1
        b = []
        for e in ENGS:
            for op in reversed(self.ops[e]):
                if not op.dma:
                    b.append(op)
                    break
        b.extend(self.dma_last.values())
        self.bar_ops = b

    def _add(self, eng, fn, reads, writes, dma=False, pe_acc=False):
        op = Op(eng, fn)
        op.dma = dma
        deps = []
        for k in reads:
            w = self.last_w.get(k)
            if w is not None:
                deps.append(w)
        for k in writes:
            w = self.last_w.get(k)
            if w is not None:
                deps.append(w)
            for r in self.readers.get(k, ()):
                deps.append(r)
        if self.bar_applied[eng] != self.bar_gen:
            deps.extend(self.bar_ops)
            self.bar_applied[eng] = self.bar_gen
        seen = set()
        for d in deps:
            if d is op or id(d) in seen:
                continue
            seen.add(id(d))
            if (not d.dma) and (not dma) and d.eng == "tensor" and eng == "tensor":
                continue
            op.deps.append(d)
        for k in reads:
            self.readers.setdefault(k, []).append(op)
        for k in writes:
            self.last_w[k] = op
            self.readers[k] = []
        op.idx = len(self.ops[eng])
        self.ops[eng].append(op)
        if dma:
            slot = self.dma_rr[eng] % N_DMA_SEMS
            self.dma_rr[eng] += 1
            key = (eng, slot)
            cnt = self.dma_cnt.get(key, 0) + 1
            self.dma_cnt[key] = cnt
            op.dma_tok = (key, cnt * 16)
            op.dma_prev = self.dma_last.get(key)
            self.dma_last[key] = op
        self.n_ops += 1
        return op

    def pe(self, fn, reads=(), writes=()):
        return self._add("tensor", fn, reads, writes)

    def dve(self, fn, reads=(), writes=()):
        return self._add("vector", fn, reads, writes)

    def act(self, fn, reads=(), writes=()):
        return self._add("scalar", fn, reads, writes)

    def pool(self, fn, reads=(), writes=()):
        return self._add("gpsimd", fn, reads, writes)

    def dma(self, out, in_, reads=(), writes=(), eng="sync"):
        return self._add(eng, lambda e: e.dma_start(out=out, in_=in_), reads, writes, dma=True)

    def dma_fn(self, fn, reads=(), writes=(), eng="gpsimd"):
        return self._add(eng, fn, reads, writes, dma=True)

    def dma_cast(self, out, in_, reads=(), writes=()):
        return self.dma(out, in_, reads, writes, eng="gpsimd")

    def emit(self, final_keys=()):
        nc = self.nc
        fin = Op("sync", None)
        for k in final_keys:
            w = self.last_w.get(k)
            if w is not None:
                fin.deps.append(w)
        for e in ENGS:
            for op in self.ops[e]:
                for d in op.deps:
                    if not d.dma:
                        d.needs_inc = True
        for d in fin.deps:
            if not d.dma:
                d.needs_inc = True
        n_inc = {}
        for e in ENGS:
            c = 0
            for op in self.ops[e]:
                if (not op.dma) and op.needs_inc:
                    op.inc_no = c
                    c += 1
            n_inc[e] = c
        import contextlib
        stack = contextlib.ExitStack()
        sems = {}
        with stack:
            for e in ENGS:
                n = n_inc[e]
                for ep in range((n + EPOCH - 1) // EPOCH + 1):
                    sems[(e, ep)] = stack.enter_context(nc.semaphore(f"p_{e}_{ep}"))
            for key in self.dma_cnt:
                sems[("dma",) + key] = stack.enter_context(nc.semaphore(f"d_{key[0]}_{key[1]}"))
            block = stack.enter_context(nc.Block())

            def tok(d):
                if d.dma:
                    return (sems[("dma",) + d.dma_tok[0]], d.dma_tok[1], ("dma",) + d.dma_tok[0])
                ep, r = divmod(d.inc_no, EPOCH)
                return (sems[(d.eng, ep)], r + 1, (d.eng, ep))

            def run(engname):
                def body(eng):
                    waited = {}
                    def do_wait(d):
                        s, v, key = tok(d)
                        if waited.get(key, 0) >= v:
                            return
                        waited[key] = v
                        eng.wait_ge(s, v)
                    for op in self.ops[engname]:
                        for d in op.deps:
                            if (not d.dma) and d.eng == engname and d.idx >= op.idx:
                                continue
                            do_wait(d)
                        if op.dma and op.dma_prev is not None:
                            do_wait(op.dma_prev)
                        ins = op.fn(eng)
                        if op.dma:
                            s, v, _ = tok(op)
                            ins.then_inc(s, 16)
                        elif op.needs_inc:
                            ep, _r = divmod(op.inc_no, EPOCH)
                            ins.then_inc(sems[(engname, ep)], 1)
                    if engname == "sync":
                        for d in fin.deps:
                            do_wait(d)
                return body

            block.tensor(run("tensor"))
            block.vector(run("vector"))
            block.scalar(run("scalar"))
            block.gpsimd(run("gpsimd"))
            block.sync(run("sync"))

D = 2048; KC = 16; DFF = 5632; HC = 44; L = 4096; LC = 256; NT = 512
HYW = 1024; GW = 1024; NH = 8
HY_COLS = 3072; Q_OFF = 3072; KV_OFF = 4096; Z_OFF = 6144; AB_OFF = 7168; GATE_OFF = 7200; IN_COLS = 11296
EPS = 1e-6
LTOT = L + LC


class Ctx:
    pass


def build_program(nc, dbg=False, stages=("A", "F", "H", "G", "T"), ext=()):
    S = Sched(nc)
    C = Ctx()
    C.nc = nc; C.S = S
    din = {}
    def inp(name, shape, dt=F32):
        din[name] = nc.dram_tensor(name, list(shape), dt, kind="ExternalInput").ap()
        return din[name]
    def scratch(name, shape, dt=F32):
        if name in ext:
            return inp(name, shape, dt)
        return nc.dram_tensor(name, list(shape), dt, kind="Internal").ap()
    xT = inp("xT", [D, L]); cT = inp("ctxT", [D, LC]); sT = inp("sT", [128, KC, 2])
    w_mod = inp("w_mod", [D, 9 * D]); b_mod = inp("b_mod", [128, 9 * KC])
    norm_g = inp("norm_g", [128, 3 * KC]); fin_g = inp("fin_g", [128, KC])
    wgu = [inp("ffn1_wgu", [D, 2 * DFF]), inp("ffn2_wgu", [D, 2 * DFF])]
    wd = [inp("ffn1_wd", [DFF, D]), inp("ffn2_wd", [DFF, D])]
    w_in = inp("w_in", [D, IN_COLS])
    hy_cw = inp("hy_cw", [128, 24, 3]); hy_cb = inp("hy_cb", [128, 24]); g_cw = inp("g_cw", [128, 24, 3])
    hy_out = inp("hy_out", [HYW, D]); gdn_out = inp("gdn_out", [GW, D]); w_o = inp("w_o", [D, D])
    zT_d = inp("zT", [33, L]); fw1 = inp("f_w1", [33, 64]); fw2 = inp("f_w2", [64, 64]); fw3 = inp("f_w3", [64, 4096])
    fb1 = inp("f_b1", [64, 1]); fb2 = inp("f_b2", [64, 1]); ffr = inp("f_freq", [64, 1])
    deltas_d = inp("deltas", [1, 1024]); that_d = inp("that", [128, 32]); mb0_d = inp("mb0", [128, 1])
    CFt = inp("CFt", [32, 128, 32, 128], BF16); SFt = inp("SFt", [32, 128, 32, 128], BF16)
    CIt = inp("CIt", [32, 128, 32, 128], BF16); SIt = inp("SIt", [32, 128, 32, 128], BF16)
    hyb_d = inp("hy_bias", [2, 1024])
    tri_d = inp("tri", [128, 4, 128]); alog_d = inp("a_log", [1, 16]); dtb_d = inp("dt_bias", [1, 16]); gng_d = inp("gdn_ng", [128, 1])
    out_d = nc.dram_tensor("outT", [D, L], F32, kind="ExternalOutput").ap()
    C.din = din
    H1T = scratch("H1T", [D, L]); HYT = scratch("HYT", [L, HY_COLS], BF16)
    QT = scratch("QT", [GW, LTOT]); KT = scratch("KT", [GW, LTOT]); VT = scratch("VT", [GW, LTOT])
    ZG = scratch("ZG", [GW, L], BF16); ABs = scratch("ABs", [LTOT, 32])
    ZH = scratch("ZH", [HYW, L], BF16); OG = scratch("OG", [GW, L], BF16)
    OFs = scratch("OFs", [L, 8, 128])
    KCs = scratch("KCs", [L, 2048]); KSs = scratch("KSs", [L, 2048])
    Z2T = scratch("Z2T", [L, 1024], BF16); Ps = scratch("Ps", [L, 1024], BF16); Qs = scratch("Qs", [L, 1024], BF16)
    C.scr = dict(KCs=KCs, KSs=KSs, Z2T=Z2T, H1T=H1T, HYT=HYT, QT=QT, KT=KT, VT=VT, ZG=ZG, ABs=ABs, ZH=ZH, OG=OG)
    dbg_outs = {}
    C.dbg_outs = dbg_outs
    Wf = dict(ffn1_wgu=wgu[0], ffn1_wd=wd[0], w_in=w_in, ffn2_wgu=wgu[1], ffn2_wd=wd[1], hy_out=hy_out, gdn_out=gdn_out, w_o=w_o)
    Wb = {k: nc.dram_tensor(k + "_b16", list(v.shape), BF16, kind="Internal").ap() for k, v in Wf.items()}
    def convert_w(name):
        src = Wf[name]; dst = Wb[name]
        R_ = src.shape[0]
        step = 512 if R_ % 512 == 0 else 128
        for r0 in range(0, R_, step):
            S.dma_cast(dst[r0:r0 + step, :].rearrange("(p a) n -> p (a n)", p=128), src[r0:r0 + step, :].rearrange("(p a) n -> p (a n)", p=128), writes=["Wb_" + name])
    wgu_b = [Wb["ffn1_wgu"], Wb["ffn2_wgu"]]; wd_b = [Wb["ffn1_wd"], Wb["ffn2_wd"]]

    stack = contextlib.ExitStack()
    with stack:
        def sb(name, shape, dt=F32):
            return stack.enter_context(nc.sbuf_tensor(name, list(shape), dt)).ap()
        PS = [nc.alloc_psum_tensor(f"psb{i}", [128, 512], F32).ap() for i in range(8)]
        C.PS = PS
        C.ps_rr = 0
        def next_ps():
            i = C.ps_rr % 6
            C.ps_rr += 1
            return PS[i], f"PS{i}"
        C.next_ps = next_ps
        ident = sb("ident", [128, 128]); identb = sb("identb", [128, 128], BF16)
        onesb = sb("onesb", [128, 128], BF16); onesf = sb("onesf", [128, 128])
        S.pool(lambda e: e.memset(ident, 1.0), writes=["ident"])
        S.pool(lambda e: e.affine_select(ident, ident, pattern=[[-1, 128]], compare_op=ALU.is_equal, fill=0.0, base=0, channel_multiplier=1), reads=["ident"], writes=["ident"])
        S.dve(lambda e: e.tensor_copy(identb, ident), reads=["ident"], writes=["identb"])
        S.pool(lambda e: e.memset(onesf, 1.0), writes=["onesf"])
        S.dve(lambda e: e.memset(onesb, 1.0), writes=["onesb"])
        C.ident = ident; C.identb = identb; C.onesb = onesb; C.onesf = onesf
        modv = sb("modv", [128, 9 * KC, 2]); sTs = sb("sTs", [128, KC, 2]); sTb = sb("sTb", [128, KC, 2], BF16)
        bmod = sb("bmod", [128, 9 * KC]); ng = sb("ng", [128, 3 * KC]); fg = sb("fg", [128, KC])
        Av = sb("Av", [128, 3, KC, 2]); Bv = sb("Bv", [128, 3, KC, 2]); Gv = sb("Gv", [128, 3, KC, 2])
        S.dma(sTs, sT, writes=["sTs"]); S.dma(bmod, b_mod, writes=["bmod"]); S.dma(ng, norm_g, writes=["ng"]); S.dma(fg, fin_g, writes=["fg"])
        S.act(lambda e: e.activation(sTb, sTs, AF.Silu), reads=["sTs"], writes=["sTb"])
        C.wb_rr = 0
        def next_wb():
            i = C.wb_rr % 3
            C.wb_rr += 1
            return C.WB[i], f"WB{i}"
        C.next_wb = next_wb
        def alloc_main(st):
            def sbl(name, shape, dt=F32):
                return st.enter_context(nc.sbuf_tensor(name, list(shape), dt)).ap()
            C.gen = getattr(C, "gen", 0) + 1
            g = C.gen
            C.WB = [sbl(f"wb{i}_{g}", [128, 8192], BF16) for i in range(3)]
            C.H = sbl(f"H_{g}", [128, KC, NT]); C.XM = sbl(f"XM_{g}", [128, KC, NT], BF16); C.ACT = sbl(f"ACT_{g}", [128, HC, NT], BF16)
            C.TMP = [sbl(f"tmp{i}_{g}", [128, NT]) for i in range(6)]
            return sbl
        st_mod = contextlib.ExitStack()
        st_mod.__enter__()
        C.WB = [st_mod.enter_context(nc.sbuf_tensor(f"wbm{i}", [128, 8192], BF16)).ap() for i in range(3)]
        for nm_ in ("ffn1_wgu", "ffn1_wd", "w_in"):
            convert_w(nm_)
        for cb in range(9 * D // 512):
            wbuf, wkey = next_wb()
            wv = wbuf.rearrange("p (k n) -> p k n", k=KC)
            S.dma_cast(wv, w_mod[:, cb * 512:(cb + 1) * 512].rearrange("(k p) n -> p k n", p=128), writes=[wkey])
            for j in range(4):
                ps, pk = next_ps()
                for k in range(KC):
                    S.pe(lambda e, ps=ps, wv=wv, k=k, j=j: e.matmul(ps[:, 0:2], lhsT=wv[:, k, j * 128:(j + 1) * 128], rhs=sTb[:, k, :], start=(k == 0), stop=(k == KC - 1)),
                         reads=[wkey, "sTb"], writes=[pk])
                col = cb * 4 + j
                S.dve(lambda e, ps=ps, col=col: e.tensor_scalar(modv[:, col, :], ps[:, 0:2], bmod[:, col:col + 1], None, op0=ALU.add),
                      reads=[pk, "bmod"], writes=["modv"])
        for j in range(3):
            for r in range(2):
                sh = modv[:, (3 * j) * KC:(3 * j + 1) * KC, r]; sc = modv[:, (3 * j + 1) * KC:(3 * j + 2) * KC, r]; gt = modv[:, (3 * j + 2) * KC:(3 * j + 3) * KC, r]
                S.dve(lambda e, sc=sc, j=j, r=r: e.scalar_tensor_tensor(Av[:, j, :, r], in0=sc, scalar=1.0, in1=ng[:, j * KC:(j + 1) * KC], op0=ALU.add, op1=ALU.mult),
                      reads=["modv", "ng"], writes=["Av"])
                S.dve(lambda e, sh=sh, j=j, r=r: e.tensor_copy(Bv[:, j, :, r], sh), reads=["modv"], writes=["Bv"])
                S.dve(lambda e, gt=gt, j=j, r=r: e.tensor_scalar(Gv[:, j, :, r], gt, (1.0 if j == 1 else 0.5), None, op0=ALU.mult), reads=["modv"], writes=["Gv"])
        C.Av = Av; C.Bv = Bv; C.Gv = Gv; C.fg = fg
        if dbg:
            dbg_outs["modv"] = nc.dram_tensor("dbg_modv", [128, 9 * KC, 2], F32, kind="ExternalOutput").ap()
            S.dma(dbg_outs["modv"], modv, reads=["modv"], writes=["dbg_modv"])
        for nm_ in ("ffn2_wgu", "ffn2_wd", "hy_out", "gdn_out", "w_o"):
            convert_w(nm_)
        st_mod.__exit__(None, None, None)
        S.barrier()

        def norm_mod(T, j, r, src_key="H"):
            H, XM, ACT, TMP = C.H, C.XM, C.ACT, C.TMP
            pss, psk = PS[6], "PS6"
            sq = ACT
            for k in range(KC):
                S.act(lambda e, k=k: e.activation(sq[:, k, :T], H[:, k, :T], AF.Square), reads=["H"], writes=["ACT"])
            for k in range(KC):
                S.pe(lambda e, k=k: e.matmul(pss[:, :T], lhsT=onesb, rhs=sq[:, k, :T], start=(k == 0), stop=(k == KC - 1)), reads=["ACT", "onesb"], writes=[psk])
            rs = TMP[5]
            S.dve(lambda e: e.tensor_scalar(rs[:, :T], pss[:, :T], 1.0 / D, EPS, op0=ALU.mult, op1=ALU.add), reads=[psk], writes=["tmp5"])
            S.act(lambda e: e.activation(rs[:, :T], rs[:, :T], AF.Ln), reads=["tmp5"], writes=["tmp5"])
            S.act(lambda e: e.activation(rs[:, :T], rs[:, :T], AF.Exp, scale=-0.5), reads=["tmp5"], writes=["tmp5"])
            for k in range(KC):
                t = TMP[k % 2]; tk = f"tmp{k % 2}"
                S.dve(lambda e, k=k, t=t: e.tensor_tensor(t[:, :T], H[:, k, :T], rs[:, :T], op=ALU.mult), reads=["H", "tmp5"], writes=[tk])
                S.act(lambda e, k=k, t=t: e.activation(XM[:, k, :T], t[:, :T], AF.Identity, bias=Bv[:, j, k, r:r + 1], scale=Av[:, j, k, r:r + 1]),
                      reads=[tk, "Av", "Bv"], writes=["XM"])
        C.norm_mod = norm_mod

        def linear(T, wname, kc, col0, ncols, evac, cw=512, xsrc=None, xkey="XM"):
            w_ap = Wb[wname]; wsrc_key = "Wb_" + wname
            xs = C.XM if xsrc is None else xsrc
            cw = min(cw, 8192 // kc)
            nblk = (ncols + cw - 1) // cw
            ci = 0
            for b in range(nblk):
                c0 = col0 + b * cw
                w = min(cw, col0 + ncols - c0)
                wbuf, wkey = next_wb()
                wv = wbuf[:, :kc * w].rearrange("p (k n) -> p k n", k=kc)
                S.dma(wv, w_ap[:, c0:c0 + w].rearrange("(k p) n -> p k n", p=128), reads=[wsrc_key], writes=[wkey])
                for jj in range((w + 127) // 128):
                    m = min(128, w - jj * 128)
                    ps, pk = next_ps()
                    for k in range(kc):
                        S.pe(lambda e, ps=ps, wv=wv, k=k, jj=jj, m=m: e.matmul(ps[:m, :T], lhsT=wv[:, k, jj * 128:jj * 128 + m], rhs=xs[:, k, :T], start=(k == 0), stop=(k == kc - 1)),
                             reads=[wkey, xkey], writes=[pk])
                    evac(ci, ps, pk)
                    ci += 1
        C.linear = linear

        def ffn(T, fi, j, r):
            H, XM, ACT, TMP = C.H, C.XM, C.ACT, C.TMP
            norm_mod(T, j, r)
            for hb in range(HC // 4):
                wg_b, wgk = next_wb(); wu_b, wuk = next_wb()
                wgv = wg_b.rearrange("p (k n) -> p k n", k=KC); wuv = wu_b.rearrange("p (k n) -> p k n", k=KC)
                gk_ = "Wb_ffn%d_wgu" % (fi + 1)
                S.dma(wgv, wgu_b[fi][:, hb * 512:(hb + 1) * 512].rearrange("(k p) n -> p k n", p=128), reads=[gk_], writes=[wgk])
                S.dma(wuv, wgu_b[fi][:, DFF + hb * 512:DFF + (hb + 1) * 512].rearrange("(k p) n -> p k n", p=128), reads=[gk_], writes=[wuk])
                for jj in range(4):
                    hc = hb * 4 + jj
                    psg, pgk = next_ps(); psu, puk = next_ps()
                    for k in range(KC):
                        S.pe(lambda e, psg=psg, wgv=wgv, k=k, jj=jj: e.matmul(psg[:, :T], lhsT=wgv[:, k, jj * 128:(jj + 1) * 128], rhs=XM[:, k, :T], start=(k == 0), stop=(k == KC - 1)), reads=[wgk, "XM"], writes=[pgk])
                    for k in range(KC):
                        S.pe(lambda e, psu=psu, wuv=wuv, k=k, jj=jj: e.matmul(psu[:, :T], lhsT=wuv[:, k, jj * 128:(jj + 1) * 128], rhs=XM[:, k, :T], start=(k == 0), stop=(k == KC - 1)), reads=[wuk, "XM"], writes=[puk])
                    t = TMP[2 + (hc % 2)]; tk = f"tmp{2 + (hc % 2)}"
                    S.act(lambda e, psg=psg, t=t: e.activation(t[:, :T], psg[:, :T], AF.Silu), reads=[pgk], writes=[tk])
                    S.dve(lambda e, psu=psu, t=t, hc=hc: e.tensor_tensor(ACT[:, hc, :T], t[:, :T], psu[:, :T], op=ALU.mult), reads=[tk, puk], writes=["ACT"])
            def ev(ci, ps, pk):
                S.dve(lambda e, ps=ps, ci=ci: e.scalar_tensor_tensor(H[:, ci, :T], in0=ps[:, :T], scalar=Gv[:, j, ci, r:r + 1], in1=H[:, ci, :T], op0=ALU.mult, op1=ALU.add),
                      reads=[pk, "H", "Gv"], writes=["H"])
            linear(T, "ffn%d_wd" % (fi + 1), HC, 0, D, ev, cw=128, xsrc=ACT, xkey="ACT")
        C.ffn = ffn

        if "A" in stages:
            st_a = contextlib.ExitStack(); st_a.__enter__()
            sb = alloc_main(st_a)
            H, XM, ACT, TMP = C.H, C.XM, C.ACT, C.TMP
            cwh = sb("cwh", [128, 24, 3]); cbh = sb("cbh", [128, 24]); cwg = sb("cwg", [128, 24, 3]); zb = sb("zb", [128, 24])
            S.dma(cwh, hy_cw, writes=["cwh"]); S.dma(cbh, hy_cb, writes=["cbh"]); S.dma(cwg, g_cw, writes=["cwg"])
            S.pool(lambda e: e.memset(zb, 0.0), writes=["zb"])
            wab = sb("wab", [128, KC, 32], BF16)
            S.dma(wab, Wb["w_in"][:, AB_OFF:AB_OFF + 32].rearrange("(k p) n -> p k n", p=128), reads=["Wb_w_in"], writes=["wab"])
            STG = sb("STG", [128, 4, 1024], BF16)
            tiles = [(0, t * NT, NT, 0) for t in range(L // NT)] + [(1, 0, LC, 1)]
            if dbg == "short":
                tiles = [tiles[0], tiles[-1]]
            def do_tile(isctx, t0, T, r):
                src = cT if isctx else xT
                S.dma(H[:, :, :T], src[:, t0:t0 + T].rearrange("(k p) t -> p k t", p=128), writes=["H"])
                ffn(T, 0, 0, r)
                if not isctx:
                    S.dma(H1T[:, t0:t0 + T].rearrange("(k p) t -> p k t", p=128), H[:, :, :T], reads=["H"], writes=["H1T"])
                norm_mod(T, 1, r)
                seg = 64 if not isctx else LC
                nseg = T // seg
                tcol = (L + t0) if isctx else t0

                def conv(ci, ps, pk, cw_t, cb_t, widx, outk):
                    pr = TMP[0]; y = TMP[1]
                    S.act(lambda e, ps=ps: e.copy(pr[:, :T], ps[:, :T]), reads=[pk], writes=["tmp0"])
                    S.dve(lambda e: e.tensor_scalar(y[:, :T], pr[:, :T], cw_t[:, widx, 1:2], cb_t[:, widx:widx + 1], op0=ALU.mult, op1=ALU.add), reads=["tmp0"], writes=["tmp1"])
                    prv = pr[:, :T].rearrange("p (s n) -> p s n", n=seg); yv = y[:, :T].rearrange("p (s n) -> p s n", n=seg)
                    S.dve(lambda e: e.scalar_tensor_tensor(yv[:, :, 1:], in0=prv[:, :, :seg - 1], scalar=cw_t[:, widx, 0:1], in1=yv[:, :, 1:], op0=ALU.mult, op1=ALU.add), reads=["tmp0", "tmp1"], writes=["tmp1"])
                    S.dve(lambda e: e.scalar_tensor_tensor(yv[:, :, :seg - 1], in0=prv[:, :, 1:], scalar=cw_t[:, widx, 2:3], in1=yv[:, :, :seg - 1], op0=ALU.mult, op1=ALU.add), reads=["tmp0", "tmp1"], writes=["tmp1"])
                    return y

                if not isctx:
                    def ev_hy(ci, ps, pk):
                        y = conv(ci, ps, pk, cwh, cbh, ci, None)
                        yb = TMP[2].bitcast(BF16)
                        S.act(lambda e: e.copy(yb[:, :T], y[:, :T]), reads=["tmp1"], writes=["tmp2"])
                        pt = PS[7].bitcast(BF16)
                        for bl in range(T // 128):
                            S.pe(lambda e, bl=bl: e.transpose(pt[:, bl * 128:(bl + 1) * 128], yb[:, bl * 128:(bl + 1) * 128], identb), reads=["tmp2", "identb"], writes=["PS7"])
                        c8 = ci % 8
                        S.dve(lambda e, c8=c8: e.tensor_copy(STG[:, :T // 128, c8 * 128:(c8 + 1) * 128], pt[:, :T].rearrange("p (b c) -> p b c", c=128)), reads=["PS7"], writes=["STG"])
                        if c8 == 7:
                            g8 = ci // 8
                            S.dma(HYT[t0:t0 + T, g8 * 1024:(g8 + 1) * 1024].rearrange("(b p) c -> p b c", p=128), STG[:, :T // 128, :], reads=["STG"], writes=["HYT"])
                    linear(T, "w_in", KC, 0, HY_COLS, ev_hy)
                def ev_qkv(ci, ps, pk, base):
                    gi = base + ci
                    y = conv(ci, ps, pk, cwg, zb, gi, None)
                    a = TMP[2]
                    S.act(lambda e: e.activation(a[:, :T], y[:, :T], AF.Silu), reads=["tmp1"], writes=["tmp2"])
                    which = gi // 8; h = gi % 8
                    dst = (QT, KT, VT)[which]
                    if which < 2:
                        sq = TMP[3]
                        S.dve(lambda e: e.tensor_tensor(sq[:, :T], a[:, :T], a[:, :T], op=ALU.mult), reads=["tmp2"], writes=["tmp3"])
                        pss = PS[6]
                        S.pe(lambda e: e.matmul(pss[:, :T], lhsT=onesf, rhs=sq[:, :T], start=True, stop=True), reads=["tmp3", "onesf"], writes=["PS6"])
                        rn = TMP[4]
                        S.dve(lambda e: e.tensor_scalar(rn[:, :T], pss[:, :T], 1e-6, None, op0=ALU.add), reads=["PS6"], writes=["tmp4"])
                        S.act(lambda e: e.activation(rn[:, :T], rn[:, :T], AF.Ln), reads=["tmp4"], writes=["tmp4"])
                        S.act(lambda e: e.activation(rn[:, :T], rn[:, :T], AF.Exp, scale=-0.5), reads=["tmp4"], writes=["tmp4"])
                        scl = (128.0 ** -0.5) if which == 0 else 1.0
                        S.dve(lambda e: e.scalar_tensor_tensor(sq[:, :T], in0=a[:, :T], scalar=scl, in1=rn[:, :T], op0=ALU.mult, op1=ALU.mult), reads=["tmp2", "tmp4"], writes=["tmp3"])
                        res = sq; ak = "tmp3"
                    else:
                        res = a; ak = "tmp2"
                    S.dma(dst[h * 128:(h + 1) * 128, tcol:tcol + T], res[:, :T], reads=[ak], writes=[("QT", "KT", "VT")[which]])
                if not isctx:
                    linear(T, "w_in", KC, Q_OFF, 1024, lambda ci, ps, pk: ev_qkv(ci, ps, pk, 0))
                linear(T, "w_in", KC, KV_OFF, 2048, lambda ci, ps, pk: ev_qkv(ci, ps, pk, 8))
                if not isctx:
                    def ev_z(ci, ps, pk):
                        zt = TMP[2].bitcast(BF16)
                        S.act(lambda e, ps=ps: e.activation(zt[:, :T], ps[:, :T], AF.Silu), reads=[pk], writes=["tmp2"])
                        S.dma(ZG[ci * 128:(ci + 1) * 128, t0:t0 + T], zt[:, :T], reads=["tmp2"], writes=["ZG"])
                    linear(T, "w_in", KC, Z_OFF, 1024, ev_z)
                for bl in range(T // 128):
                    ps, pk = next_ps()
                    for k in range(KC):
                        S.pe(lambda e, ps=ps, k=k, bl=bl: e.matmul(ps[:, :32], lhsT=XM[:, k, bl * 128:(bl + 1) * 128], rhs=wab[:, k, :], start=(k == 0), stop=(k == KC - 1)), reads=["XM", "wab"], writes=[pk])
                    a = TMP[3]
                    S.dve(lambda e, ps=ps: e.tensor_copy(a[:, :32], ps[:, :32]), reads=[pk], writes=["tmp3"])
                    S.dma(ABs[tcol + bl * 128:tcol + (bl + 1) * 128, :], a[:, :32], reads=["tmp3"], writes=["ABs"])
            for tl in tiles:
                do_tile(*tl)
            st_a.__exit__(None, None, None)
            S.barrier()


        if "F" in stages:
            st_f = contextlib.ExitStack(); st_f.__enter__()
            def sbf(name, shape, dt=F32):
                return st_f.enter_context(nc.sbuf_tensor("sF_" + name, list(shape), dt)).ap()
            TWO_PI = 2.0 * math.pi
            w3s = sbf("w3s", [64, 4096]); h2T = sbf("h2T", [64, 4096]); deltab = sbf("deltab", [128, 1024]); thn = sbf("thn", [128, 32])
            mb0 = sbf("mb0", [128, 1]); negpi = sbf("negpi", [128, 1]); frs = sbf("frs", [64, 1]); s1 = sbf("s1", [64, 1]); s2a = sbf("s2a", [64, 1]); s2b = sbf("s2b", [64, 1])
            b1s = sbf("b1s", [64, 1]); b2s = sbf("b2s", [64, 1])
            S.dma(w3s, fw3, writes=["w3s"]); S.dma(deltab, deltas_d.partition_broadcast(128), writes=["deltab"]); S.dma(thn, that_d, writes=["thn"])
            S.dma(mb0, mb0_d, writes=["mb0"]); S.dma(frs, ffr, writes=["frs"]); S.dma(b1s, fb1, writes=["b1s"]); S.dma(b2s, fb2, writes=["b2s"])
            S.pool(lambda e: e.memset(negpi, -math.pi), writes=["negpi"])
            S.dve(lambda e: e.tensor_scalar(thn, thn, -1.0, None, op0=ALU.mult), reads=["thn"], writes=["thn"])
            S.dve(lambda e: e.tensor_scalar(s1, frs, 1.0 / TWO_PI, None, op0=ALU.mult), reads=["frs"], writes=["s1"])
            S.dve(lambda e: e.tensor_tensor(s2a, frs, b1s, op=ALU.mult), reads=["frs", "b1s"], writes=["s2a"])
            S.dve(lambda e: e.tensor_scalar(s2a, s2a, 1.0 / TWO_PI, 16.5, op0=ALU.mult, op1=ALU.add), reads=["s2a"], writes=["s2a"])
            S.dve(lambda e: e.tensor_tensor(s2b, frs, b2s, op=ALU.mult), reads=["frs", "b2s"], writes=["s2b"])
            S.dve(lambda e: e.tensor_scalar(s2b, s2b, 1.0 / TWO_PI, 16.5, op0=ALU.mult, op1=ALU.add), reads=["s2b"], writes=["s2b"])
            st_f1 = contextlib.ExitStack(); st_f1.__enter__()
            def sbf1(name, shape, dt=F32):
                return st_f1.enter_context(nc.sbuf_tensor("sF1_" + name, list(shape), dt)).ap()
            zTs = sbf1("zTs", [33, 4096]); w1s = sbf1("w1s", [33, 64]); w2s = sbf1("w2s", [64, 64]); h1T = sbf1("h1T", [64, 4096])
            yt = sbf1("yt", [64, 512]); kit = sbf1("kit", [64, 512], mybir.dt.int32); kft = sbf1("kft", [64, 512])
            S.dma(zTs, zT_d, writes=["zTs"]); S.dma(w1s, fw1, writes=["w1s"]); S.dma(w2s, fw2, writes=["w2s"])
            def sin_layer(srcT, skey, kdim, wS, wkey, s2, s2key, dstT, dkey):
                for tt in range(8):
                    ps, pk = next_ps()
                    S.pe(lambda e, ps=ps, tt=tt: e.matmul(ps[:64, :512], lhsT=wS[:kdim, :], rhs=srcT[:kdim, tt * 512:(tt + 1) * 512], start=True, stop=True), reads=[skey, wkey], writes=[pk])
                    S.dve(lambda e, ps=ps: e.tensor_scalar(yt, ps[:64, :512], s1[:, 0:1], s2[:, 0:1], op0=ALU.mult, op1=ALU.add), reads=[pk, "s1", s2key], writes=["yt"])
                    S.dve(lambda e: e.tensor_copy(kit, yt), reads=["yt"], writes=["kit"])
                    S.dve(lambda e: e.tensor_copy(kft, kit), reads=["kit"], writes=["kft"])
                    S.dve(lambda e: e.tensor_tensor(yt, yt, kft, op=ALU.subtract), reads=["yt", "kft"], writes=["yt"])
                    S.dve(lambda e: e.tensor_single_scalar(kft, yt, 0.0, op=ALU.is_lt), reads=["yt"], writes=["kft"])
                    S.dve(lambda e: e.tensor_tensor(yt, yt, kft, op=ALU.add), reads=["yt", "kft"], writes=["yt"])
                    S.act(lambda e, tt=tt: e.activation(dstT[:, tt * 512:(tt + 1) * 512], yt, AF.Sin, bias=negpi[:64, 0:1], scale=TWO_PI), reads=["yt", "negpi"], writes=[dkey])
            sin_layer(zTs, "zTs", 33, w1s, "w1s", s2a, "s2a", h1T, "h1T")
            sin_layer(h1T, "h1T", 64, w2s, "w2s", s2b, "s2b", h2T, "h2T")
            st_f1.__exit__(None, None, None)
            HS = sbf("HS", [128, 32, 512], BF16); HD = sbf("HD", [128, 32, 512], BF16)
            CB = [sbf(f"CBf{i}", [128, 32, 128], BF16) for i in range(2)]; SBk = [sbf(f"SBf{i}", [128, 32, 128], BF16) for i in range(2)]
            FT = [[sbf(f"ft{i}_{j}", [128, 512]) for j in range(6)] for i in range(2)]
            RN = sbf("RN", [128, 512])
            for o in range(2):
                for hh in range(2):
                    colf = o * 2048 + hh * 512; colb = o * 2048 + 1024 + hh * 512
                    for tc in range(32):
                        tw, thf, thb, ta1, ta2, thm = FT[tc % 2]; fk = [f"ft{tc % 2}_{j}" for j in range(6)]
                        S.act(lambda e, tw=tw, tc=tc, hh=hh: e.activation(tw, deltab[:, hh * 512:(hh + 1) * 512], AF.Exp, scale=thn[:, tc:tc + 1]), reads=["deltab", "thn"], writes=[fk[0]])
                        psf, pkf = next_ps(); psb, pkb = next_ps()
                        S.pe(lambda e, psf=psf, tc=tc, colf=colf: e.matmul(psf, lhsT=h2T[:, tc * 128:(tc + 1) * 128], rhs=w3s[:, colf:colf + 512], start=True, stop=True), reads=["h2T", "w3s"], writes=[pkf])
                        S.pe(lambda e, psb=psb, tc=tc, colb=colb: e.matmul(psb, lhsT=h2T[:, tc * 128:(tc + 1) * 128], rhs=w3s[:, colb:colb + 512], start=True, stop=True), reads=["h2T", "w3s"], writes=[pkb])
                        S.dve(lambda e, psf=psf, thf=thf, tw=tw: e.tensor_tensor(thf, psf, tw, op=ALU.mult), reads=[pkf, fk[0]], writes=[fk[1]])
                        S.dve(lambda e, psb=psb, thb=thb, tw=tw: e.tensor_tensor(thb, psb, tw, op=ALU.mult), reads=[pkb, fk[0]], writes=[fk[2]])
                        S.act(lambda e, ta1=ta1, thf=thf: e.activation(ta1, thf, AF.Abs), reads=[fk[1]], writes=[fk[3]])
                        S.act(lambda e, ta2=ta2, thb=thb: e.activation(ta2, thb, AF.Abs), reads=[fk[2]], writes=[fk[4]])
                        S.pool(lambda e, ta1=ta1, ta2=ta2: e.tensor_tensor(ta1, ta1, ta2, op=ALU.add), reads=[fk[3], fk[4]], writes=[fk[3]])
                        S.pe(lambda e, ta1=ta1, tc=tc: e.matmul(PS[6], lhsT=onesf, rhs=ta1, start=(tc == 0), stop=(tc == 31)), reads=[fk[3], "onesf"], writes=["PS6"])
                        if tc == 0:
                            S.dve(lambda e, thm=thm, thb=thb: e.tensor_scalar(thm, thb, mb0[:, 0:1], None, op0=ALU.mult), reads=[fk[2], "mb0"], writes=[fk[5]])
                            hbm = thm; hbk = fk[5]
                        else:
                            hbm = thb; hbk = fk[2]
                        S.pool(lambda e, thf=thf, hbm=hbm, tc=tc: e.tensor_tensor(HS[:, tc, :], thf, hbm, op=ALU.add), reads=[fk[1], hbk], writes=["HS"])
                        S.dve(lambda e, thf=thf, thb=thb, tc=tc: e.tensor_tensor(HD[:, tc, :], thf, thb, op=ALU.subtract), reads=[fk[1], fk[2]], writes=["HD"])
                    S.dve(lambda e: e.reciprocal(RN, PS[6]), reads=["PS6"], writes=["RN"])
                    S.dve(lambda e: e.tensor_scalar(RN, RN, 2.0 / 8192.0, None, op0=ALU.mult), reads=["RN"], writes=["RN"])
                    for fc in range(32):
                        cb = CB[fc % 2]; sk = SBk[fc % 2]; cbk = f"CB{fc % 2}"; skk = f"SBk{fc % 2}"
                        S.dma(cb, CFt[fc], writes=[cbk]); S.dma(sk, SFt[fc], writes=[skk])
                        psC, pkC = next_ps(); psS, pkS = next_ps()
                        for tc in range(32):
                            S.pe(lambda e, psC=psC, cb=cb, tc=tc: e.matmul(psC, lhsT=cb[:, tc, :], rhs=HS[:, tc, :], start=(tc == 0), stop=(tc == 31)), reads=[cbk, "HS"], writes=[pkC])
                        for tc in range(32):
                            S.pe(lambda e, psS=psS, sk=sk, tc=tc: e.matmul(psS, lhsT=sk[:, tc, :], rhs=HD[:, tc, :], start=(tc == 0), stop=(tc == 31)), reads=[skk, "HD"], writes=[pkS])
                        ta, tb = FT[fc % 2][0], FT[fc % 2][1]; tak, tbk = f"ft{fc % 2}_0", f"ft{fc % 2}_1"
                        S.dve(lambda e, psC=psC, ta=ta: e.tensor_tensor(ta, psC, RN, op=ALU.mult), reads=[pkC, "RN"], writes=[tak])
                        S.dve(lambda e, psS=psS, tb=tb: e.tensor_tensor(tb, psS, RN, op=ALU.mult), reads=[pkS, "RN"], writes=[tbk])
                        cc = o * 1024 + hh * 512
                        S.dma(KCs[fc * 128:(fc + 1) * 128, cc:cc + 512], ta, reads=[tak], writes=["KCs"])
                        S.dma(KSs[fc * 128:(fc + 1) * 128, cc:cc + 512], tb, reads=[tbk], writes=["KSs"])
            st_f.__exit__(None, None, None)
            S.barrier()

        if "H" in stages:
            st_h = contextlib.ExitStack(); st_h.__enter__()
            def sbh(name, shape, dt=F32):
                return st_h.enter_context(nc.sbuf_tensor("sH_" + name, list(shape), dt)).ap()
            ZIN = sbh("ZIN", [128, 32, 1024], BF16)
            CBh = [sbh(f"CB{i}", [128, 32, 128], BF16) for i in range(2)]; SBh = [sbh(f"SB{i}", [128, 32, 128], BF16) for i in range(2)]
            HT = [[sbh(f"ht{i}_{j}", [128, 512]) for j in range(8)] for i in range(2)]
            PQ = [[sbh(f"pq{i}_{j}", [128, 512], BF16) for j in range(4)] for i in range(2)]
            biasb = sbh("biasb", [128, 2, 1024]); ZST = sbh("ZST", [128, 4, 512], BF16)
            for o in range(2):
                S.dma(biasb[:, o, :], hyb_d[o:o + 1, :].partition_broadcast(128), writes=["biasb"])
            for o in range(2):
                src = HYT[:, 0:1024] if o == 0 else Z2T
                skey = "HYT" if o == 0 else "Z2T"
                gsrc = HYT[:, 1024:2048] if o == 0 else HYT[:, 2048:3072]
                for q4 in range(4):
                    S.dma(ZIN[:, q4 * 8:(q4 + 1) * 8, :], src[q4 * 1024:(q4 + 1) * 1024, :].rearrange("(tc p) c -> p tc c", p=128), reads=[skey], writes=["ZIN"])
                for fc in range(32):
                    cb = CBh[fc % 2]; sk = SBh[fc % 2]; cbk = f"hCB{fc % 2}"; skk = f"hSB{fc % 2}"
                    S.dma(cb, CFt[fc], writes=[cbk]); S.dma(sk, SFt[fc], writes=[skk])
                    for ct in range(2):
                        par = (fc * 2 + ct) % 2
                        ht = HT[par]; hk = [f"ht{par}_{j}" for j in range(8)]; pq = PQ[par]; pk_ = [f"pq{par}_{j}" for j in range(4)]
                        psA, pkA = next_ps(); psB, pkB = next_ps()
                        for tc in range(32):
                            S.pe(lambda e, psA=psA, cb=cb, tc=tc, ct=ct: e.matmul(psA, lhsT=cb[:, tc, :], rhs=ZIN[:, tc, ct * 512:(ct + 1) * 512], start=(tc == 0), stop=(tc == 31)), reads=[cbk, "ZIN"], writes=[pkA])
                        for tc in range(32):
                            S.pe(lambda e, psB=psB, sk=sk, tc=tc, ct=ct: e.matmul(psB, lhsT=sk[:, tc, :], rhs=ZIN[:, tc, ct * 512:(ct + 1) * 512], start=(tc == 0), stop=(tc == 31)), reads=[skk, "ZIN"], writes=[pkB])
                        kc, ks, A, B, t1, t2, t3, t4 = ht
                        cc = o * 1024 + ct * 512
                        S.dma(kc, KCs[fc * 128:(fc + 1) * 128, cc:cc + 512], reads=["KCs"], writes=[hk[0]])
                        S.dma(ks, KSs[fc * 128:(fc + 1) * 128, cc:cc + 512], reads=["KSs"], writes=[hk[1]])
                        S.act(lambda e, A=A, psA=psA: e.copy(A, psA), reads=[pkA], writes=[hk[2]])
                        S.act(lambda e, B=B, psB=psB: e.copy(B, psB), reads=[pkB], writes=[hk[3]])
                        S.dve(lambda e, t1=t1, A=A, kc=kc: e.tensor_tensor(t1, A, kc, op=ALU.mult), reads=[hk[2], hk[0]], writes=[hk[4]])
                        S.pool(lambda e, t2=t2, B=B, ks=ks: e.tensor_tensor(t2, B, ks, op=ALU.mult), reads=[hk[3], hk[1]], writes=[hk[5]])
                        S.dve(lambda e, t1=t1, t2=t2, p=pq[0]: e.tensor_tensor(p, t1, t2, op=ALU.subtract), reads=[hk[4], hk[5]], writes=[pk_[0]])
                        S.pool(lambda e, t3=t3, A=A, ks=ks: e.tensor_tensor(t3, A, ks, op=ALU.mult), reads=[hk[2], hk[1]], writes=[hk[6]])
                        S.dve(lambda e, t4=t4, B=B, kc=kc: e.tensor_tensor(t4, B, kc, op=ALU.mult), reads=[hk[3], hk[0]], writes=[hk[7]])
                        S.pool(lambda e, t3=t3, t4=t4, q=pq[1]: e.tensor_tensor(q, t3, t4, op=ALU.add), reads=[hk[6], hk[7]], writes=[pk_[1]])
                        S.dma(Ps[fc * 128:(fc + 1) * 128, ct * 512:(ct + 1) * 512], pq[0], reads=[pk_[0]], writes=["Ps"])
                        S.dma(Qs[fc * 128:(fc + 1) * 128, ct * 512:(ct + 1) * 512], pq[1], reads=[pk_[1]], writes=["Qs"])
                for ct in range(2):
                    PB = ZIN[:, :, 0:512]; QB = ZIN[:, :, 512:1024]
                    S.dma(PB, Ps[:, ct * 512:(ct + 1) * 512].rearrange("(fc p) c -> p fc c", p=128), reads=["Ps"], writes=["ZIN"])
                    S.dma(QB, Qs[:, ct * 512:(ct + 1) * 512].rearrange("(fc p) c -> p fc c", p=128), reads=["Qs"], writes=["ZIN"])
                    for tch in range(32):
                        ci_ = CBh[tch % 2]; si_ = SBh[tch % 2]; cbk = f"hCB{tch % 2}"; skk = f"hSB{tch % 2}"
                        S.dma(ci_, CIt[tch], writes=[cbk]); S.dma(si_, SIt[tch], writes=[skk])
                        par = tch % 2
                        ht = HT[par]; hk = [f"ht{par}_{j}" for j in range(8)]; pq = PQ[par]; pk_ = [f"pq{par}_{j}" for j in range(4)]
                        psY, pkY = next_ps()
                        for fc in range(32):
                            S.pe(lambda e, psY=psY, ci_=ci_, fc=fc: e.matmul(psY, lhsT=ci_[:, fc, :], rhs=PB[:, fc, :], start=(fc == 0), stop=False), reads=[cbk, "ZIN"], writes=[pkY])
                        for fc in range(32):
                            S.pe(lambda e, psY=psY, si_=si_, fc=fc: e.matmul(psY, lhsT=si_[:, fc, :], rhs=QB[:, fc, :], start=False, stop=(fc == 31)), reads=[skk, "ZIN"], writes=[pkY])
                        zin_t = pq[2]; gate_t = pq[3]; res = pq[0]
                        S.dma(zin_t, src[tch * 128:(tch + 1) * 128, ct * 512:(ct + 1) * 512], reads=[skey], writes=[pk_[2]])
                        S.dma(gate_t, gsrc[tch * 128:(tch + 1) * 128, ct * 512:(ct + 1) * 512], reads=["HYT"], writes=[pk_[3]])
                        t1, t2 = ht[4], ht[5]
                        S.dve(lambda e, t1=t1, zin_t=zin_t, o=o, ct=ct: e.tensor_tensor(t1, zin_t, biasb[:, o, ct * 512:(ct + 1) * 512], op=ALU.mult), reads=[pk_[2], "biasb"], writes=[hk[4]])
                        S.dve(lambda e, t2=t2, t1=t1, psY=psY: e.tensor_tensor(t2, psY, t1, op=ALU.add), reads=[pkY, hk[4]], writes=[hk[5]])
                        S.pool(lambda e, res=res, t2=t2, gate_t=gate_t: e.tensor_tensor(res, t2, gate_t, op=ALU.mult), reads=[hk[5], pk_[3]], writes=[pk_[0]])
                        if o == 0:
                            S.dma(Z2T[tch * 128:(tch + 1) * 128, ct * 512:(ct + 1) * 512], res, reads=[pk_[0]], writes=["Z2T"])
                        else:
                            pt = PS[7].bitcast(BF16)
                            for j in range(4):
                                S.pe(lambda e, res=res, j=j: e.transpose(pt[:, j * 128:(j + 1) * 128], res[:, j * 128:(j + 1) * 128], identb), reads=[pk_[0], "identb"], writes=["PS7"])
                            t4_ = tch % 4
                            S.act(lambda e, t4_=t4_: e.copy(ZST[:, :, t4_ * 128:(t4_ + 1) * 128], pt[:, 0:512].rearrange("p (j t) -> p j t", j=4)), reads=["PS7"], writes=["ZST"])
                            if t4_ == 3:
                                tg = tch // 4
                                S.dma(ZH[ct * 512:(ct + 1) * 512, tg * 512:(tg + 1) * 512].rearrange("(j p) t -> p j t", p=128), ZST, reads=["ZST"], writes=["ZH"])
            st_h.__exit__(None, None, None)
            S.barrier()

        if "G" in stages:
            st_g = contextlib.ExitStack(); st_g.__enter__()
            def sbg(name, shape, dt=F32):
                return st_g.enter_context(nc.sbuf_tensor("sG_" + name, list(shape), dt)).ap()
            NCH = LTOT // 128
            tri = sbg("tri", [128, 4, 128]); S.dma(tri, tri_d, writes=["tri"])
            TRI_LE, TRI_GE, TRI_GT, TRI_LT = tri[:, 0, :], tri[:, 1, :], tri[:, 2, :], tri[:, 3, :]
            gng = sbg("gng", [128, 1]); S.dma(gng, gng_d, writes=["gng"])
            alb = sbg("alb", [128, 16]); dtbb = sbg("dtbb", [128, 16]); negea = sbg("negea", [128, 16])
            S.dma(alb, alog_d.partition_broadcast(128), writes=["alb"]); S.dma(dtbb, dtb_d.partition_broadcast(128), writes=["dtbb"])
            S.act(lambda e: e.activation(negea, alb, AF.Exp), reads=["alb"], writes=["negea"])
            S.dve(lambda e: e.tensor_scalar(negea, negea, -1.0, None, op0=ALU.mult), reads=["negea"], writes=["negea"])
            ABt = sbg("ABt", [128, NCH, 32]); S.dma(ABt, ABs.rearrange("(c p) n -> p c n", p=128), reads=["ABs"], writes=["ABt"])
            X = sbg("X", [128, NCH, 16])
            for col in range(16):
                S.dve(lambda e, col=col: e.tensor_scalar(X[:, :, col], ABt[:, :, col], dtbb[:, col:col + 1], None, op0=ALU.add), reads=["ABt", "dtbb"], writes=["X"])
            S.act(lambda e: e.activation(X, X, AF.Exp), reads=["X"], writes=["X"])
            S.act(lambda e: e.activation(X, X, AF.Ln, bias=1.0), reads=["X"], writes=["X"])
            Gg = sbg("Gg", [128, 2, NCH, 8]); BETA = sbg("BETA", [128, 2, NCH, 8]); GC = sbg("GC", [128, 2, NCH, 8]); GLt = sbg("GLt", [128, 2, NCH, 8])
            EG = sbg("EG", [128, 2, NCH, 8]); NEG = sbg("NEG", [128, 2, NCH, 8]); EKD = sbg("EKD", [128, 2, NCH, 8]); EGL = sbg("EGL", [128, 2, NCH, 8])
            for col in range(16):
                d_, h_ = col // 8, col % 8
                S.dve(lambda e, col=col, d_=d_, h_=h_: e.tensor_scalar(Gg[:, d_, :, h_], X[:, :, col], negea[:, col:col + 1], None, op0=ALU.mult), reads=["X", "negea"], writes=["Gg"])
            for d_ in range(2):
                S.act(lambda e, d_=d_: e.activation(BETA[:, d_, :, :], ABt[:, :, 16 + 8 * d_:24 + 8 * d_], AF.Sigmoid), reads=["ABt"], writes=["BETA"])
                gflat = Gg[:, d_, :, :].rearrange("p c h -> p (c h)")
                S.pe(lambda e, d_=d_, gflat=gflat: e.matmul(PS[0][:, :NCH * 8], lhsT=(TRI_LE if d_ == 0 else TRI_GE), rhs=gflat, start=True, stop=True), reads=["Gg", "tri"], writes=["PS0"])
                S.dve(lambda e, d_=d_: e.tensor_copy(GC[:, d_, :, :].rearrange("p c h -> p (c h)"), PS[0][:, :NCH * 8]), reads=["PS0"], writes=["GC"])
                S.pe(lambda e, d_=d_, gflat=gflat: e.matmul(PS[1][:, :NCH * 8], lhsT=onesf, rhs=gflat, start=True, stop=True), reads=["Gg", "onesf"], writes=["PS1"])
                S.dve(lambda e, d_=d_: e.tensor_copy(GLt[:, d_, :, :].rearrange("p c h -> p (c h)"), PS[1][:, :NCH * 8]), reads=["PS1"], writes=["GLt"])
            S.act(lambda e: e.activation(EG, GC, AF.Exp), reads=["GC"], writes=["EG"])
            S.dve(lambda e: e.tensor_scalar(NEG, EG, -1.0, None, op0=ALU.mult), reads=["EG"], writes=["NEG"])
            S.dve(lambda e: e.tensor_tensor(EKD, GLt, GC, op=ALU.subtract), reads=["GLt", "GC"], writes=["EKD"])
            S.act(lambda e: e.activation(EKD, EKD, AF.Exp), reads=["EKD"], writes=["EKD"])
            S.act(lambda e: e.activation(EGL, GLt, AF.Exp), reads=["GLt"], writes=["EGL"])
            if dbg:
                for nm, t_ in (("Gg", Gg), ("BETA", BETA)):
                    dbg_outs[nm] = nc.dram_tensor("dbg_" + nm, [128, 2, NCH, 8], F32, kind="ExternalOutput").ap()
                    S.dma(dbg_outs[nm], t_, reads=[nm], writes=["dbg_" + nm])
            S8 = sbg("S8", [128, 8, 128])
            KB = [sbg(f"kT8_{i}", [128, 8, 128]) for i in range(2)]; VB = [sbg(f"vT8_{i}", [128, 8, 128]) for i in range(2)]; QB_ = [sbg(f"qT8_{i}", [128, 8, 128]) for i in range(2)]
            O8 = [sbg(f"O8_{i}", [128, 8, 128]) for i in range(2)]; OFt = [sbg(f"OFt_{i}", [128, 8, 128]) for i in range(2)]
            ZGt = [sbg(f"ZGt_{i}", [128, 8, 128], BF16) for i in range(2)]; OGc = [sbg(f"OGc_{i}", [128, 8, 128], BF16) for i in range(2)]
            ONn = sbg("ONn", [128, 8, 128]); SQn = sbg("SQn", [128, 8, 128]); ssn = sbg("ssn", [128, 8])
            NSET = 8
            names = ["gU", "ET", "ETs", "ETi", "Pa", "Pb", "PTa", "PTb", "TT", "kd", "vtok", "R", "vnew", "qkT", "o2s"]
            TS = [{n: sbg(f"{n}_{i}", [128, 128]) for n in names} for i in range(NSET)]
            C.pg_rr = 0
            def next_pg():
                i = C.pg_rr % 8
                C.pg_rr += 1
                return PS[i][:, 0:128], f"PS{i}"

            GSTOP = 99; GPROB = 10 ** 9
            def prob(d, c, h, pi, par):
                isctx = c >= 32
                ts = TS[pi % NSET]; tk = {n: f"{n}_{pi % NSET}" for n in names}
                gcol = Gg[:, d, c, h:h + 1]; bcol = BETA[:, d, c, h:h + 1]; negeg = NEG[:, d, c, h:h + 1]; eg = EG[:, d, c, h:h + 1]
                ekd = EKD[:, d, c, h:h + 1]; egl = EGL[:, d, c, h:h + 1]
                U, Lm, MsT, MiT = (TRI_LE, TRI_GT, TRI_LT, TRI_LE) if d == 0 else (TRI_GE, TRI_LT, TRI_GT, TRI_GE)
                kT = KB[par][:, h, :]; vT = VB[par][:, h, :]; qT = QB_[par][:, h, :]
                kk, vk, qk_ = f"kT8_{par}", f"vT8_{par}", f"qT8_{par}"
                gU, ET, ETs, ETi, TT = ts["gU"], ts["ET"], ts["ETs"], ts["ETi"], ts["TT"]
                pA = PS[h][:, 0:128]; pB = PS[h][:, 128:256]; pk = f"PS{h}"
                S.dve(lambda e: e.tensor_scalar(gU, U, gcol, None, op0=ALU.mult), reads=["tri", "Gg"], writes=[tk["gU"]])
                S.pe(lambda e: e.matmul(pA, lhsT=Lm, rhs=gU, start=True, stop=True), reads=["tri", tk["gU"]], writes=[pk])
                S.act(lambda e: e.activation(ET, pA, AF.Exp), reads=[pk], writes=[tk["ET"]])
                S.pool(lambda e: e.tensor_tensor(ETs, ET, MsT, op=ALU.mult), reads=[tk["ET"], "tri"], writes=[tk["ETs"]])
                yield
                P0 = ts["Pa"]; PT0 = ts["PTa"]
                S.pe(lambda e: e.matmul(pA, lhsT=kT, rhs=kT, start=True, stop=True), reads=[kk], writes=[pk])
                S.dve(lambda e: e.scalar_tensor_tensor(P0, in0=pA, scalar=bcol, in1=ETs, op0=ALU.mult, op1=ALU.mult), reads=[pk, "BETA", tk["ETs"]], writes=[tk["Pa"]])
                yield
                if not isctx:
                    qkT = ts["qkT"]
                    S.pool(lambda e: e.tensor_tensor(ETi, ET, MiT, op=ALU.mult), reads=[tk["ET"], "tri"], writes=[tk["ETi"]])
                    S.pe(lambda e: e.matmul(pA, lhsT=kT, rhs=qT, start=True, stop=True), reads=[kk, qk_], writes=[pk])
                    S.dve(lambda e: e.tensor_tensor(qkT, pA, ETi, op=ALU.mult), reads=[pk, tk["ETi"]], writes=[tk["qkT"]])
                    yield
                S.pe(lambda e: e.transpose(pA, P0, ident), reads=[tk["Pa"], "ident"], writes=[pk])
                S.act(lambda e: e.copy(PT0, pA), reads=[pk], writes=[tk["PTa"]])
                S.pool(lambda e: e.tensor_tensor(TT, ident, P0, op=ALU.subtract), reads=["ident", tk["Pa"]], writes=[tk["TT"]])
                yield
                Pc, PTc, Pck, PTck = P0, PT0, tk["Pa"], tk["PTa"]
                for l in range(1, 7):
                    Pn, PTn = (ts["Pb"], ts["PTb"]) if l % 2 == 1 else (ts["Pa"], ts["PTa"])
                    Pnk, PTnk = (tk["Pb"], tk["PTb"]) if l % 2 == 1 else (tk["Pa"], tk["PTa"])
                    if l < 6:
                        S.pe(lambda e, PTc=PTc, Pc=Pc: e.matmul(pA, lhsT=PTc, rhs=Pc, start=True, stop=True), reads=[PTck, Pck], writes=[pk])
                        S.act(lambda e, Pn=Pn: e.copy(Pn, pA), reads=[pk], writes=[Pnk])
                        yield
                    S.pe(lambda e, PTc=PTc, Pc=Pc: e.matmul(pA, lhsT=Pc, rhs=PTc, start=True, stop=True), reads=[PTck, Pck], writes=[pk])
                    S.dve(lambda e, PTn=PTn: e.tensor_copy(PTn, pA), reads=[pk], writes=[PTnk])
                    yield
                    S.pe(lambda e, PTn=PTn: e.matmul(pA, lhsT=PTn, rhs=TT, start=True, stop=True), reads=[PTnk, tk["TT"]], writes=[pk])
                    S.dve(lambda e: e.tensor_tensor(TT, TT, pA, op=ALU.add), reads=[tk["TT"], pk], writes=[tk["TT"]])
                    yield
                    Pc, PTc, Pck, PTck = Pn, PTn, Pnk, PTnk
                kd, vtok, R_, vnew = ts["kd"], ts["vtok"], ts["R"], ts["vnew"]
                S.pe(lambda e: e.transpose(pA, kT, ident), reads=[kk, "ident"], writes=[pk])
                S.dve(lambda e: e.tensor_scalar(kd, pA, ekd, None, op0=ALU.mult), reads=[pk, "EKD"], writes=[tk["kd"]])
                yield
                S.pe(lambda e: e.transpose(pA, vT, ident), reads=[vk, "ident"], writes=[pk])
                S.act(lambda e: e.copy(vtok, pA), reads=[pk], writes=[tk["vtok"]])
                yield
                Sh = S8[:, h, :]; sk_ = f"S8_{h}"
                S.pe(lambda e: e.matmul(pA, lhsT=kT, rhs=Sh, start=True, stop=True), reads=[kk, sk_], writes=[pk])
                S.dve(lambda e: e.scalar_tensor_tensor(R_, in0=pA, scalar=negeg, in1=vtok, op0=ALU.mult, op1=ALU.add), reads=[pk, "NEG", tk["vtok"]], writes=[tk["R"]])
                yield
                S.pe(lambda e: e.matmul(pA, lhsT=TT, rhs=R_, start=True, stop=True), reads=[tk["TT"], tk["R"]], writes=[pk])
                S.act(lambda e: e.activation(vnew, pA, AF.Identity, scale=bcol), reads=[pk, "BETA"], writes=[tk["vnew"]])
                yield
                if not isctx:
                    o2s = ts["o2s"]
                    S.pe(lambda e: e.matmul(pA, lhsT=qT, rhs=Sh, start=True, stop=True), reads=[qk_, sk_], writes=[pk])
                    S.pe(lambda e: e.matmul(pB, lhsT=qkT, rhs=vnew, start=True, stop=True), reads=[tk["qkT"], tk["vnew"]], writes=[pk])
                    S.act(lambda e: e.copy(o2s, pB), reads=[pk], writes=[tk["o2s"], pk])
                    S.dve(lambda e: e.scalar_tensor_tensor(O8[par][:, h, :], in0=pA, scalar=eg, in1=o2s, op0=ALU.mult, op1=ALU.add), reads=[pk, "EG", tk["o2s"]], writes=[f"O8_{par}", pk])
                    yield
                S.pe(lambda e: e.matmul(pA, lhsT=kd, rhs=vnew, start=True, stop=True), reads=[tk["kd"], tk["vnew"]], writes=[pk])
                S.dve(lambda e: e.scalar_tensor_tensor(Sh, in0=Sh, scalar=egl, in1=pA, op0=ALU.mult, op1=ALU.add), reads=[sk_, "EGL", pk], writes=[sk_])
                yield

            pi = 0
            lat = list(range(32))
            if dbg == "short":
                lat = [0, 1]

            for d in range(2 if GSTOP > 0 else 0):
                S.pool(lambda e: e.memset(S8, 0.0), reads=[f"S8_{h}" for h in range(8)], writes=[f"S8_{h}" for h in range(8)])
                order = ([32, 33] + lat) if d == 0 else ([33, 32] + lat[::-1])
                for n_, c in enumerate(order):
                    par = n_ % 2
                    isctx = c >= 32
                    S.dma(KB[par], KT[:, c * 128:(c + 1) * 128].rearrange("(h p) t -> p h t", p=128), reads=["KT"], writes=[f"kT8_{par}"])
                    S.dma(VB[par], VT[:, c * 128:(c + 1) * 128].rearrange("(h p) t -> p h t", p=128), reads=["VT"], writes=[f"vT8_{par}"])
                    if not isctx:
                        S.dma(QB_[par], QT[:, c * 128:(c + 1) * 128].rearrange("(h p) t -> p h t", p=128), reads=["QT"], writes=[f"qT8_{par}"])
                    gens = []
                    for h in range(8):
                        gens.append(prob(d, c, h, pi, par))
                        pi += 1
                    while gens:
                        alive = []
                        for g_ in gens:
                            try:
                                next(g_)
                                alive.append(g_)
                            except StopIteration:
                                pass
                        gens = alive
                    if isctx:
                        continue
                    if GSTOP < 4: continue
                    if d == 0:
                        S.dma(OFs[c * 128:(c + 1) * 128], O8[par], reads=[f"O8_{par}"], writes=["OFs"])
                    else:
                        def epilogue(c=c, par=par):
                            oft = OFt[par]; zg = ZGt[par]; ogc = OGc[par]; o8 = O8[par]
                            S.dma(oft, OFs[c * 128:(c + 1) * 128], reads=["OFs"], writes=[f"OFt_{par}"])
                            S.dma(zg, ZG[:, c * 128:(c + 1) * 128].rearrange("(h p) t -> p h t", p=128), reads=["ZG"], writes=[f"ZGt_{par}"])
                            S.pool(lambda e: e.tensor_tensor(o8, o8, oft, op=ALU.add), reads=[f"O8_{par}", f"OFt_{par}"], writes=[f"O8_{par}"])
                            S.dve(lambda e: e.tensor_tensor(SQn, o8, o8, op=ALU.mult), reads=[f"O8_{par}"], writes=["SQn"])
                            S.dve(lambda e: e.tensor_reduce(ssn, SQn, axis=AX.X, op=ALU.add), reads=["SQn"], writes=["ssn"])
                            S.dve(lambda e: e.tensor_scalar(ssn, ssn, 1.0 / 128.0, EPS, op0=ALU.mult, op1=ALU.add), reads=["ssn"], writes=["ssn"])
                            S.act(lambda e: e.activation(ssn, ssn, AF.Ln), reads=["ssn"], writes=["ssn"])
                            S.act(lambda e: e.activation(ssn, ssn, AF.Exp, scale=-0.5), reads=["ssn"], writes=["ssn"])
                            for h in range(8):
                                S.dve(lambda e, h=h: e.tensor_scalar(ONn[:, h, :], o8[:, h, :], ssn[:, h:h + 1], None, op0=ALU.mult), reads=[f"O8_{par}", "ssn"], writes=["ONn"])
                                pt_, kt_ = next_pg()
                                S.pe(lambda e, h=h, pt_=pt_: e.transpose(pt_, ONn[:, h, :], ident), reads=["ONn", "ident"], writes=[kt_])
                                S.dve(lambda e, h=h, pt_=pt_: e.scalar_tensor_tensor(ogc[:, h, :], in0=pt_, scalar=gng[:, 0:1], in1=zg[:, h, :], op0=ALU.mult, op1=ALU.mult), reads=[kt_, "gng", f"ZGt_{par}"], writes=[f"OGc_{par}"])
                            S.dma(OG[:, c * 128:(c + 1) * 128].rearrange("(h p) t -> p h t", p=128), ogc, reads=[f"OGc_{par}"], writes=["OG"])
                        if GSTOP >= 5: epilogue()
            st_g.__exit__(None, None, None)
            S.barrier()

        if "T" in stages:
            st_t = contextlib.ExitStack(); st_t.__enter__()
            sb = alloc_main(st_t)
            H, XM, ACT, TMP = C.H, C.XM, C.ACT, C.TMP
            ZHt = sb("ZHt", [128, 8, NT], BF16); OGt = sb("OGt", [128, 8, NT], BF16); MRG = sb("MRG", [128, KC, NT], BF16)
            GATES = ACT
            ttiles = [t * NT for t in range(L // NT)]
            if dbg == "short":
                ttiles = ttiles[:1]
            def do_tail(t0):
                T = NT
                S.dma(H, H1T[:, t0:t0 + T].rearrange("(k p) t -> p k t", p=128), reads=["H1T"], writes=["H"])
                S.dma(ZHt, ZH[:, t0:t0 + T].rearrange("(k p) t -> p k t", p=128), reads=["ZH"], writes=["ZHt"])
                S.dma(OGt, OG[:, t0:t0 + T].rearrange("(k p) t -> p k t", p=128), reads=["OG"], writes=["OGt"])
                norm_mod(T, 1, 0)
                def ev_gate(ci, ps, pk):
                    S.act(lambda e, ps=ps, ci=ci: e.activation(GATES[:, ci, :T], ps[:, :T], AF.Sigmoid), reads=[pk], writes=["ACT"])
                linear(T, "w_in", KC, GATE_OFF, 2 * D, ev_gate)
                for fb in range(4):
                    wbuf, wkey = next_wb()
                    wa = wbuf[:, 0:4096].rearrange("p (k n) -> p k n", k=8); wb_ = wbuf[:, 4096:8192].rearrange("p (k n) -> p k n", k=8)
                    S.dma(wa, Wb["hy_out"][:, fb * 512:(fb + 1) * 512].rearrange("(k p) n -> p k n", p=128), reads=["Wb_hy_out"], writes=[wkey])
                    S.dma(wb_, Wb["gdn_out"][:, fb * 512:(fb + 1) * 512].rearrange("(k p) n -> p k n", p=128), reads=["Wb_gdn_out"], writes=[wkey])
                    for jj in range(4):
                        ci = fb * 4 + jj
                        psA, pkA = next_ps(); psB, pkB = next_ps()
                        for k in range(8):
                            S.pe(lambda e, psA=psA, wa=wa, k=k, jj=jj: e.matmul(psA[:, :T], lhsT=wa[:, k, jj * 128:(jj + 1) * 128], rhs=ZHt[:, k, :T], start=(k == 0), stop=(k == 7)), reads=[wkey, "ZHt"], writes=[pkA])
                        for k in range(8):
                            S.pe(lambda e, psB=psB, wb_=wb_, k=k, jj=jj: e.matmul(psB[:, :T], lhsT=wb_[:, k, jj * 128:(jj + 1) * 128], rhs=OGt[:, k, :T], start=(k == 0), stop=(k == 7)), reads=[wkey, "OGt"], writes=[pkB])
                        t1 = TMP[0]; t2 = TMP[1]
                        S.dve(lambda e, psA=psA, ci=ci: e.tensor_tensor(t1[:, :T], GATES[:, ci, :T], psA[:, :T], op=ALU.mult), reads=["ACT", pkA], writes=["tmp0"])
                        S.dve(lambda e, psB=psB, ci=ci: e.tensor_tensor(t2[:, :T], GATES[:, KC + ci, :T], psB[:, :T], op=ALU.mult), reads=["ACT", pkB], writes=["tmp1"])
                        S.pool(lambda e, ci=ci: e.tensor_tensor(MRG[:, ci, :T], t1[:, :T], t2[:, :T], op=ALU.add), reads=["tmp0", "tmp1"], writes=["MRG"])
                def ev_o(ci, ps, pk):
                    S.dve(lambda e, ps=ps, ci=ci: e.scalar_tensor_tensor(H[:, ci, :T], in0=ps[:, :T], scalar=Gv[:, 1, ci, 0:1], in1=H[:, ci, :T], op0=ALU.mult, op1=ALU.add),
                          reads=[pk, "H", "Gv"], writes=["H"])
                linear(T, "w_o", KC, 0, D, ev_o, xsrc=MRG, xkey="MRG")
                if dbg:
                    S.dma(dbg_outs["h2"][:, t0:t0 + T].rearrange("(k p) t -> p k t", p=128), H, reads=["H"], writes=["dbg_h2"])
                ffn(T, 1, 2, 0)
                pss = PS[6]
                for k in range(KC):
                    S.act(lambda e, k=k: e.activation(ACT[:, k, :T], H[:, k, :T], AF.Square), reads=["H"], writes=["ACT"])
                for k in range(KC):
                    S.pe(lambda e, k=k: e.matmul(pss[:, :T], lhsT=onesb, rhs=ACT[:, k, :T], start=(k == 0), stop=(k == KC - 1)), reads=["ACT", "onesb"], writes=["PS6"])
                rs = TMP[5]
                S.dve(lambda e: e.tensor_scalar(rs[:, :T], pss[:, :T], 1.0 / D, EPS, op0=ALU.mult, op1=ALU.add), reads=["PS6"], writes=["tmp5"])
                S.act(lambda e: e.activation(rs[:, :T], rs[:, :T], AF.Ln), reads=["tmp5"], writes=["tmp5"])
                S.act(lambda e: e.activation(rs[:, :T], rs[:, :T], AF.Exp, scale=-0.5), reads=["tmp5"], writes=["tmp5"])
                for k in range(KC):
                    S.dve(lambda e, k=k: e.scalar_tensor_tensor(H[:, k, :T], in0=H[:, k, :T], scalar=fg[:, k:k + 1], in1=rs[:, :T], op0=ALU.mult, op1=ALU.mult), reads=["H", "tmp5", "fg"], writes=["H"])
                S.dma(out_d[:, t0:t0 + T].rearrange("(k p) t -> p k t", p=128), H, reads=["H"], writes=["outT"])
            if dbg:
                dbg_outs["h2"] = nc.dram_tensor("dbg_h2", [D, L], F32, kind="ExternalOutput").ap()
            for t0 in ttiles:
                do_tail(t0)
            st_t.__exit__(None, None, None)
            S.barrier()
        C.final_keys = []
        if dbg and "G" in stages:
            dbg_outs["OG"] = nc.dram_tensor("dbg_OG", [GW, L], BF16, kind="ExternalOutput").ap()
            S.dma(dbg_outs["OG"], OG, reads=["OG"], writes=["dbg_OG"])
            dbg_outs["OFs"] = nc.dram_tensor("dbg_OFs", [L, 8, 128], F32, kind="ExternalOutput").ap()
            S.dma(dbg_outs["OFs"], OFs, reads=["OFs"], writes=["dbg_OFs"])
        if dbg and "H" in stages:
            for nm in ("Z2T", "ZH"):
                a = C.scr[nm]
                dbg_outs[nm] = nc.dram_tensor("dbg_" + nm, list(a.shape), BF16, kind="ExternalOutput").ap()
                S.dma(dbg_outs[nm], a, reads=[nm], writes=["dbg_" + nm])
        if dbg and "F" in stages:
            for nm in ("KCs", "KSs"):
                a = C.scr[nm]
                dbg_outs[nm] = nc.dram_tensor("dbg_" + nm, list(a.shape), F32, kind="ExternalOutput").ap()
                S.dma(dbg_outs[nm], a, reads=[nm], writes=["dbg_" + nm])
        if dbg and "A" in stages:
            for nm in ("H1T", "QT", "KT", "VT", "ABs"):
                a = C.scr[nm]
                dbg_outs[nm] = nc.dram_tensor("dbg_" + nm, list(a.shape), F32, kind="ExternalOutput").ap()
                S.dma(dbg_outs[nm], a, reads=[nm], writes=["dbg_" + nm])
            for nm in ("HYT", "ZG"):
                a = C.scr[nm]
                if nm in ext: continue
                dbg_outs[nm] = nc.dram_tensor("dbg_" + nm, list(a.shape), BF16, kind="ExternalOutput").ap()
                S.dma(dbg_outs[nm], a, reads=[nm], writes=["dbg_" + nm])
        S.emit(final_keys=[k for k in S.last_w if isinstance(k, str) and (k.startswith("dbg_") or k == "outT")])
    return nc, C


def _prep_core(inputs, b):
    f = np.float32
    g = lambda k: np.asarray(inputs[k])
    m = {}
    m["xT"] = np.ascontiguousarray(g("x")[b].T, dtype=f)
    m["ctxT"] = np.ascontiguousarray(g("ctx")[b].T, dtype=f)
    s = np.stack([g("c")[b], g("c_ctx")], 0)
    m["sT"] = np.ascontiguousarray(s.reshape(2, KC, 128).transpose(2, 1, 0), dtype=f)
    m["w_mod"] = np.ascontiguousarray(g("w_mod")[0], dtype=f)
    m["b_mod"] = np.ascontiguousarray(g("b_mod")[0].reshape(9 * KC, 128).T, dtype=f)
    m["norm_g"] = np.ascontiguousarray(g("norm_g")[0].reshape(3 * KC, 128).T, dtype=f)
    m["fin_g"] = np.ascontiguousarray(g("final_norm_g").reshape(KC, 128).T, dtype=f)
    for k in ("ffn1_wgu", "ffn1_wd", "ffn2_wgu", "ffn2_wd", "w_in", "hy_out", "gdn_out", "w_o"):
        m[k] = np.ascontiguousarray(g(k)[0], dtype=f)
    m["hy_cw"] = np.ascontiguousarray(g("hy_conv_w")[0].reshape(3, 24, 128).transpose(2, 1, 0), dtype=f)
    m["hy_cb"] = np.ascontiguousarray(g("hy_conv_b")[0].reshape(24, 128).T, dtype=f)
    m["g_cw"] = np.ascontiguousarray(g("gdn_conv_w")[0].reshape(3, 24, 128).transpose(2, 1, 0), dtype=f)
    return m


_CONST = {}
def _consts():
    if _CONST:
        return _CONST
    import ml_dtypes
    f = np.float32
    Lq = 4096; N = 8192
    pos = np.arange(Lq, dtype=f)
    t = (pos / f(Lq))[:, None].astype(f)
    fb = np.linspace(1e-4, 15, 16, dtype=f)
    ang = (f(2 * math.pi) * t * fb).astype(f)
    z = np.concatenate([t, np.cos(ang), -np.sin(ang)], -1).astype(f)
    _CONST["zT"] = np.ascontiguousarray(z.T)
    _CONST["deltas"] = np.abs(np.linspace(math.log(1e-2) / 1.5, math.log(1e-2) / 0.3, 1024, dtype=f)).reshape(1, 1024).astype(f)
    _CONST["that"] = np.ascontiguousarray((pos / f(Lq)).reshape(32, 128).T)
    mb0 = np.ones((128, 1), f); mb0[0, 0] = 0.0
    _CONST["mb0"] = mb0
    tt = np.arange(Lq, dtype=np.int64)[:, None]; ff = np.arange(Lq, dtype=np.int64)[None, :]
    m = ((2 * ff + 1) * tt) % (2 * N)
    th = (2.0 * np.pi / (2 * N)) * m.astype(np.float64)
    Cm = np.cos(th).astype(ml_dtypes.bfloat16); Sm = np.sin(th).astype(ml_dtypes.bfloat16)
    del th, m
    def fwd_tile(M):
        return np.ascontiguousarray(M.reshape(32, 128, 32, 128).transpose(2, 1, 0, 3))
    def inv_tile(M):
        return np.ascontiguousarray(M.reshape(32, 128, 32, 128).transpose(0, 3, 2, 1))
    _CONST["CFt"] = fwd_tile(Cm); _CONST["SFt"] = fwd_tile(Sm); _CONST["CIt"] = inv_tile(Cm); _CONST["SIt"] = inv_tile(Sm)
    return _CONST


def _prep_core2(inputs, b, m):
    f = np.float32
    g = lambda k: np.asarray(inputs[k])
    m.update(_consts())
    m["f_w1"] = np.ascontiguousarray(g("hy_f_w1")[0], dtype=f); m["f_w2"] = np.ascontiguousarray(g("hy_f_w2")[0], dtype=f)
    m["f_w3"] = np.ascontiguousarray(g("hy_f_w3")[0], dtype=f)
    m["f_b1"] = np.ascontiguousarray(g("hy_f_b1")[0].reshape(64, 1), dtype=f); m["f_b2"] = np.ascontiguousarray(g("hy_f_b2")[0].reshape(64, 1), dtype=f)
    m["f_freq"] = np.ascontiguousarray(g("hy_sin_freq")[0].reshape(64, 1), dtype=f)
    m["hy_bias"] = np.ascontiguousarray(g("hy_bias")[0], dtype=f)
    a = np.arange(128)
    tri = np.stack([a[:, None] <= a[None, :], a[:, None] >= a[None, :], a[:, None] > a[None, :], a[:, None] < a[None, :]], 1).astype(f)
    m["tri"] = np.ascontiguousarray(tri)
    m["a_log"] = np.ascontiguousarray(g("gdn_a_log")[0].reshape(1, 16), dtype=f)
    m["dt_bias"] = np.ascontiguousarray(g("gdn_dt_bias")[0].reshape(1, 16), dtype=f)
    m["gdn_ng"] = np.ascontiguousarray(g("gdn_norm_g")[0].reshape(128, 1), dtype=f)
    return m


_PROG = {}

def kernel(**inputs):
    if "nc" not in _PROG:
        nc = bass.Bass("TRN2", target_bir_lowering=False)
        nc, C = build_program(nc, dbg=False)
        _PROG["nc"] = nc; _PROG["din"] = set(C.din)
    nc = _PROG["nc"]
    B = np.asarray(inputs["x"]).shape[0]
    in_maps = []
    percore = {}
    for r in range(8):
        b = r // 2
        if b not in percore:
            m = _prep_core2(inputs, b, _prep_core(inputs, b))
            percore[b] = {k: v for k, v in m.items() if k in _PROG["din"]}
        in_maps.append(percore[b])
    res = run_bass_kernel_spmd(nc, in_maps, core_ids=list(range(8)))
    out = np.empty((B, L, D), np.float32)
    for b in range(B):
        out[b] = res.results[2 * b]["outT"].T
    return out
```

```python
import math
import contextlib
import numpy as np
import concourse.bass as bass
import concourse.mybir as mybir
from concourse.bass_utils import run_bass_kernel_spmd

F32 = mybir.dt.float32
BF16 = mybir.dt.bfloat16
AF = mybir.ActivationFunctionType
ALU = mybir.AluOpType
AX = mybir.AxisListType

EPOCH = 4096
ENGS = ("tensor", "vector", "scalar", "gpsimd", "sync")
N_DMA_SEMS = 12


class Op:
    __slots__ = ("eng", "fn", "deps", "idx", "needs_inc", "dma", "dma_tok", "dma_prev", "inc_no")

    def __init__(self, eng, fn):
        self.eng = eng
        self.fn = fn
        self.deps = []
        self.idx = None
        self.needs_inc = False
        self.dma = False
        self.dma_tok = None
        self.dma_prev = None


class Sched:
    def __init__(self, nc):
        self.nc = nc
        self.ops = {e: [] for e in ENGS}
        self.last_w = {}
        self.readers = {}
        self.dma_rr = {e: 0 for e in ENGS}
        self.dma_cnt = {}
        self.dma_last = {}
        self.n_ops = 0
        self.bar_ops = []
        self.bar_gen = 0
        self.bar_applied = {e: 0 for e in ENGS}

    def barrier(self):
        self.bar_gen += 1
        b = []
        for e in ENGS:
            for op in reversed(self.ops[e]):
                if not op.dma:
                    b.append(op)
                    break
        b.extend(self.dma_last.values())
        self.bar_ops = b

    def _add(self, eng, fn, reads, writes, dma=False, pe_acc=False):
        op = Op(eng, fn)
        op.dma = dma
        deps = []
        for k in reads:
            w = self.last_w.get(k)
            if w is not None:
                deps.append(w)
        for k in writes:
            w = self.last_w.get(k)
            if w is not None:
                deps.append(w)
            for r in self.readers.get(k, ()):
                deps.append(r)
        if self.bar_applied[eng] != self.bar_gen:
            deps.extend(self.bar_ops)
            self.bar_applied[eng] = self.bar_gen
        seen = set()
        for d in deps:
            if d is op or id(d) in seen:
                continue
            seen.add(id(d))
            if (not d.dma) and (not dma) and d.eng == "tensor" and eng == "tensor":
                continue
            op.deps.append(d)
        for k in reads:
            self.readers.setdefault(k, []).append(op)
        for k in writes:
            self.last_w[k] = op
            self.readers[k] = []
        op.idx = len(self.ops[eng])
        self.ops[eng].append(op)
        if dma:
            slot = self.dma_rr[eng] % N_DMA_SEMS
            self.dma_rr[eng] += 1
            key = (eng, slot)
            cnt = self.dma_cnt.get(key, 0) + 1
            self.dma_cnt[key] = cnt
            op.dma_tok = (key, cnt * 16)
            op.dma_prev = self.dma_last.get(key)
            self.dma_last[key] = op
        self.n_ops += 1
        return op

    def pe(self, fn, reads=(), writes=()):
        return self._add("tensor", fn, reads, writes)

    def dve(self, fn, reads=(), writes=()):
        return self._add("vector", fn, reads, writes)

    def act(self, fn, reads=(), writes=()):
        return self._add("scalar", fn, reads, writes)

    def pool(self, fn, reads=(), writes=()):
        return self._add("gpsimd", fn, reads, writes)

    def dma(self, out, in_, reads=(), writes=(), eng="sync"):
        return self._add(eng, lambda e: e.dma_start(out=out, in_=in_), reads, writes, dma=True)

    def dma_fn(self, fn, reads=(), writes=(), eng="gpsimd"):
        return self._add(eng, fn, reads, writes, dma=True)

    def dma_cast(self, out, in_, reads=(), writes=()):
        return self.dma(out, in_, reads, writes, eng="gpsimd")

    def emit(self, final_keys=()):
        nc = self.nc
        fin = Op("sync", None)
        for k in final_keys:
            w = self.last_w.get(k)
            if w is not None:
                fin.deps.append(w)
        for e in ENGS:
            for op in self.ops[e]:
                for d in op.deps:
                    if not d.dma:
                        d.needs_inc = True
        for d in fin.deps:
            if not d.dma:
                d.needs_inc = True
        n_inc = {}
        for e in ENGS:
            c = 0
            for op in self.ops[e]:
                if (not op.dma) and op.needs_inc:
                    op.inc_no = c
                    c += 1
            n_inc[e] = c
        import contextlib
        stack = contextlib.ExitStack()
        sems = {}
        with stack:
            for e in ENGS:
                n = n_inc[e]
                for ep in range((n + EPOCH - 1) // EPOCH + 1):
                    sems[(e, ep)] = stack.enter_context(nc.semaphore(f"p_{e}_{ep}"))
            for key in self.dma_cnt:
                sems[("dma",) + key] = stack.enter_context(nc.semaphore(f"d_{key[0]}_{key[1]}"))
            block = stack.enter_context(nc.Block())

            def tok(d):
                if d.dma:
                    return (sems[("dma",) + d.dma_tok[0]], d.dma_tok[1], ("dma",) + d.dma_tok[0])
                ep, r = divmod(d.inc_no, EPOCH)
                return (sems[(d.eng, ep)], r + 1, (d.eng, ep))

            def run(engname):
                def body(eng):
                    waited = {}
                    def do_wait(d):
                        s, v, key = tok(d)
                        if waited.get(key, 0) >= v:
                            return
                        waited[key] = v
                        eng.wait_ge(s, v)
                    for op in self.ops[engname]:
                        for d in op.deps:
                            if (not d.dma) and d.eng == engname and d.idx >= op.idx:
                                continue
                            do_wait(d)
                        if op.dma and op.dma_prev is not None:
                            do_wait(op.dma_prev)
                        ins = op.fn(eng)
                        if op.dma:
                            s, v, _ = tok(op)
                            ins.then_inc(s, 16)
                        elif op.needs_inc:
                            ep, _r = divmod(op.inc_no, EPOCH)
                            ins.then_inc(sems[(engname, ep)], 1)
                    if engname == "sync":
                        for d in fin.deps:
                            do_wait(d)
                return body

            block.tensor(run("tensor"))
            block.vector(run("vector"))
            block.scalar(run("scalar"))
            block.gpsimd(run("gpsimd"))
            block.sync(run("sync"))

D = 2048; KC = 16; DFF = 5632; HC = 44; L = 4096; LC = 256; NT = 512
HYW = 1024; GW = 1024; NH = 8
HY_COLS = 3072; Q_OFF = 3072; KV_OFF = 4096; Z_OFF = 6144; AB_OFF = 7168; GATE_OFF = 7200; IN_COLS = 11296
EPS = 1e-6
LTOT = L + LC
OWN = L // 2


class Ctx:
    pass


def build_program(nc, dbg=False, stages=("A", "F", "H", "G", "T"), ext=()):
    S = Sched(nc)
    C = Ctx()
    C.nc = nc; C.S = S
    din = {}
    def inp(name, shape, dt=F32):
        din[name] = nc.dram_tensor(name, list(shape), dt, kind="ExternalInput").ap()
        return din[name]
    def scratch(name, shape, dt=F32):
        if name in ext:
            return inp(name, shape, dt)
        return nc.dram_tensor(name, list(shape), dt, kind="Internal").ap()
    xT = inp("xT", [D, L]); cT = inp("ctxT", [D, LC]); sT = inp("sT", [128, KC, 2])
    w_mod = inp("w_mod", [D, 9 * D]); b_mod = inp("b_mod", [128, 9 * KC])
    norm_g = inp("norm_g", [128, 3 * KC]); fin_g = inp("fin_g", [128, KC])
    wgu = [inp("ffn1_wgu", [D, 2 * DFF]), inp("ffn2_wgu", [D, 2 * DFF])]
    wd = [inp("ffn1_wd", [DFF, D]), inp("ffn2_wd", [DFF, D])]
    w_in = inp("w_in", [D, IN_COLS])
    hy_cw = inp("hy_cw", [128, 24, 3]); hy_cb = inp("hy_cb", [128, 24]); g_cw = inp("g_cw", [128, 24, 3])
    hy_out = inp("hy_out", [HYW, D]); gdn_out = inp("gdn_out", [GW, D]); w_o = inp("w_o", [D, D])
    zT_d = inp("zT", [33, L]); fw1 = inp("f_w1", [33, 64]); fw2 = inp("f_w2", [64, 64]); fw3 = inp("f_w3", [64, 4096])
    fb1 = inp("f_b1", [64, 1]); fb2 = inp("f_b2", [64, 1]); ffr = inp("f_freq", [64, 1])
    deltas_d = inp("deltas", [1, 1024]); that_d = inp("that", [128, 32]); mb0_d = inp("mb0", [128, 2])
    CFt = inp("CFt", [32, 128, 32, 128], BF16); SFt = inp("SFt", [32, 128, 32, 128], BF16)
    CIt = inp("CIt", [32, 128, 32, 128], BF16); SIt = inp("SIt", [32, 128, 32, 128], BF16)
    hyb_d = inp("hy_bias", [2, 1024])
    tri_d = inp("tri", [128, 4, 128]); alog_d = inp("a_log", [1, 16]); dtb_d = inp("dt_bias", [1, 16]); gng_d = inp("gdn_ng", [128, 1])
    out_d = nc.dram_tensor("outT", [D, OWN], F32, kind="ExternalOutput").ap()
    C.din = din
    H1T = scratch("H1T", [D, L]); HYT = scratch("HYT", [L, HY_COLS], BF16)
    QT = scratch("QT", [GW, LTOT]); KT = scratch("KT", [GW, LTOT]); VT = scratch("VT", [GW, LTOT])
    ZG = scratch("ZG", [GW, L], BF16); ABs = scratch("ABs", [LTOT, 32])
    ZH = scratch("ZH", [HYW, L], BF16); OG = scratch("OG", [GW, L], BF16)
    OFs = scratch("OFs", [L, 8, 128])
    KCs = scratch("KCs", [L, 2048]); KSs = scratch("KSs", [L, 2048])
    Z2T = scratch("Z2T", [L, 1024], BF16); Ps = scratch("Ps", [L, 1024], BF16); Qs = scratch("Qs", [L, 1024], BF16)
    C.scr = dict(KCs=KCs, KSs=KSs, Z2T=Z2T, H1T=H1T, HYT=HYT, QT=QT, KT=KT, VT=VT, ZG=ZG, ABs=ABs, ZH=ZH, OG=OG)
    dbg_outs = {}
    C.dbg_outs = dbg_outs
    Wf = dict(ffn1_wgu=wgu[0], ffn1_wd=wd[0], w_in=w_in, ffn2_wgu=wgu[1], ffn2_wd=wd[1], hy_out=hy_out, gdn_out=gdn_out, w_o=w_o)
    Wb = {k: nc.dram_tensor(k + "_b16", list(v.shape), BF16, kind="Internal").ap() for k, v in Wf.items()}
    def convert_w(name):
        src = Wf[name]; dst = Wb[name]
        R_ = src.shape[0]
        step = 512 if R_ % 512 == 0 else 128
        for r0 in range(0, R_, step):
            S.dma_cast(dst[r0:r0 + step, :].rearrange("(p a) n -> p (a n)", p=128), src[r0:r0 + step, :].rearrange("(p a) n -> p (a n)", p=128), writes=["Wb_" + name])
    wgu_b = [Wb["ffn1_wgu"], Wb["ffn2_wgu"]]; wd_b = [Wb["ffn1_wd"], Wb["ffn2_wd"]]

    stack = contextlib.ExitStack()
    with stack:
        def sb(name, shape, dt=F32):
            return stack.enter_context(nc.sbuf_tensor(name, list(shape), dt)).ap()
        PS = [nc.alloc_psum_tensor(f"psb{i}", [128, 512], F32).ap() for i in range(8)]
        C.PS = PS
        C.ps_rr = 0
        def next_ps():
            i = C.ps_rr % 6
            C.ps_rr += 1
            return PS[i], f"PS{i}"
        C.next_ps = next_ps
        ident = sb("ident", [128, 128]); identb = sb("identb", [128, 128], BF16)
        onesb = sb("onesb", [128, 128], BF16); onesf = sb("onesf", [128, 128])
        S.pool(lambda e: e.memset(ident, 1.0), writes=["ident"])
        S.pool(lambda e: e.affine_select(ident, ident, pattern=[[-1, 128]], compare_op=ALU.is_equal, fill=0.0, base=0, channel_multiplier=1), reads=["ident"], writes=["ident"])
        S.dve(lambda e: e.tensor_copy(identb, ident), reads=["ident"], writes=["identb"])
        S.pool(lambda e: e.memset(onesf, 1.0), writes=["onesf"])
        S.dve(lambda e: e.memset(onesb, 1.0), writes=["onesb"])
        C.ident = ident; C.identb = identb; C.onesb = onesb; C.onesf = onesf
        modv = sb("modv", [128, 9 * KC, 2]); sTs = sb("sTs", [128, KC, 2]); sTb = sb("sTb", [128, KC, 2], BF16)
        bmod = sb("bmod", [128, 9 * KC]); ng = sb("ng", [128, 3 * KC]); fg = sb("fg", [128, KC])
        Av = sb("Av", [128, 3, KC, 2]); Bv = sb("Bv", [128, 3, KC, 2]); Gv = sb("Gv", [128, 3, KC, 2])
        S.dma(sTs, sT, writes=["sTs"]); S.dma(bmod, b_mod, writes=["bmod"]); S.dma(ng, norm_g, writes=["ng"]); S.dma(fg, fin_g, writes=["fg"])
        S.act(lambda e: e.activation(sTb, sTs, AF.Silu), reads=["sTs"], writes=["sTb"])
        C.wb_rr = 0
        def next_wb():
            i = C.wb_rr % 3
            C.wb_rr += 1
            return C.WB[i], f"WB{i}"
        C.next_wb = next_wb
        def alloc_main(st):
            def sbl(name, shape, dt=F32):
                return st.enter_context(nc.sbuf_tensor(name, list(shape), dt)).ap()
            C.gen = getattr(C, "gen", 0) + 1
            g = C.gen
            C.WB = [sbl(f"wb{i}_{g}", [128, 8192], BF16) for i in range(3)]
            C.H = sbl(f"H_{g}", [128, KC, NT]); C.XM = sbl(f"XM_{g}", [128, KC, NT], BF16); C.ACT = sbl(f"ACT_{g}", [128, HC, NT], BF16)
            C.TMP = [sbl(f"tmp{i}_{g}", [128, NT]) for i in range(6)]
            return sbl
        st_mod = contextlib.ExitStack()
        st_mod.__enter__()
        C.WB = [st_mod.enter_context(nc.sbuf_tensor(f"wbm{i}", [128, 8192], BF16)).ap() for i in range(3)]
        for nm_ in ("ffn1_wgu", "ffn1_wd", "w_in"):
            convert_w(nm_)
        for cb in range(9 * D // 512):
            wbuf, wkey = next_wb()
            wv = wbuf.rearrange("p (k n) -> p k n", k=KC)
            S.dma_cast(wv, w_mod[:, cb * 512:(cb + 1) * 512].rearrange("(k p) n -> p k n", p=128), writes=[wkey])
            for j in range(4):
                ps, pk = next_ps()
                for k in range(KC):
                    S.pe(lambda e, ps=ps, wv=wv, k=k, j=j: e.matmul(ps[:, 0:2], lhsT=wv[:, k, j * 128:(j + 1) * 128], rhs=sTb[:, k, :], start=(k == 0), stop=(k == KC - 1)),
                         reads=[wkey, "sTb"], writes=[pk])
                col = cb * 4 + j
                S.dve(lambda e, ps=ps, col=col: e.tensor_scalar(modv[:, col, :], ps[:, 0:2], bmod[:, col:col + 1], None, op0=ALU.add),
                      reads=[pk, "bmod"], writes=["modv"])
        for j in range(3):
            for r in range(2):
                sh = modv[:, (3 * j) * KC:(3 * j + 1) * KC, r]; sc = modv[:, (3 * j + 1) * KC:(3 * j + 2) * KC, r]; gt = modv[:, (3 * j + 2) * KC:(3 * j + 3) * KC, r]
                S.dve(lambda e, sc=sc, j=j, r=r: e.scalar_tensor_tensor(Av[:, j, :, r], in0=sc, scalar=1.0, in1=ng[:, j * KC:(j + 1) * KC], op0=ALU.add, op1=ALU.mult),
                      reads=["modv", "ng"], writes=["Av"])
                S.dve(lambda e, sh=sh, j=j, r=r: e.tensor_copy(Bv[:, j, :, r], sh), reads=["modv"], writes=["Bv"])
                S.dve(lambda e, gt=gt, j=j, r=r: e.tensor_scalar(Gv[:, j, :, r], gt, (1.0 if j == 1 else 0.5), None, op0=ALU.mult), reads=["modv"], writes=["Gv"])
        C.Av = Av; C.Bv = Bv; C.Gv = Gv; C.fg = fg
        if dbg:
            dbg_outs["modv"] = nc.dram_tensor("dbg_modv", [128, 9 * KC, 2], F32, kind="ExternalOutput").ap()
            S.dma(dbg_outs["modv"], modv, reads=["modv"], writes=["dbg_modv"])
        for nm_ in ("ffn2_wgu", "ffn2_wd", "hy_out", "gdn_out", "w_o"):
            convert_w(nm_)
        st_mod.__exit__(None, None, None)
        S.barrier()

        def norm_mod(T, j, r, src_key="H"):
            H, XM, ACT, TMP = C.H, C.XM, C.ACT, C.TMP
            pss, psk = PS[6], "PS6"
            sq = ACT
            for k in range(KC):
                S.act(lambda e, k=k: e.activation(sq[:, k, :T], H[:, k, :T], AF.Square), reads=["H"], writes=["ACT"])
            for k in range(KC):
                S.pe(lambda e, k=k: e.matmul(pss[:, :T], lhsT=onesb, rhs=sq[:, k, :T], start=(k == 0), stop=(k == KC - 1)), reads=["ACT", "onesb"], writes=[psk])
            rs = TMP[5]
            S.dve(lambda e: e.tensor_scalar(rs[:, :T], pss[:, :T], 1.0 / D, EPS, op0=ALU.mult, op1=ALU.add), reads=[psk], writes=["tmp5"])
            S.act(lambda e: e.activation(rs[:, :T], rs[:, :T], AF.Ln), reads=["tmp5"], writes=["tmp5"])
            S.act(lambda e: e.activation(rs[:, :T], rs[:, :T], AF.Exp, scale=-0.5), reads=["tmp5"], writes=["tmp5"])
            for k in range(KC):
                t = TMP[k % 2]; tk = f"tmp{k % 2}"
                S.dve(lambda e, k=k, t=t: e.tensor_tensor(t[:, :T], H[:, k, :T], rs[:, :T], op=ALU.mult), reads=["H", "tmp5"], writes=[tk])
                S.act(lambda e, k=k, t=t: e.activation(XM[:, k, :T], t[:, :T], AF.Identity, bias=Bv[:, j, k, r:r + 1], scale=Av[:, j, k, r:r + 1]),
                      reads=[tk, "Av", "Bv"], writes=["XM"])
        C.norm_mod = norm_mod

        def linear(T, wname, kc, col0, ncols, evac, cw=512, xsrc=None, xkey="XM"):
            w_ap = Wb[wname]; wsrc_key = "Wb_" + wname
            xs = C.XM if xsrc is None else xsrc
            cw = min(cw, 8192 // kc)
            nblk = (ncols + cw - 1) // cw
            ci = 0
            pend = []
            for b in range(nblk):
                c0 = col0 + b * cw
                w = min(cw, col0 + ncols - c0)
                wbuf, wkey = next_wb()
                wv = wbuf[:, :kc * w].rearrange("p (k n) -> p k n", k=kc)
                S.dma(wv, w_ap[:, c0:c0 + w].rearrange("(k p) n -> p k n", p=128), reads=[wsrc_key], writes=[wkey])
                for jj in range((w + 127) // 128):
                    m = min(128, w - jj * 128)
                    ps, pk = next_ps()
                    for k in range(kc):
                        S.pe(lambda e, ps=ps, wv=wv, k=k, jj=jj, m=m: e.matmul(ps[:m, :T], lhsT=wv[:, k, jj * 128:jj * 128 + m], rhs=xs[:, k, :T], start=(k == 0), stop=(k == kc - 1)),
                             reads=[wkey, xkey], writes=[pk])
                    if pend:
                        evac(*pend.pop())
                    pend.append((ci, ps, pk))
                    ci += 1
            if pend:
                evac(*pend.pop())
        C.linear = linear

        def ffn(T, fi, j, r):
            H, XM, ACT, TMP = C.H, C.XM, C.ACT, C.TMP
            norm_mod(T, j, r)
            for hb in range(HC // 4):
                wg_b, wgk = next_wb(); wu_b, wuk = next_wb()
                wgv = wg_b.rearrange("p (k n) -> p k n", k=KC); wuv = wu_b.rearrange("p (k n) -> p k n", k=KC)
                gk_ = "Wb_ffn%d_wgu" % (fi + 1)
                S.dma(wgv, wgu_b[fi][:, hb * 512:(hb + 1) * 512].rearrange("(k p) n -> p k n", p=128), reads=[gk_], writes=[wgk])
                S.dma(wuv, wgu_b[fi][:, DFF + hb * 512:DFF + (hb + 1) * 512].rearrange("(k p) n -> p k n", p=128), reads=[gk_], writes=[wuk])
                for jj in range(4):
                    hc = hb * 4 + jj
                    psg, pgk = next_ps(); psu, puk = next_ps()
                    for k in range(KC):
                        S.pe(lambda e, psg=psg, wgv=wgv, k=k, jj=jj: e.matmul(psg[:, :T], lhsT=wgv[:, k, jj * 128:(jj + 1) * 128], rhs=XM[:, k, :T], start=(k == 0), stop=(k == KC - 1)), reads=[wgk, "XM"], writes=[pgk])
                    for k in range(KC):
                        S.pe(lambda e, psu=psu, wuv=wuv, k=k, jj=jj: e.matmul(psu[:, :T], lhsT=wuv[:, k, jj * 128:(jj + 1) * 128], rhs=XM[:, k, :T], start=(k == 0), stop=(k == KC - 1)), reads=[wuk, "XM"], writes=[puk])
                    t = TMP[2 + (hc % 2)]; tk = f"tmp{2 + (hc % 2)}"
                    S.act(lambda e, psg=psg, t=t: e.activation(t[:, :T], psg[:, :T], AF.Silu), reads=[pgk], writes=[tk])
                    S.dve(lambda e, psu=psu, t=t, hc=hc: e.tensor_tensor(ACT[:, hc, :T], t[:, :T], psu[:, :T], op=ALU.mult), reads=[tk, puk], writes=["ACT"])
            def ev(ci, ps, pk):
                S.dve(lambda e, ps=ps, ci=ci: e.scalar_tensor_tensor(H[:, ci, :T], in0=ps[:, :T], scalar=Gv[:, j, ci, r:r + 1], in1=H[:, ci, :T], op0=ALU.mult, op1=ALU.add),
                      reads=[pk, "H", "Gv"], writes=["H"])
            linear(T, "ffn%d_wd" % (fi + 1), HC, 0, D, ev, cw=128, xsrc=ACT, xkey="ACT")
        C.ffn = ffn

        if "A" in stages:
            st_a = contextlib.ExitStack(); st_a.__enter__()
            sb = alloc_main(st_a)
            H, XM, ACT, TMP = C.H, C.XM, C.ACT, C.TMP
            cwh = sb("cwh", [128, 24, 3]); cbh = sb("cbh", [128, 24]); cwg = sb("cwg", [128, 24, 3]); zb = sb("zb", [128, 24])
            S.dma(cwh, hy_cw, writes=["cwh"]); S.dma(cbh, hy_cb, writes=["cbh"]); S.dma(cwg, g_cw, writes=["cwg"])
            S.pool(lambda e: e.memset(zb, 0.0), writes=["zb"])
            wab = sb("wab", [128, KC, 32], BF16)
            S.dma(wab, Wb["w_in"][:, AB_OFF:AB_OFF + 32].rearrange("(k p) n -> p k n", p=128), reads=["Wb_w_in"], writes=["wab"])
            STG = sb("STG", [128, 4, 1024], BF16)
            tiles = [(0, t * NT, NT, 0) for t in range(L // NT)] + [(1, 0, LC, 1)]
            if dbg == "short":
                tiles = [tiles[0], tiles[-1]]
            def do_tile(isctx, t0, T, r):
                src = cT if isctx else xT
                S.dma(H[:, :, :T], src[:, t0:t0 + T].rearrange("(k p) t -> p k t", p=128), writes=["H"])
                ffn(T, 0, 0, r)
                own = (not isctx) and (t0 < OWN)
                if own:
                    S.dma(H1T[:, t0:t0 + T].rearrange("(k p) t -> p k t", p=128), H[:, :, :T], reads=["H"], writes=["H1T"])
                norm_mod(T, 1, r)
                seg = 64 if not isctx else LC
                nseg = T // seg
                tcol = (L + t0) if isctx else t0

                def conv(ci, ps, pk, cw_t, cb_t, widx, outk):
                    pr = TMP[0]; y = TMP[1]
                    S.act(lambda e, ps=ps: e.copy(pr[:, :T], ps[:, :T]), reads=[pk], writes=["tmp0"])
                    S.dve(lambda e: e.tensor_scalar(y[:, :T], pr[:, :T], cw_t[:, widx, 1:2], cb_t[:, widx:widx + 1], op0=ALU.mult, op1=ALU.add), reads=["tmp0"], writes=["tmp1"])
                    prv = pr[:, :T].rearrange("p (s n) -> p s n", n=seg); yv = y[:, :T].rearrange("p (s n) -> p s n", n=seg)
                    S.dve(lambda e: e.scalar_tensor_tensor(yv[:, :, 1:], in0=prv[:, :, :seg - 1], scalar=cw_t[:, widx, 0:1], in1=yv[:, :, 1:], op0=ALU.mult, op1=ALU.add), reads=["tmp0", "tmp1"], writes=["tmp1"])
                    S.dve(lambda e: e.scalar_tensor_tensor(yv[:, :, :seg - 1], in0=prv[:, :, 1:], scalar=cw_t[:, widx, 2:3], in1=yv[:, :, :seg - 1], op0=ALU.mult, op1=ALU.add), reads=["tmp0", "tmp1"], writes=["tmp1"])
                    return y

                if not isctx:
                    def ev_hy(ci, ps, pk):
                        y = conv(ci, ps, pk, cwh, cbh, ci, None)
                        yb = TMP[2].bitcast(BF16)
                        S.act(lambda e: e.copy(yb[:, :T], y[:, :T]), reads=["tmp1"], writes=["tmp2"])
                        pt = PS[7].bitcast(BF16)
                        for bl in range(T // 128):
                            S.pe(lambda e, bl=bl: e.transpose(pt[:, bl * 128:(bl + 1) * 128], yb[:, bl * 128:(bl + 1) * 128], identb), reads=["tmp2", "identb"], writes=["PS7"])
                        c8 = ci % 8
                        S.dve(lambda e, c8=c8: e.tensor_copy(STG[:, :T // 128, c8 * 128:(c8 + 1) * 128], pt[:, :T].rearrange("p (b c) -> p b c", c=128)), reads=["PS7"], writes=["STG"])
                        if c8 == 7:
                            g8 = ci // 8
                            S.dma(HYT[t0:t0 + T, g8 * 1024:(g8 + 1) * 1024].rearrange("(b p) c -> p b c", p=128), STG[:, :T // 128, :], reads=["STG"], writes=["HYT"])
                    linear(T, "w_in", KC, 0, HY_COLS, ev_hy)
                def ev_qkv(ci, ps, pk, base):
                    gi = base + ci
                    y = conv(ci, ps, pk, cwg, zb, gi, None)
                    a = TMP[2]
                    S.act(lambda e: e.activation(a[:, :T], y[:, :T], AF.Silu), reads=["tmp1"], writes=["tmp2"])
                    which = gi // 8; h = gi % 8
                    dst = (QT, KT, VT)[which]
                    if which < 2:
                        sq = TMP[3]
                        S.dve(lambda e: e.tensor_tensor(sq[:, :T], a[:, :T], a[:, :T], op=ALU.mult), reads=["tmp2"], writes=["tmp3"])
                        pss = PS[6]
                        S.pe(lambda e: e.matmul(pss[:, :T], lhsT=onesf, rhs=sq[:, :T], start=True, stop=True), reads=["tmp3", "onesf"], writes=["PS6"])
                        rn = TMP[4]
                        S.dve(lambda e: e.tensor_scalar(rn[:, :T], pss[:, :T], 1e-6, None, op0=ALU.add), reads=["PS6"], writes=["tmp4"])
                        S.act(lambda e: e.activation(rn[:, :T], rn[:, :T], AF.Ln), reads=["tmp4"], writes=["tmp4"])
                        S.act(lambda e: e.activation(rn[:, :T], rn[:, :T], AF.Exp, scale=-0.5), reads=["tmp4"], writes=["tmp4"])
                        scl = (128.0 ** -0.5) if which == 0 else 1.0
                        S.dve(lambda e: e.scalar_tensor_tensor(sq[:, :T], in0=a[:, :T], scalar=scl, in1=rn[:, :T], op0=ALU.mult, op1=ALU.mult), reads=["tmp2", "tmp4"], writes=["tmp3"])
                        res = sq; ak = "tmp3"
                    else:
                        res = a; ak = "tmp2"
                    S.dma(dst[h * 128:(h + 1) * 128, tcol:tcol + T], res[:, :T], reads=[ak], writes=[("QT", "KT", "VT")[which]])
                if own:
                    linear(T, "w_in", KC, Q_OFF, 1024, lambda ci, ps, pk: ev_qkv(ci, ps, pk, 0))
                linear(T, "w_in", KC, KV_OFF, 2048, lambda ci, ps, pk: ev_qkv(ci, ps, pk, 8))
                if own:
                    def ev_z(ci, ps, pk):
                        zt = TMP[2].bitcast(BF16)
                        S.act(lambda e, ps=ps: e.activation(zt[:, :T], ps[:, :T], AF.Silu), reads=[pk], writes=["tmp2"])
                        S.dma(ZG[ci * 128:(ci + 1) * 128, t0:t0 + T], zt[:, :T], reads=["tmp2"], writes=["ZG"])
                    linear(T, "w_in", KC, Z_OFF, 1024, ev_z)
                for bl in range(T // 128):
                    ps, pk = next_ps()
                    for k in range(KC):
                        S.pe(lambda e, ps=ps, k=k, bl=bl: e.matmul(ps[:, :32], lhsT=XM[:, k, bl * 128:(bl + 1) * 128], rhs=wab[:, k, :], start=(k == 0), stop=(k == KC - 1)), reads=["XM", "wab"], writes=[pk])
                    a = TMP[3]
                    S.dve(lambda e, ps=ps: e.tensor_copy(a[:, :32], ps[:, :32]), reads=[pk], writes=["tmp3"])
                    S.dma(ABs[tcol + bl * 128:tcol + (bl + 1) * 128, :], a[:, :32], reads=["tmp3"], writes=["ABs"])
            for tl in tiles:
                do_tile(*tl)
            st_a.__exit__(None, None, None)
            S.barrier()


        if "F" in stages:
            st_f = contextlib.ExitStack(); st_f.__enter__()
            def sbf(name, shape, dt=F32):
                return st_f.enter_context(nc.sbuf_tensor("sF_" + name, list(shape), dt)).ap()
            TWO_PI = 2.0 * math.pi
            w3s = sbf("w3s", [64, 4096]); h2T = sbf("h2T", [64, 4096]); deltab = sbf("deltab", [128, 1024]); thn = sbf("thn", [128, 32])
            mb0 = sbf("mb0", [128, 2]); negpi = sbf("negpi", [128, 1]); frs = sbf("frs", [64, 1]); s1 = sbf("s1", [64, 1]); s2a = sbf("s2a", [64, 1]); s2b = sbf("s2b", [64, 1])
            b1s = sbf("b1s", [64, 1]); b2s = sbf("b2s", [64, 1])
            S.dma(w3s, fw3, writes=["w3s"]); S.dma(deltab, deltas_d.partition_broadcast(128), writes=["deltab"]); S.dma(thn, that_d, writes=["thn"])
            S.dma(mb0, mb0_d, writes=["mb0"]); S.dma(frs, ffr, writes=["frs"]); S.dma(b1s, fb1, writes=["b1s"]); S.dma(b2s, fb2, writes=["b2s"])
            S.pool(lambda e: e.memset(negpi, -math.pi), writes=["negpi"])
            S.dve(lambda e: e.tensor_scalar(thn, thn, -1.0, None, op0=ALU.mult), reads=["thn"], writes=["thn"])
            S.dve(lambda e: e.tensor_scalar(s1, frs, 1.0 / TWO_PI, None, op0=ALU.mult), reads=["frs"], writes=["s1"])
            S.dve(lambda e: e.tensor_tensor(s2a, frs, b1s, op=ALU.mult), reads=["frs", "b1s"], writes=["s2a"])
            S.dve(lambda e: e.tensor_scalar(s2a, s2a, 1.0 / TWO_PI, 16.5, op0=ALU.mult, op1=ALU.add), reads=["s2a"], writes=["s2a"])
            S.dve(lambda e: e.tensor_tensor(s2b, frs, b2s, op=ALU.mult), reads=["frs", "b2s"], writes=["s2b"])
            S.dve(lambda e: e.tensor_scalar(s2b, s2b, 1.0 / TWO_PI, 16.5, op0=ALU.mult, op1=ALU.add), reads=["s2b"], writes=["s2b"])
            st_f1 = contextlib.ExitStack(); st_f1.__enter__()
            def sbf1(name, shape, dt=F32):
                return st_f1.enter_context(nc.sbuf_tensor("sF1_" + name, list(shape), dt)).ap()
            zTs = sbf1("zTs", [33, 4096]); w1s = sbf1("w1s", [33, 64]); w2s = sbf1("w2s", [64, 64]); h1T = sbf1("h1T", [64, 4096])
            yt = sbf1("yt", [64, 512]); kit = sbf1("kit", [64, 512], mybir.dt.int32); kft = sbf1("kft", [64, 512])
            S.dma(zTs, zT_d, writes=["zTs"]); S.dma(w1s, fw1, writes=["w1s"]); S.dma(w2s, fw2, writes=["w2s"])
            def sin_layer(srcT, skey, kdim, wS, wkey, s2, s2key, dstT, dkey):
                for tt in range(8):
                    ps, pk = next_ps()
                    S.pe(lambda e, ps=ps, tt=tt: e.matmul(ps[:64, :512], lhsT=wS[:kdim, :], rhs=srcT[:kdim, tt * 512:(tt + 1) * 512], start=True, stop=True), reads=[skey, wkey], writes=[pk])
                    S.dve(lambda e, ps=ps: e.tensor_scalar(yt, ps[:64, :512], s1[:, 0:1], s2[:, 0:1], op0=ALU.mult, op1=ALU.add), reads=[pk, "s1", s2key], writes=["yt"])
                    S.dve(lambda e: e.tensor_copy(kit, yt), reads=["yt"], writes=["kit"])
                    S.dve(lambda e: e.tensor_copy(kft, kit), reads=["kit"], writes=["kft"])
                    S.dve(lambda e: e.tensor_tensor(yt, yt, kft, op=ALU.subtract), reads=["yt", "kft"], writes=["yt"])
                    S.dve(lambda e: e.tensor_single_scalar(kft, yt, 0.0, op=ALU.is_lt), reads=["yt"], writes=["kft"])
                    S.dve(lambda e: e.tensor_tensor(yt, yt, kft, op=ALU.add), reads=["yt", "kft"], writes=["yt"])
                    S.act(lambda e, tt=tt: e.activation(dstT[:, tt * 512:(tt + 1) * 512], yt, AF.Sin, bias=negpi[:64, 0:1], scale=TWO_PI), reads=["yt", "negpi"], writes=[dkey])
            sin_layer(zTs, "zTs", 33, w1s, "w1s", s2a, "s2a", h1T, "h1T")
            sin_layer(h1T, "h1T", 64, w2s, "w2s", s2b, "s2b", h2T, "h2T")
            st_f1.__exit__(None, None, None)
            HS = sbf("HS", [128, 32, 512], BF16); HD = sbf("HD", [128, 32, 512], BF16)
            CB = [sbf(f"CBf{i}", [128, 32, 128], BF16) for i in range(2)]; SBk = [sbf(f"SBf{i}", [128, 32, 128], BF16) for i in range(2)]
            FT = [[sbf(f"ft{i}_{j}", [128, 512]) for j in range(6)] for i in range(2)]
            RN = sbf("RN", [128, 512])
            for o in range(2):
                for hh in range(2):
                    colf = o * 2048 + hh * 512; colb = o * 2048 + 1024 + hh * 512
                    for tc in range(32):
                        tw, thf, thb, ta1, ta2, thm = FT[tc % 2]; fk = [f"ft{tc % 2}_{j}" for j in range(6)]
                        S.act(lambda e, tw=tw, tc=tc, hh=hh: e.activation(tw, deltab[:, hh * 512:(hh + 1) * 512], AF.Exp, scale=thn[:, tc:tc + 1]), reads=["deltab", "thn"], writes=[fk[0]])
                        psf, pkf = next_ps(); psb, pkb = next_ps()
                        S.pe(lambda e, psf=psf, tc=tc, colf=colf: e.matmul(psf, lhsT=h2T[:, tc * 128:(tc + 1) * 128], rhs=w3s[:, colf:colf + 512], start=True, stop=True), reads=["h2T", "w3s"], writes=[pkf])
                        S.pe(lambda e, psb=psb, tc=tc, colb=colb: e.matmul(psb, lhsT=h2T[:, tc * 128:(tc + 1) * 128], rhs=w3s[:, colb:colb + 512], start=True, stop=True), reads=["h2T", "w3s"], writes=[pkb])
                        S.dve(lambda e, psf=psf, thf=thf, tw=tw: e.tensor_tensor(thf, psf, tw, op=ALU.mult), reads=[pkf, fk[0]], writes=[fk[1]])
                        S.dve(lambda e, psb=psb, thb=thb, tw=tw: e.tensor_tensor(thb, psb, tw, op=ALU.mult), reads=[pkb, fk[0]], writes=[fk[2]])
                        S.act(lambda e, ta1=ta1, thf=thf: e.activation(ta1, thf, AF.Abs), reads=[fk[1]], writes=[fk[3]])
                        S.act(lambda e, ta2=ta2, thb=thb: e.activation(ta2, thb, AF.Abs), reads=[fk[2]], writes=[fk[4]])
                        S.pool(lambda e, ta1=ta1, ta2=ta2: e.tensor_tensor(ta1, ta1, ta2, op=ALU.add), reads=[fk[3], fk[4]], writes=[fk[3]])
                        S.pe(lambda e, ta1=ta1, tc=tc: e.matmul(PS[6], lhsT=onesf, rhs=ta1, start=(tc == 0), stop=(tc == 31)), reads=[fk[3], "onesf"], writes=["PS6"])
                        if tc == 0:
                            S.dve(lambda e, thm=thm, thb=thb: e.tensor_scalar(thm, thb, mb0[:, 1:2], None, op0=ALU.mult), reads=[fk[2], "mb0"], writes=[fk[5]])
                            S.dve(lambda e, tw=tw, thf=thf: e.tensor_scalar(tw, thf, mb0[:, 0:1], None, op0=ALU.mult), reads=[fk[1], "mb0"], writes=[fk[0]])
                            S.pool(lambda e, tw=tw, thm=thm, tc=tc: e.tensor_tensor(HS[:, tc, :], tw, thm, op=ALU.add), reads=[fk[0], fk[5]], writes=["HS"])
                        else:
                            S.pool(lambda e, thf=thf, thb=thb, tc=tc: e.tensor_tensor(HS[:, tc, :], thf, thb, op=ALU.add), reads=[fk[1], fk[2]], writes=["HS"])
                        S.dve(lambda e, thf=thf, thb=thb, tc=tc: e.tensor_tensor(HD[:, tc, :], thf, thb, op=ALU.subtract), reads=[fk[1], fk[2]], writes=["HD"])
                    S.dve(lambda e: e.reciprocal(RN, PS[6]), reads=["PS6"], writes=["RN"])
                    S.dve(lambda e: e.tensor_scalar(RN, RN, 2.0 / 8192.0, None, op0=ALU.mult), reads=["RN"], writes=["RN"])
                    for fc in range(32):
                        cb = CB[fc % 2]; sk = SBk[fc % 2]; cbk = f"CB{fc % 2}"; skk = f"SBk{fc % 2}"
                        S.dma(cb, CFt[fc], writes=[cbk]); S.dma(sk, SFt[fc], writes=[skk])
                        psC, pkC = next_ps(); psS, pkS = next_ps()
                        for tc in range(32):
                            S.pe(lambda e, psC=psC, cb=cb, tc=tc: e.matmul(psC, lhsT=cb[:, tc, :], rhs=HS[:, tc, :], start=(tc == 0), stop=(tc == 31)), reads=[cbk, "HS"], writes=[pkC])
                        for tc in range(32):
                            S.pe(lambda e, psS=psS, sk=sk, tc=tc: e.matmul(psS, lhsT=sk[:, tc, :], rhs=HD[:, tc, :], start=(tc == 0), stop=(tc == 31)), reads=[skk, "HD"], writes=[pkS])
                        ta, tb = FT[fc % 2][0], FT[fc % 2][1]; tak, tbk = f"ft{fc % 2}_0", f"ft{fc % 2}_1"
                        S.dve(lambda e, psC=psC, ta=ta: e.tensor_tensor(ta, psC, RN, op=ALU.mult), reads=[pkC, "RN"], writes=[tak])
                        S.dve(lambda e, psS=psS, tb=tb: e.tensor_tensor(tb, psS, RN, op=ALU.mult), reads=[pkS, "RN"], writes=[tbk])
                        cc = o * 1024 + hh * 512
                        S.dma(KCs[fc * 128:(fc + 1) * 128, cc:cc + 512], ta, reads=[tak], writes=["KCs"])
                        S.dma(KSs[fc * 128:(fc + 1) * 128, cc:cc + 512], tb, reads=[tbk], writes=["KSs"])
            st_f.__exit__(None, None, None)
            S.barrier()

        if "H" in stages:
            st_h = contextlib.ExitStack(); st_h.__enter__()
            def sbh(name, shape, dt=F32):
                return st_h.enter_context(nc.sbuf_tensor("sH_" + name, list(shape), dt)).ap()
            ZIN = sbh("ZIN", [128, 32, 1024], BF16)
            CBh = [sbh(f"CB{i}", [128, 32, 128], BF16) for i in range(2)]; SBh = [sbh(f"SB{i}", [128, 32, 128], BF16) for i in range(2)]
            HT = [[sbh(f"ht{i}_{j}", [128, 512]) for j in range(8)] for i in range(2)]
            PQ = [[sbh(f"pq{i}_{j}", [128, 512], BF16) for j in range(4)] for i in range(2)]
            biasb = sbh("biasb", [128, 2, 1024]); ZST = sbh("ZST", [128, 4, 512], BF16)
            for o in range(2):
                S.dma(biasb[:, o, :], hyb_d[o:o + 1, :].partition_broadcast(128), writes=["biasb"])
            for o in range(2):
                src = HYT[:, 0:1024] if o == 0 else Z2T
                skey = "HYT" if o == 0 else "Z2T"
                gsrc = HYT[:, 1024:2048] if o == 0 else HYT[:, 2048:3072]
                for q4 in range(4):
                    S.dma(ZIN[:, q4 * 8:(q4 + 1) * 8, :], src[q4 * 1024:(q4 + 1) * 1024, :].rearrange("(tc p) c -> p tc c", p=128), reads=[skey], writes=["ZIN"])
                for fc in range(32):
                    cb = CBh[fc % 2]; sk = SBh[fc % 2]; cbk = f"hCB{fc % 2}"; skk = f"hSB{fc % 2}"
                    S.dma(cb, CFt[fc], writes=[cbk]); S.dma(sk, SFt[fc], writes=[skk])
                    for ct in range(2):
                        par = (fc * 2 + ct) % 2
                        ht = HT[par]; hk = [f"ht{par}_{j}" for j in range(8)]; pq = PQ[par]; pk_ = [f"pq{par}_{j}" for j in range(4)]
                        psA, pkA = next_ps(); psB, pkB = next_ps()
                        for tc in range(32):
                            S.pe(lambda e, psA=psA, cb=cb, tc=tc, ct=ct: e.matmul(psA, lhsT=cb[:, tc, :], rhs=ZIN[:, tc, ct * 512:(ct + 1) * 512], start=(tc == 0), stop=(tc == 31)), reads=[cbk, "ZIN"], writes=[pkA])
                        for tc in range(32):
                            S.pe(lambda e, psB=psB, sk=sk, tc=tc, ct=ct: e.matmul(psB, lhsT=sk[:, tc, :], rhs=ZIN[:, tc, ct * 512:(ct + 1) * 512], start=(tc == 0), stop=(tc == 31)), reads=[skk, "ZIN"], writes=[pkB])
                        kc, ks, A, B, t1, t2, t3, t4 = ht
                        cc = o * 1024 + ct * 512
                        S.dma(kc, KCs[fc * 128:(fc + 1) * 128, cc:cc + 512], reads=["KCs"], writes=[hk[0]])
                        S.dma(ks, KSs[fc * 128:(fc + 1) * 128, cc:cc + 512], reads=["KSs"], writes=[hk[1]])
                        S.act(lambda e, A=A, psA=psA: e.copy(A, psA), reads=[pkA], writes=[hk[2]])
                        S.act(lambda e, B=B, psB=psB: e.copy(B, psB), reads=[pkB], writes=[hk[3]])
                        S.dve(lambda e, t1=t1, A=A, kc=kc: e.tensor_tensor(t1, A, kc, op=ALU.mult), reads=[hk[2], hk[0]], writes=[hk[4]])
                        S.pool(lambda e, t2=t2, B=B, ks=ks: e.tensor_tensor(t2, B, ks, op=ALU.mult), reads=[hk[3], hk[1]], writes=[hk[5]])
                        S.dve(lambda e, t1=t1, t2=t2, p=pq[0]: e.tensor_tensor(p, t1, t2, op=ALU.subtract), reads=[hk[4], hk[5]], writes=[pk_[0]])
                        S.pool(lambda e, t3=t3, A=A, ks=ks: e.tensor_tensor(t3, A, ks, op=ALU.mult), reads=[hk[2], hk[1]], writes=[hk[6]])
                        S.dve(lambda e, t4=t4, B=B, kc=kc: e.tensor_tensor(t4, B, kc, op=ALU.mult), reads=[hk[3], hk[0]], writes=[hk[7]])
                        S.pool(lambda e, t3=t3, t4=t4, q=pq[1]: e.tensor_tensor(q, t3, t4, op=ALU.add), reads=[hk[6], hk[7]], writes=[pk_[1]])
                        S.dma(Ps[fc * 128:(fc + 1) * 128, ct * 512:(ct + 1) * 512], pq[0], reads=[pk_[0]], writes=["Ps"])
                        S.dma(Qs[fc * 128:(fc + 1) * 128, ct * 512:(ct + 1) * 512], pq[1], reads=[pk_[1]], writes=["Qs"])
                for ct in range(2):
                    PB = ZIN[:, :, 0:512]; QB = ZIN[:, :, 512:1024]
                    S.dma(PB, Ps[:, ct * 512:(ct + 1) * 512].rearrange("(fc p) c -> p fc c", p=128), reads=["Ps"], writes=["ZIN"])
                    S.dma(QB, Qs[:, ct * 512:(ct + 1) * 512].rearrange("(fc p) c -> p fc c", p=128), reads=["Qs"], writes=["ZIN"])
                    for tch in range(32 if o == 0 else OWN // 128):
                        ci_ = CBh[tch % 2]; si_ = SBh[tch % 2]; cbk = f"hCB{tch % 2}"; skk = f"hSB{tch % 2}"
                        S.dma(ci_, CIt[tch], writes=[cbk]); S.dma(si_, SIt[tch], writes=[skk])
                        par = tch % 2
                        ht = HT[par]; hk = [f"ht{par}_{j}" for j in range(8)]; pq = PQ[par]; pk_ = [f"pq{par}_{j}" for j in range(4)]
                        psY, pkY = next_ps()
                        for fc in range(32):
                            S.pe(lambda e, psY=psY, ci_=ci_, fc=fc: e.matmul(psY, lhsT=ci_[:, fc, :], rhs=PB[:, fc, :], start=(fc == 0), stop=False), reads=[cbk, "ZIN"], writes=[pkY])
                        for fc in range(32):
                            S.pe(lambda e, psY=psY, si_=si_, fc=fc: e.matmul(psY, lhsT=si_[:, fc, :], rhs=QB[:, fc, :], start=False, stop=(fc == 31)), reads=[skk, "ZIN"], writes=[pkY])
                        zin_t = pq[2]; gate_t = pq[3]; res = pq[0]
                        S.dma(zin_t, src[tch * 128:(tch + 1) * 128, ct * 512:(ct + 1) * 512], reads=[skey], writes=[pk_[2]])
                        S.dma(gate_t, gsrc[tch * 128:(tch + 1) * 128, ct * 512:(ct + 1) * 512], reads=["HYT"], writes=[pk_[3]])
                        t1, t2 = ht[4], ht[5]
                        S.dve(lambda e, t1=t1, zin_t=zin_t, o=o, ct=ct: e.tensor_tensor(t1, zin_t, biasb[:, o, ct * 512:(ct + 1) * 512], op=ALU.mult), reads=[pk_[2], "biasb"], writes=[hk[4]])
                        S.dve(lambda e, t2=t2, t1=t1, psY=psY: e.tensor_tensor(t2, psY, t1, op=ALU.add), reads=[pkY, hk[4]], writes=[hk[5]])
                        S.pool(lambda e, res=res, t2=t2, gate_t=gate_t: e.tensor_tensor(res, t2, gate_t, op=ALU.mult), reads=[hk[5], pk_[3]], writes=[pk_[0]])
                        if o == 0:
                            S.dma(Z2T[tch * 128:(tch + 1) * 128, ct * 512:(ct + 1) * 512], res, reads=[pk_[0]], writes=["Z2T"])
                        else:
                            pt = PS[7].bitcast(BF16)
                            for j in range(4):
                                S.pe(lambda e, res=res, j=j: e.transpose(pt[:, j * 128:(j + 1) * 128], res[:, j * 128:(j + 1) * 128], identb), reads=[pk_[0], "identb"], writes=["PS7"])
                            t4_ = tch % 4
                            S.act(lambda e, t4_=t4_: e.copy(ZST[:, :, t4_ * 128:(t4_ + 1) * 128], pt[:, 0:512].rearrange("p (j t) -> p j t", j=4)), reads=["PS7"], writes=["ZST"])
                            if t4_ == 3:
                                tg = tch // 4
                                S.dma(ZH[ct * 512:(ct + 1) * 512, tg * 512:(tg + 1) * 512].rearrange("(j p) t -> p j t", p=128), ZST, reads=["ZST"], writes=["ZH"])
            st_h.__exit__(None, None, None)
            S.barrier()

        if "G" in stages:
            st_g = contextlib.ExitStack(); st_g.__enter__()
            def sbg(name, shape, dt=F32):
                return st_g.enter_context(nc.sbuf_tensor("sG_" + name, list(shape), dt)).ap()
            NCH = LTOT // 128
            tri = sbg("tri", [128, 4, 128]); S.dma(tri, tri_d, writes=["tri"])
            TRI_LE, TRI_GE, TRI_GT, TRI_LT = tri[:, 0, :], tri[:, 1, :], tri[:, 2, :], tri[:, 3, :]
            gng = sbg("gng", [128, 1]); S.dma(gng, gng_d, writes=["gng"])
            alb = sbg("alb", [128, 16]); dtbb = sbg("dtbb", [128, 16]); negea = sbg("negea", [128, 16])
            S.dma(alb, alog_d.partition_broadcast(128), writes=["alb"]); S.dma(dtbb, dtb_d.partition_broadcast(128), writes=["dtbb"])
            S.act(lambda e: e.activation(negea, alb, AF.Exp), reads=["alb"], writes=["negea"])
            S.dve(lambda e: e.tensor_scalar(negea, negea, -1.0, None, op0=ALU.mult), reads=["negea"], writes=["negea"])
            ABt = sbg("ABt", [128, NCH, 32]); S.dma(ABt, ABs.rearrange("(c p) n -> p c n", p=128), reads=["ABs"], writes=["ABt"])
            X = sbg("X", [128, NCH, 16])
            for col in range(16):
                S.dve(lambda e, col=col: e.tensor_scalar(X[:, :, col], ABt[:, :, col], dtbb[:, col:col + 1], None, op0=ALU.add), reads=["ABt", "dtbb"], writes=["X"])
            S.act(lambda e: e.activation(X, X, AF.Exp), reads=["X"], writes=["X"])
            S.act(lambda e: e.activation(X, X, AF.Ln, bias=1.0), reads=["X"], writes=["X"])
            Gg = sbg("Gg", [128, 2, NCH, 8]); BETA = sbg("BETA", [128, 2, NCH, 8]); GC = sbg("GC", [128, 2, NCH, 8]); GLt = sbg("GLt", [128, 2, NCH, 8])
            EG = sbg("EG", [128, 2, NCH, 8]); NEG = sbg("NEG", [128, 2, NCH, 8]); EKD = sbg("EKD", [128, 2, NCH, 8]); EGL = sbg("EGL", [128, 2, NCH, 8])
            for col in range(16):
                d_, h_ = col // 8, col % 8
                S.dve(lambda e, col=col, d_=d_, h_=h_: e.tensor_scalar(Gg[:, d_, :, h_], X[:, :, col], negea[:, col:col + 1], None, op0=ALU.mult), reads=["X", "negea"], writes=["Gg"])
            for d_ in range(2):
                S.act(lambda e, d_=d_: e.activation(BETA[:, d_, :, :], ABt[:, :, 16 + 8 * d_:24 + 8 * d_], AF.Sigmoid), reads=["ABt"], writes=["BETA"])
                gflat = Gg[:, d_, :, :].rearrange("p c h -> p (c h)")
                S.pe(lambda e, d_=d_, gflat=gflat: e.matmul(PS[0][:, :NCH * 8], lhsT=(TRI_LE if d_ == 0 else TRI_GE), rhs=gflat, start=True, stop=True), reads=["Gg", "tri"], writes=["PS0"])
                S.dve(lambda e, d_=d_: e.tensor_copy(GC[:, d_, :, :].rearrange("p c h -> p (c h)"), PS[0][:, :NCH * 8]), reads=["PS0"], writes=["GC"])
                S.pe(lambda e, d_=d_, gflat=gflat: e.matmul(PS[1][:, :NCH * 8], lhsT=onesf, rhs=gflat, start=True, stop=True), reads=["Gg", "onesf"], writes=["PS1"])
                S.dve(lambda e, d_=d_: e.tensor_copy(GLt[:, d_, :, :].rearrange("p c h -> p (c h)"), PS[1][:, :NCH * 8]), reads=["PS1"], writes=["GLt"])
            S.act(lambda e: e.activation(EG, GC, AF.Exp), reads=["GC"], writes=["EG"])
            S.dve(lambda e: e.tensor_scalar(NEG, EG, -1.0, None, op0=ALU.mult), reads=["EG"], writes=["NEG"])
            S.dve(lambda e: e.tensor_tensor(EKD, GLt, GC, op=ALU.subtract), reads=["GLt", "GC"], writes=["EKD"])
            S.act(lambda e: e.activation(EKD, EKD, AF.Exp), reads=["EKD"], writes=["EKD"])
            S.act(lambda e: e.activation(EGL, GLt, AF.Exp), reads=["GLt"], writes=["EGL"])
            if dbg:
                for nm, t_ in (("Gg", Gg), ("BETA", BETA)):
                    dbg_outs[nm] = nc.dram_tensor("dbg_" + nm, [128, 2, NCH, 8], F32, kind="ExternalOutput").ap()
                    S.dma(dbg_outs[nm], t_, reads=[nm], writes=["dbg_" + nm])
            S8 = sbg("S8", [128, 8, 128])
            KB = [sbg(f"kT8_{i}", [128, 8, 128]) for i in range(2)]; VB = [sbg(f"vT8_{i}", [128, 8, 128]) for i in range(2)]; QB_ = [sbg(f"qT8_{i}", [128, 8, 128]) for i in range(2)]
            O8 = [sbg(f"O8_{i}", [128, 8, 128]) for i in range(2)]; OFt = [sbg(f"OFt_{i}", [128, 8, 128]) for i in range(2)]
            ZGt = [sbg(f"ZGt_{i}", [128, 8, 128], BF16) for i in range(2)]; OGc = [sbg(f"OGc_{i}", [128, 8, 128], BF16) for i in range(2)]
            ONn = sbg("ONn", [128, 8, 128]); SQn = sbg("SQn", [128, 8, 128]); ssn = sbg("ssn", [128, 8])
            NSET = 8
            names = ["gU", "ET", "ETs", "ETi", "Pa", "Pb", "PTa", "PTb", "TT", "kd", "vtok", "R", "vnew", "qkT", "o2s"]
            TS = [{n: sbg(f"{n}_{i}", [128, 128]) for n in names} for i in range(NSET)]
            C.pg_rr = 0
            def next_pg():
                i = C.pg_rr % 8
                C.pg_rr += 1
                return PS[i][:, 0:128], f"PS{i}"

            GSTOP = 99; GPROB = 10 ** 9
            def prob(d, c, h, pi, par):
                isctx = (c >= 32) or (d == 1 and c * 128 >= OWN)
                ts = TS[pi % NSET]; tk = {n: f"{n}_{pi % NSET}" for n in names}
                gcol = Gg[:, d, c, h:h + 1]; bcol = BETA[:, d, c, h:h + 1]; negeg = NEG[:, d, c, h:h + 1]; eg = EG[:, d, c, h:h + 1]
                ekd = EKD[:, d, c, h:h + 1]; egl = EGL[:, d, c, h:h + 1]
                U, Lm, MsT, MiT = (TRI_LE, TRI_GT, TRI_LT, TRI_LE) if d == 0 else (TRI_GE, TRI_LT, TRI_GT, TRI_GE)
                kT = KB[par][:, h, :]; vT = VB[par][:, h, :]; qT = QB_[par][:, h, :]
                kk, vk, qk_ = f"kT8_{par}", f"vT8_{par}", f"qT8_{par}"
                gU, ET, ETs, ETi, TT = ts["gU"], ts["ET"], ts["ETs"], ts["ETi"], ts["TT"]
                pA = PS[h][:, 0:128]; pB = PS[h][:, 128:256]; pk = f"PS{h}"
                S.dve(lambda e: e.tensor_scalar(gU, U, gcol, None, op0=ALU.mult), reads=["tri", "Gg"], writes=[tk["gU"]])
                S.pe(lambda e: e.matmul(pA, lhsT=Lm, rhs=gU, start=True, stop=True), reads=["tri", tk["gU"]], writes=[pk])
                S.act(lambda e: e.activation(ET, pA, AF.Exp), reads=[pk], writes=[tk["ET"]])
                S.pool(lambda e: e.tensor_tensor(ETs, ET, MsT, op=ALU.mult), reads=[tk["ET"], "tri"], writes=[tk["ETs"]])
                yield
                P0 = ts["Pa"]; PT0 = ts["PTa"]
                S.pe(lambda e: e.matmul(pA, lhsT=kT, rhs=kT, start=True, stop=True), reads=[kk], writes=[pk])
                S.dve(lambda e: e.scalar_tensor_tensor(P0, in0=pA, scalar=bcol, in1=ETs, op0=ALU.mult, op1=ALU.mult), reads=[pk, "BETA", tk["ETs"]], writes=[tk["Pa"]])
                yield
                if not isctx:
                    qkT = ts["qkT"]
                    S.pool(lambda e: e.tensor_tensor(ETi, ET, MiT, op=ALU.mult), reads=[tk["ET"], "tri"], writes=[tk["ETi"]])
                    S.pe(lambda e: e.matmul(pA, lhsT=kT, rhs=qT, start=True, stop=True), reads=[kk, qk_], writes=[pk])
                    S.dve(lambda e: e.tensor_tensor(qkT, pA, ETi, op=ALU.mult), reads=[pk, tk["ETi"]], writes=[tk["qkT"]])
                    yield
                S.pe(lambda e: e.transpose(pA, P0, ident), reads=[tk["Pa"], "ident"], writes=[pk])
                S.act(lambda e: e.copy(PT0, pA), reads=[pk], writes=[tk["PTa"]])
                S.pool(lambda e: e.tensor_tensor(TT, ident, P0, op=ALU.subtract), reads=["ident", tk["Pa"]], writes=[tk["TT"]])
                yield
                Pc, PTc, Pck, PTck = P0, PT0, tk["Pa"], tk["PTa"]
                for l in range(1, 7):
                    Pn, PTn = (ts["Pb"], ts["PTb"]) if l % 2 == 1 else (ts["Pa"], ts["PTa"])
                    Pnk, PTnk = (tk["Pb"], tk["PTb"]) if l % 2 == 1 else (tk["Pa"], tk["PTa"])
                    if l < 6:
                        S.pe(lambda e, PTc=PTc, Pc=Pc: e.matmul(pA, lhsT=PTc, rhs=Pc, start=True, stop=True), reads=[PTck, Pck], writes=[pk])
                        S.act(lambda e, Pn=Pn: e.copy(Pn, pA), reads=[pk], writes=[Pnk])
                        yield
                    S.pe(lambda e, PTc=PTc, Pc=Pc: e.matmul(pA, lhsT=Pc, rhs=PTc, start=True, stop=True), reads=[PTck, Pck], writes=[pk])
                    S.dve(lambda e, PTn=PTn: e.tensor_copy(PTn, pA), reads=[pk], writes=[PTnk])
                    yield
                    S.pe(lambda e, PTn=PTn: e.matmul(pA, lhsT=PTn, rhs=TT, start=True, stop=True), reads=[PTnk, tk["TT"]], writes=[pk])
                    S.dve(lambda e: e.tensor_tensor(TT, TT, pA, op=ALU.add), reads=[tk["TT"], pk], writes=[tk["TT"]])
                    yield
                    Pc, PTc, Pck, PTck = Pn, PTn, Pnk, PTnk
                kd, vtok, R_, vnew = ts["kd"], ts["vtok"], ts["R"], ts["vnew"]
                S.pe(lambda e: e.transpose(pA, kT, ident), reads=[kk, "ident"], writes=[pk])
                S.dve(lambda e: e.tensor_scalar(kd, pA, ekd, None, op0=ALU.mult), reads=[pk, "EKD"], writes=[tk["kd"]])
                yield
                S.pe(lambda e: e.transpose(pA, vT, ident), reads=[vk, "ident"], writes=[pk])
                S.act(lambda e: e.copy(vtok, pA), reads=[pk], writes=[tk["vtok"]])
                yield
                Sh = S8[:, h, :]; sk_ = f"S8_{h}"
                S.pe(lambda e: e.matmul(pA, lhsT=kT, rhs=Sh, start=True, stop=True), reads=[kk, sk_], writes=[pk])
                S.dve(lambda e: e.scalar_tensor_tensor(R_, in0=pA, scalar=negeg, in1=vtok, op0=ALU.mult, op1=ALU.add), reads=[pk, "NEG", tk["vtok"]], writes=[tk["R"]])
                yield
                S.pe(lambda e: e.matmul(pA, lhsT=TT, rhs=R_, start=True, stop=True), reads=[tk["TT"], tk["R"]], writes=[pk])
                S.act(lambda e: e.activation(vnew, pA, AF.Identity, scale=bcol), reads=[pk, "BETA"], writes=[tk["vnew"]])
                yield
                if not isctx:
                    o2s = ts["o2s"]
                    S.pe(lambda e: e.matmul(pA, lhsT=qT, rhs=Sh, start=True, stop=True), reads=[qk_, sk_], writes=[pk])
                    S.pe(lambda e: e.matmul(pB, lhsT=qkT, rhs=vnew, start=True, stop=True), reads=[tk["qkT"], tk["vnew"]], writes=[pk])
                    S.act(lambda e: e.copy(o2s, pB), reads=[pk], writes=[tk["o2s"], pk])
                    S.dve(lambda e: e.scalar_tensor_tensor(O8[par][:, h, :], in0=pA, scalar=eg, in1=o2s, op0=ALU.mult, op1=ALU.add), reads=[pk, "EG", tk["o2s"]], writes=[f"O8_{par}", pk])
                    yield
                S.pe(lambda e: e.matmul(pA, lhsT=kd, rhs=vnew, start=True, stop=True), reads=[tk["kd"], tk["vnew"]], writes=[pk])
                S.dve(lambda e: e.scalar_tensor_tensor(Sh, in0=Sh, scalar=egl, in1=pA, op0=ALU.mult, op1=ALU.add), reads=[sk_, "EGL", pk], writes=[sk_])
                yield

            pi = 0
            lat = list(range(32))
            if dbg == "short":
                lat = [0, 1]

            for d in range(2 if GSTOP > 0 else 0):
                S.pool(lambda e: e.memset(S8, 0.0), reads=[f"S8_{h}" for h in range(8)], writes=[f"S8_{h}" for h in range(8)])
                order = ([32, 33] + [c_ for c_ in lat if c_ * 128 < OWN]) if d == 0 else ([33, 32] + lat[::-1])
                for n_, c in enumerate(order):
                    par = n_ % 2
                    isctx = (c >= 32) or (d == 1 and c * 128 >= OWN)
                    S.dma(KB[par], KT[:, c * 128:(c + 1) * 128].rearrange("(h p) t -> p h t", p=128), reads=["KT"], writes=[f"kT8_{par}"])
                    S.dma(VB[par], VT[:, c * 128:(c + 1) * 128].rearrange("(h p) t -> p h t", p=128), reads=["VT"], writes=[f"vT8_{par}"])
                    if not isctx:
                        S.dma(QB_[par], QT[:, c * 128:(c + 1) * 128].rearrange("(h p) t -> p h t", p=128), reads=["QT"], writes=[f"qT8_{par}"])
                    gens = []
                    for h in range(8):
                        gens.append(prob(d, c, h, pi, par))
                        pi += 1
                    while gens:
                        alive = []
                        for g_ in gens:
                            try:
                                next(g_)
                                alive.append(g_)
                            except StopIteration:
                                pass
                        gens = alive
                    if isctx:
                        continue
                    if GSTOP < 4: continue
                    if d == 0:
                        S.dma(OFs[c * 128:(c + 1) * 128], O8[par], reads=[f"O8_{par}"], writes=["OFs"])
                    else:
                        def epilogue(c=c, par=par):
                            oft = OFt[par]; zg = ZGt[par]; ogc = OGc[par]; o8 = O8[par]
                            S.dma(oft, OFs[c * 128:(c + 1) * 128], reads=["OFs"], writes=[f"OFt_{par}"])
                            S.dma(zg, ZG[:, c * 128:(c + 1) * 128].rearrange("(h p) t -> p h t", p=128), reads=["ZG"], writes=[f"ZGt_{par}"])
                            S.pool(lambda e: e.tensor_tensor(o8, o8, oft, op=ALU.add), reads=[f"O8_{par}", f"OFt_{par}"], writes=[f"O8_{par}"])
                            S.dve(lambda e: e.tensor_tensor(SQn, o8, o8, op=ALU.mult), reads=[f"O8_{par}"], writes=["SQn"])
                            S.dve(lambda e: e.tensor_reduce(ssn, SQn, axis=AX.X, op=ALU.add), reads=["SQn"], writes=["ssn"])
                            S.dve(lambda e: e.tensor_scalar(ssn, ssn, 1.0 / 128.0, EPS, op0=ALU.mult, op1=ALU.add), reads=["ssn"], writes=["ssn"])
                            S.act(lambda e: e.activation(ssn, ssn, AF.Ln), reads=["ssn"], writes=["ssn"])
                            S.act(lambda e: e.activation(ssn, ssn, AF.Exp, scale=-0.5), reads=["ssn"], writes=["ssn"])
                            for h in range(8):
                                S.dve(lambda e, h=h: e.tensor_scalar(ONn[:, h, :], o8[:, h, :], ssn[:, h:h + 1], None, op0=ALU.mult), reads=[f"O8_{par}", "ssn"], writes=["ONn"])
                                pt_, kt_ = next_pg()
                                S.pe(lambda e, h=h, pt_=pt_: e.transpose(pt_, ONn[:, h, :], ident), reads=["ONn", "ident"], writes=[kt_])
                                S.dve(lambda e, h=h, pt_=pt_: e.scalar_tensor_tensor(ogc[:, h, :], in0=pt_, scalar=gng[:, 0:1], in1=zg[:, h, :], op0=ALU.mult, op1=ALU.mult), reads=[kt_, "gng", f"ZGt_{par}"], writes=[f"OGc_{par}"])
                            S.dma(OG[:, c * 128:(c + 1) * 128].rearrange("(h p) t -> p h t", p=128), ogc, reads=[f"OGc_{par}"], writes=["OG"])
                        if GSTOP >= 5: epilogue()
            st_g.__exit__(None, None, None)
            S.barrier()

        if "T" in stages:
            st_t = contextlib.ExitStack(); st_t.__enter__()
            sb = alloc_main(st_t)
            H, XM, ACT, TMP = C.H, C.XM, C.ACT, C.TMP
            ZHt = sb("ZHt", [128, 8, NT], BF16); OGt = sb("OGt", [128, 8, NT], BF16); MRG = sb("MRG", [128, KC, NT], BF16)
            GATES = ACT
            ttiles = [t * NT for t in range(OWN // NT)]
            if dbg == "short":
                ttiles = ttiles[:1]
            def do_tail(t0):
                T = NT
                S.dma(H, H1T[:, t0:t0 + T].rearrange("(k p) t -> p k t", p=128), reads=["H1T"], writes=["H"])
                S.dma(ZHt, ZH[:, t0:t0 + T].rearrange("(k p) t -> p k t", p=128), reads=["ZH"], writes=["ZHt"])
                S.dma(OGt, OG[:, t0:t0 + T].rearrange("(k p) t -> p k t", p=128), reads=["OG"], writes=["OGt"])
                norm_mod(T, 1, 0)
                def ev_gate(ci, ps, pk):
                    S.act(lambda e, ps=ps, ci=ci: e.activation(GATES[:, ci, :T], ps[:, :T], AF.Sigmoid), reads=[pk], writes=["ACT"])
                linear(T, "w_in", KC, GATE_OFF, 2 * D, ev_gate)
                for fb in range(4):
                    wbuf, wkey = next_wb()
                    wa = wbuf[:, 0:4096].rearrange("p (k n) -> p k n", k=8); wb_ = wbuf[:, 4096:8192].rearrange("p (k n) -> p k n", k=8)
                    S.dma(wa, Wb["hy_out"][:, fb * 512:(fb + 1) * 512].rearrange("(k p) n -> p k n", p=128), reads=["Wb_hy_out"], writes=[wkey])
                    S.dma(wb_, Wb["gdn_out"][:, fb * 512:(fb + 1) * 512].rearrange("(k p) n -> p k n", p=128), reads=["Wb_gdn_out"], writes=[wkey])
                    for jj in range(4):
                        ci = fb * 4 + jj
                        psA, pkA = next_ps(); psB, pkB = next_ps()
                        for k in range(8):
                            S.pe(lambda e, psA=psA, wa=wa, k=k, jj=jj: e.matmul(psA[:, :T], lhsT=wa[:, k, jj * 128:(jj + 1) * 128], rhs=ZHt[:, k, :T], start=(k == 0), stop=(k == 7)), reads=[wkey, "ZHt"], writes=[pkA])
                        for k in range(8):
                            S.pe(lambda e, psB=psB, wb_=wb_, k=k, jj=jj: e.matmul(psB[:, :T], lhsT=wb_[:, k, jj * 128:(jj + 1) * 128], rhs=OGt[:, k, :T], start=(k == 0), stop=(k == 7)), reads=[wkey, "OGt"], writes=[pkB])
                        t1 = TMP[0]; t2 = TMP[1]
                        S.dve(lambda e, psA=psA, ci=ci: e.tensor_tensor(t1[:, :T], GATES[:, ci, :T], psA[:, :T], op=ALU.mult), reads=["ACT", pkA], writes=["tmp0"])
                        S.dve(lambda e, psB=psB, ci=ci: e.tensor_tensor(t2[:, :T], GATES[:, KC + ci, :T], psB[:, :T], op=ALU.mult), reads=["ACT", pkB], writes=["tmp1"])
                        S.pool(lambda e, ci=ci: e.tensor_tensor(MRG[:, ci, :T], t1[:, :T], t2[:, :T], op=ALU.add), reads=["tmp0", "tmp1"], writes=["MRG"])
                def ev_o(ci, ps, pk):
                    S.dve(lambda e, ps=ps, ci=ci: e.scalar_tensor_tensor(H[:, ci, :T], in0=ps[:, :T], scalar=Gv[:, 1, ci, 0:1], in1=H[:, ci, :T], op0=ALU.mult, op1=ALU.add),
                          reads=[pk, "H", "Gv"], writes=["H"])
                linear(T, "w_o", KC, 0, D, ev_o, xsrc=MRG, xkey="MRG")
                if dbg:
                    S.dma(dbg_outs["h2"][:, t0:t0 + T].rearrange("(k p) t -> p k t", p=128), H, reads=["H"], writes=["dbg_h2"])
                ffn(T, 1, 2, 0)
                pss = PS[6]
                for k in range(KC):
                    S.act(lambda e, k=k: e.activation(ACT[:, k, :T], H[:, k, :T], AF.Square), reads=["H"], writes=["ACT"])
                for k in range(KC):
                    S.pe(lambda e, k=k: e.matmul(pss[:, :T], lhsT=onesb, rhs=ACT[:, k, :T], start=(k == 0), stop=(k == KC - 1)), reads=["ACT", "onesb"], writes=["PS6"])
                rs = TMP[5]
                S.dve(lambda e: e.tensor_scalar(rs[:, :T], pss[:, :T], 1.0 / D, EPS, op0=ALU.mult, op1=ALU.add), reads=["PS6"], writes=["tmp5"])
                S.act(lambda e: e.activation(rs[:, :T], rs[:, :T], AF.Ln), reads=["tmp5"], writes=["tmp5"])
                S.act(lambda e: e.activation(rs[:, :T], rs[:, :T], AF.Exp, scale=-0.5), reads=["tmp5"], writes=["tmp5"])
                for k in range(KC):
                    S.dve(lambda e, k=k: e.scalar_tensor_tensor(H[:, k, :T], in0=H[:, k, :T], scalar=fg[:, k:k + 1], in1=rs[:, :T], op0=ALU.mult, op1=ALU.mult), reads=["H", "tmp5", "fg"], writes=["H"])
                S.dma(out_d[:, t0:t0 + T].rearrange("(k p) t -> p k t", p=128), H, reads=["H"], writes=["outT"])
            if dbg:
                dbg_outs["h2"] = nc.dram_tensor("dbg_h2", [D, L], F32, kind="ExternalOutput").ap()
            for t0 in ttiles:
                do_tail(t0)
            st_t.__exit__(None, None, None)
            S.barrier()
        C.final_keys = []
        if dbg and "G" in stages:
            dbg_outs["OG"] = nc.dram_tensor("dbg_OG", [GW, L], BF16, kind="ExternalOutput").ap()
            S.dma(dbg_outs["OG"], OG, reads=["OG"], writes=["dbg_OG"])
            dbg_outs["OFs"] = nc.dram_tensor("dbg_OFs", [L, 8, 128], F32, kind="ExternalOutput").ap()
            S.dma(dbg_outs["OFs"], OFs, reads=["OFs"], writes=["dbg_OFs"])
        if dbg and "H" in stages:
            for nm in ("Z2T", "ZH"):
                a = C.scr[nm]
                dbg_outs[nm] = nc.dram_tensor("dbg_" + nm, list(a.shape), BF16, kind="ExternalOutput").ap()
                S.dma(dbg_outs[nm], a, reads=[nm], writes=["dbg_" + nm])
        if dbg and "F" in stages:
            for nm in ("KCs", "KSs"):
                a = C.scr[nm]
                dbg_outs[nm] = nc.dram_tensor("dbg_" + nm, list(a.shape), F32, kind="ExternalOutput").ap()
                S.dma(dbg_outs[nm], a, reads=[nm], writes=["dbg_" + nm])
        if dbg and "A" in stages:
            for nm in ("H1T", "QT", "KT", "VT", "ABs"):
                a = C.scr[nm]
                dbg_outs[nm] = nc.dram_tensor("dbg_" + nm, list(a.shape), F32, kind="ExternalOutput").ap()
                S.dma(dbg_outs[nm], a, reads=[nm], writes=["dbg_" + nm])
            for nm in ("HYT", "ZG"):
                a = C.scr[nm]
                if nm in ext: continue
                dbg_outs[nm] = nc.dram_tensor("dbg_" + nm, list(a.shape), BF16, kind="ExternalOutput").ap()
                S.dma(dbg_outs[nm], a, reads=[nm], writes=["dbg_" + nm])
        S.emit(final_keys=[k for k in S.last_w if isinstance(k, str) and (k.startswith("dbg_") or k == "outT")])
    return nc, C


def _prep_core(inputs, b, rev=False):
    f = np.float32
    g = lambda k: np.asarray(inputs[k])
    m = {}
    xs = g("x")[b]; cs = g("ctx")[b]
    if rev:
        xs = xs[::-1]; cs = cs[::-1]
    m["xT"] = np.ascontiguousarray(xs.T, dtype=f)
    m["ctxT"] = np.ascontiguousarray(cs.T, dtype=f)
    s = np.stack([g("c")[b], g("c_ctx")], 0)
    m["sT"] = np.ascontiguousarray(s.reshape(2, KC, 128).transpose(2, 1, 0), dtype=f)
    m["w_mod"] = np.ascontiguousarray(g("w_mod")[0], dtype=f)
    m["b_mod"] = np.ascontiguousarray(g("b_mod")[0].reshape(9 * KC, 128).T, dtype=f)
    m["norm_g"] = np.ascontiguousarray(g("norm_g")[0].reshape(3 * KC, 128).T, dtype=f)
    m["fin_g"] = np.ascontiguousarray(g("final_norm_g").reshape(KC, 128).T, dtype=f)
    for k in ("ffn1_wgu", "ffn1_wd", "ffn2_wgu", "ffn2_wd", "w_in", "hy_out", "gdn_out", "w_o"):
        m[k] = np.ascontiguousarray(g(k)[0], dtype=f)
    hcw = g("hy_conv_w")[0]; gcw = g("gdn_conv_w")[0]
    if rev:
        hcw = hcw[::-1]; gcw = gcw[::-1]
        wi = m["w_in"].copy()
        ab = wi[:, AB_OFF:AB_OFF + 32].reshape(-1, 2, 2, 8)[:, :, ::-1, :].reshape(-1, 32)
        wi[:, AB_OFF:AB_OFF + 32] = ab
        m["w_in"] = wi
    m["hy_cw"] = np.ascontiguousarray(hcw.reshape(3, 24, 128).transpose(2, 1, 0), dtype=f)
    m["hy_cb"] = np.ascontiguousarray(g("hy_conv_b")[0].reshape(24, 128).T, dtype=f)
    m["g_cw"] = np.ascontiguousarray(gcw.reshape(3, 24, 128).transpose(2, 1, 0), dtype=f)
    return m


_CONST = {}
def _consts():
    if _CONST:
        return _CONST
    import ml_dtypes
    f = np.float32
    Lq = 4096; N = 8192
    pos = np.arange(Lq, dtype=f)
    t = (pos / f(Lq))[:, None].astype(f)
    fb = np.linspace(1e-4, 15, 16, dtype=f)
    ang = (f(2 * math.pi) * t * fb).astype(f)
    z = np.concatenate([t, np.cos(ang), -np.sin(ang)], -1).astype(f)
    _CONST["zT"] = np.ascontiguousarray(z.T)
    _CONST["deltas"] = np.abs(np.linspace(math.log(1e-2) / 1.5, math.log(1e-2) / 0.3, 1024, dtype=f)).reshape(1, 1024).astype(f)
    _CONST["that"] = np.ascontiguousarray((pos / f(Lq)).reshape(32, 128).T)
    tt = np.arange(Lq, dtype=np.int64)[:, None]; ff = np.arange(Lq, dtype=np.int64)[None, :]
    m = ((2 * ff + 1) * tt) % (2 * N)
    th = (2.0 * np.pi / (2 * N)) * m.astype(np.float64)
    Cm = np.cos(th).astype(ml_dtypes.bfloat16); Sm = np.sin(th).astype(ml_dtypes.bfloat16)
    del th, m
    def fwd_tile(M):
        return np.ascontiguousarray(M.reshape(32, 128, 32, 128).transpose(2, 1, 0, 3))
    def inv_tile(M):
        return np.ascontiguousarray(M.reshape(32, 128, 32, 128).transpose(0, 3, 2, 1))
    _CONST["CFt"] = fwd_tile(Cm); _CONST["SFt"] = fwd_tile(Sm); _CONST["CIt"] = inv_tile(Cm); _CONST["SIt"] = inv_tile(Sm)
    return _CONST


def _prep_core2(inputs, b, m, rev=False):
    f = np.float32
    g = lambda k: np.asarray(inputs[k])
    m.update(_consts())
    mb0 = np.ones((128, 2), f); mb0[0, 0 if rev else 1] = 0.0
    m["mb0"] = mb0
    m["f_w1"] = np.ascontiguousarray(g("hy_f_w1")[0], dtype=f); m["f_w2"] = np.ascontiguousarray(g("hy_f_w2")[0], dtype=f)
    w3 = g("hy_f_w3")[0]
    if rev:
        w3 = w3.reshape(64, 2, 2, 1024)[:, :, ::-1, :].reshape(64, 4096)
    m["f_w3"] = np.ascontiguousarray(w3, dtype=f)
    m["f_b1"] = np.ascontiguousarray(g("hy_f_b1")[0].reshape(64, 1), dtype=f); m["f_b2"] = np.ascontiguousarray(g("hy_f_b2")[0].reshape(64, 1), dtype=f)
    m["f_freq"] = np.ascontiguousarray(g("hy_sin_freq")[0].reshape(64, 1), dtype=f)
    m["hy_bias"] = np.ascontiguousarray(g("hy_bias")[0], dtype=f)
    a = np.arange(128)
    tri = np.stack([a[:, None] <= a[None, :], a[:, None] >= a[None, :], a[:, None] > a[None, :], a[:, None] < a[None, :]], 1).astype(f)
    m["tri"] = np.ascontiguousarray(tri)
    al = g("gdn_a_log")[0]; db = g("gdn_dt_bias")[0]
    if rev:
        al = al[::-1]; db = db[::-1]
    m["a_log"] = np.ascontiguousarray(al.reshape(1, 16), dtype=f)
    m["dt_bias"] = np.ascontiguousarray(db.reshape(1, 16), dtype=f)
    m["gdn_ng"] = np.ascontiguousarray(g("gdn_norm_g")[0].reshape(128, 1), dtype=f)
    return m


_PROG = {}

def kernel(**inputs):
    if "nc" not in _PROG:
        nc = bass.Bass("TRN2", target_bir_lowering=False)
        nc, C = build_program(nc, dbg=False)
        _PROG["nc"] = nc; _PROG["din"] = set(C.din)
    nc = _PROG["nc"]
    B = np.asarray(inputs["x"]).shape[0]
    in_maps = []
    percore = {}
    shared = {}
    for r in range(8):
        b, rev = r // 2, bool(r % 2)
        m = _prep_core2(inputs, b, _prep_core(inputs, b, rev), rev)
        m = {k: v for k, v in m.items() if k in _PROG["din"]}
        if not rev:
            for k in ("w_mod", "ffn1_wgu", "ffn1_wd", "ffn2_wgu", "ffn2_wd", "hy_out", "gdn_out", "w_o"):
                m[k] = shared.setdefault(k, m[k])
        else:
            for k in shared:
                m[k] = shared[k]
        in_maps.append(m)
    res = run_bass_kernel_spmd(nc, in_maps, core_ids=list(range(8)))
    out = np.empty((B, L, D), np.float32)
    for b in range(B):
        out[b, :OWN] = res.results[2 * b]["outT"].T
        out[b, OWN:] = res.results[2 * b + 1]["outT"].T[::-1]
    return out
```

```python
import math
import contextlib
import numpy as np
import concourse.bass as bass
import concourse.mybir as mybir
from concourse.bass_utils import run_bass_kernel_spmd

F32 = mybir.dt.float32
BF16 = mybir.dt.bfloat16
AF = mybir.ActivationFunctionType
ALU = mybir.AluOpType
AX = mybir.AxisListType

EPOCH = 4096
ENGS = ("tensor", "vector", "scalar", "gpsimd", "sync")
N_DMA_SEMS = 12


class Op:
    __slots__ = ("eng", "fn", "deps", "idx", "needs_inc", "dma", "dma_tok", "dma_prev", "inc_no")

    def __init__(self, eng, fn):
        self.eng = eng
        self.fn = fn
        self.deps = []
        self.idx = None
        self.needs_inc = False
        self.dma = False
        self.dma_tok = None
        self.dma_prev = None


class Sched:
    def __init__(self, nc):
        self.nc = nc
        self.ops = {e: [] for e in ENGS}
        self.last_w = {}
        self.readers = {}
        self.dma_rr = {e: 0 for e in ENGS}
        self.dma_cnt = {}
        self.dma_last = {}
        self.n_ops = 0
        self.bar_ops = []
        self.bar_gen = 0
        self.bar_applied = {e: 0 for e in ENGS}

    def barrier(self):
        self.bar_gen += 1
        b = []
        for e in ENGS:
            for op in reversed(self.ops[e]):
                if not op.dma:
                    b.append(op)
                    break
        b.extend(self.dma_last.values())
        self.bar_ops = b

    def _add(self, eng, fn, reads, writes, dma=False, pe_acc=False):
        op = Op(eng, fn)
        op.dma = dma
        deps = []
        for k in reads:
            w = self.last_w.get(k)
            if w is not None:
                deps.append(w)
        for k in writes:
            w = self.last_w.get(k)
            if w is not None:
                deps.append(w)
            for r in self.readers.get(k, ()):
                deps.append(r)
        if self.bar_applied[eng] != self.bar_gen:
            deps.extend(self.bar_ops)
            self.bar_applied[eng] = self.bar_gen
        seen = set()
        for d in deps:
            if d is op or id(d) in seen:
                continue
            seen.add(id(d))
            if (not d.dma) and (not dma) and d.eng == "tensor" and eng == "tensor":
                continue
            op.deps.append(d)
        for k in reads:
            self.readers.setdefault(k, []).append(op)
        for k in writes:
            self.last_w[k] = op
            self.readers[k] = []
        op.idx = len(self.ops[eng])
        self.ops[eng].append(op)
        if dma:
            slot = self.dma_rr[eng] % N_DMA_SEMS
            self.dma_rr[eng] += 1
            key = (eng, slot)
            cnt = self.dma_cnt.get(key, 0) + 1
            self.dma_cnt[key] = cnt
            op.dma_tok = (key, cnt * 16)
            op.dma_prev = self.dma_last.get(key)
            self.dma_last[key] = op
        self.n_ops += 1
        return op

    def pe(self, fn, reads=(), writes=()):
        return self._add("tensor", fn, reads, writes)

    def dve(self, fn, reads=(), writes=()):
        return self._add("vector", fn, reads, writes)

    def act(self, fn, reads=(), writes=()):
        return self._add("scalar", fn, reads, writes)

    def pool(self, fn, reads=(), writes=()):
        return self._add("gpsimd", fn, reads, writes)

    def dma(self, out, in_, reads=(), writes=(), eng="sync"):
        return self._add(eng, lambda e: e.dma_start(out=out, in_=in_), reads, writes, dma=True)

    def dma_fn(self, fn, reads=(), writes=(), eng="gpsimd"):
        return self._add(eng, fn, reads, writes, dma=True)

    def dma_cast(self, out, in_, reads=(), writes=()):
        return self.dma(out, in_, reads, writes, eng="gpsimd")

    def emit(self, final_keys=()):
        nc = self.nc
        fin = Op("sync", None)
        for k in final_keys:
            w = self.last_w.get(k)
            if w is not None:
                fin.deps.append(w)
        for e in ENGS:
            for op in self.ops[e]:
                for d in op.deps:
                    if not d.dma:
                        d.needs_inc = True
        for d in fin.deps:
            if not d.dma:
                d.needs_inc = True
        n_inc = {}
        for e in ENGS:
            c = 0
            for op in self.ops[e]:
                if (not op.dma) and op.needs_inc:
                    op.inc_no = c
                    c += 1
            n_inc[e] = c
        import contextlib
        stack = contextlib.ExitStack()
        sems = {}
        with stack:
            for e in ENGS:
                n = n_inc[e]
                for ep in range((n + EPOCH - 1) // EPOCH + 1):
                    sems[(e, ep)] = stack.enter_context(nc.semaphore(f"p_{e}_{ep}"))
            for key in self.dma_cnt:
                sems[("dma",) + key] = stack.enter_context(nc.semaphore(f"d_{key[0]}_{key[1]}"))
            block = stack.enter_context(nc.Block())

            def tok(d):
                if d.dma:
                    return (sems[("dma",) + d.dma_tok[0]], d.dma_tok[1], ("dma",) + d.dma_tok[0])
                ep, r = divmod(d.inc_no, EPOCH)
                return (sems[(d.eng, ep)], r + 1, (d.eng, ep))

            def run(engname):
                def body(eng):
                    waited = {}
                    def do_wait(d):
                        s, v, key = tok(d)
                        if waited.get(key, 0) >= v:
                            return
                        waited[key] = v
                        eng.wait_ge(s, v)
                    for op in self.ops[engname]:
                        for d in op.deps:
                            if (not d.dma) and d.eng == engname and d.idx >= op.idx:
                                continue
                            do_wait(d)
                        if op.dma and op.dma_prev is not None:
                            do_wait(op.dma_prev)
                        ins = op.fn(eng)
                        if op.dma:
                            s, v, _ = tok(op)
                            ins.then_inc(s, 16)
                        elif op.needs_inc:
                            ep, _r = divmod(op.inc_no, EPOCH)
                            ins.then_inc(sems[(engname, ep)], 1)
                    if engname == "sync":
                        for d in fin.deps:
                            do_wait(d)
                return body

            block.tensor(run("tensor"))
            block.vector(run("vector"))
            block.scalar(run("scalar"))
            block.gpsimd(run("gpsimd"))
            block.sync(run("sync"))

D = 2048; KC = 16; DFF = 5632; HC = 44; L = 4096; LC = 256; NT = 512
HYW = 1024; GW = 1024; NH = 8
HY_COLS = 3072; Q_OFF = 3072; KV_OFF = 4096; Z_OFF = 6144; AB_OFF = 7168; GATE_OFF = 7200; IN_COLS = 11296
EPS = 1e-6
LTOT = L + LC
OWN = L // 2


class Ctx:
    pass


def build_program(nc, dbg=False, stages=("A", "F", "H", "G", "T"), ext=()):
    S = Sched(nc)
    C = Ctx()
    C.nc = nc; C.S = S
    din = {}
    def inp(name, shape, dt=F32):
        din[name] = nc.dram_tensor(name, list(shape), dt, kind="ExternalInput").ap()
        return din[name]
    def scratch(name, shape, dt=F32):
        if name in ext:
            return inp(name, shape, dt)
        return nc.dram_tensor(name, list(shape), dt, kind="Internal").ap()
    xT = inp("xT", [D, L]); cT = inp("ctxT", [D, LC]); sT = inp("sT", [128, KC, 2])
    w_mod = inp("w_mod", [D, 9 * D]); b_mod = inp("b_mod", [128, 9 * KC])
    norm_g = inp("norm_g", [128, 3 * KC]); fin_g = inp("fin_g", [128, KC])
    wgu = [inp("ffn1_wgu", [D, 2 * DFF]), inp("ffn2_wgu", [D, 2 * DFF])]
    wd = [inp("ffn1_wd", [DFF, D]), inp("ffn2_wd", [DFF, D])]
    w_in = inp("w_in", [D, IN_COLS])
    hy_cw = inp("hy_cw", [128, 24, 3]); hy_cb = inp("hy_cb", [128, 24]); g_cw = inp("g_cw", [128, 24, 3])
    hy_out = inp("hy_out", [HYW, D]); gdn_out = inp("gdn_out", [GW, D]); w_o = inp("w_o", [D, D])
    zT_d = inp("zT", [33, L]); fw1 = inp("f_w1", [33, 64]); fw2 = inp("f_w2", [64, 64]); fw3 = inp("f_w3", [64, 4096])
    fb1 = inp("f_b1", [64, 1]); fb2 = inp("f_b2", [64, 1]); ffr = inp("f_freq", [64, 1])
    deltas_d = inp("deltas", [1, 1024]); that_d = inp("that", [128, 32]); mb0_d = inp("mb0", [128, 2])
    CFt = inp("CFt", [32, 128, 32, 128], BF16); SFt = inp("SFt", [32, 128, 32, 128], BF16)
    CIt = inp("CIt", [32, 128, 32, 128], BF16); SIt = inp("SIt", [32, 128, 32, 128], BF16)
    hyb_d = inp("hy_bias", [2, 1024])
    tri_d = inp("tri", [128, 4, 128]); alog_d = inp("a_log", [1, 16]); dtb_d = inp("dt_bias", [1, 16]); gng_d = inp("gdn_ng", [128, 1])
    out_d = nc.dram_tensor("outT", [D, OWN], F32, kind="ExternalOutput").ap()
    C.din = din
    H1T = scratch("H1T", [D, L]); HYT = scratch("HYT", [L, HY_COLS], BF16)
    QT = scratch("QT", [GW, LTOT]); KT = scratch("KT", [GW, LTOT]); VT = scratch("VT", [GW, LTOT])
    ZG = scratch("ZG", [GW, L], BF16); ABs = scratch("ABs", [LTOT, 32])
    ZH = scratch("ZH", [HYW, L], BF16); OG = scratch("OG", [GW, L], BF16)
    OFs = scratch("OFs", [L, 8, 128])
    KCs = scratch("KCs", [L, 2048]); KSs = scratch("KSs", [L, 2048])
    Z2T = scratch("Z2T", [L, 1024], BF16); Ps = scratch("Ps", [L, 1024], BF16); Qs = scratch("Qs", [L, 1024], BF16)
    C.scr = dict(KCs=KCs, KSs=KSs, Z2T=Z2T, H1T=H1T, HYT=HYT, QT=QT, KT=KT, VT=VT, ZG=ZG, ABs=ABs, ZH=ZH, OG=OG)
    dbg_outs = {}
    C.dbg_outs = dbg_outs
    Wf = dict(ffn1_wgu=wgu[0], ffn1_wd=wd[0], w_in=w_in, ffn2_wgu=wgu[1], ffn2_wd=wd[1], hy_out=hy_out, gdn_out=gdn_out, w_o=w_o)
    Wb = {k: nc.dram_tensor(k + "_b16", list(v.shape), BF16, kind="Internal").ap() for k, v in Wf.items()}
    def convert_w(name):
        src = Wf[name]; dst = Wb[name]
        R_ = src.shape[0]
        step = 512 if R_ % 512 == 0 else 128
        for r0 in range(0, R_, step):
            S.dma_cast(dst[r0:r0 + step, :].rearrange("(p a) n -> p (a n)", p=128), src[r0:r0 + step, :].rearrange("(p a) n -> p (a n)", p=128), writes=["Wb_" + name])
    wgu_b = [Wb["ffn1_wgu"], Wb["ffn2_wgu"]]; wd_b = [Wb["ffn1_wd"], Wb["ffn2_wd"]]

    stack = contextlib.ExitStack()
    with stack:
        def sb(name, shape, dt=F32):
            return stack.enter_context(nc.sbuf_tensor(name, list(shape), dt)).ap()
        PS = [nc.alloc_psum_tensor(f"psb{i}", [128, 512], F32).ap() for i in range(8)]
        C.PS = PS
        C.ps_rr = 0
        def next_ps():
            i = C.ps_rr % 6
            C.ps_rr += 1
            return PS[i], f"PS{i}"
        C.next_ps = next_ps
        ident = sb("ident", [128, 128]); identb = sb("identb", [128, 128], BF16)
        onesb = sb("onesb", [128, 128], BF16); onesf = sb("onesf", [128, 128])
        S.pool(lambda e: e.memset(ident, 1.0), writes=["ident"])
        S.pool(lambda e: e.affine_select(ident, ident, pattern=[[-1, 128]], compare_op=ALU.is_equal, fill=0.0, base=0, channel_multiplier=1), reads=["ident"], writes=["ident"])
        S.dve(lambda e: e.tensor_copy(identb, ident), reads=["ident"], writes=["identb"])
        S.pool(lambda e: e.memset(onesf, 1.0), writes=["onesf"])
        S.dve(lambda e: e.memset(onesb, 1.0), writes=["onesb"])
        C.ident = ident; C.identb = identb; C.onesb = onesb; C.onesf = onesf
        modv = sb("modv", [128, 9 * KC, 2]); sTs = sb("sTs", [128, KC, 2]); sTb = sb("sTb", [128, KC, 2], BF16)
        bmod = sb("bmod", [128, 9 * KC]); ng = sb("ng", [128, 3 * KC]); fg = sb("fg", [128, KC])
        Av = sb("Av", [128, 3, KC, 2]); Bv = sb("Bv", [128, 3, KC, 2]); Gv = sb("Gv", [128, 3, KC, 2])
        S.dma(sTs, sT, writes=["sTs"]); S.dma(bmod, b_mod, writes=["bmod"]); S.dma(ng, norm_g, writes=["ng"]); S.dma(fg, fin_g, writes=["fg"])
        S.act(lambda e: e.activation(sTb, sTs, AF.Silu), reads=["sTs"], writes=["sTb"])
        C.wb_rr = 0
        def next_wb():
            i = C.wb_rr % 3
            C.wb_rr += 1
            return C.WB[i], f"WB{i}"
        C.next_wb = next_wb
        def alloc_main(st):
            def sbl(name, shape, dt=F32):
                return st.enter_context(nc.sbuf_tensor(name, list(shape), dt)).ap()
            C.gen = getattr(C, "gen", 0) + 1
            g = C.gen
            C.WB = [sbl(f"wb{i}_{g}", [128, 8192], BF16) for i in range(3)]
            C.H = sbl(f"H_{g}", [128, KC, NT]); C.XM = sbl(f"XM_{g}", [128, KC, NT], BF16); C.ACT = sbl(f"ACT_{g}", [128, HC, NT], BF16)
            C.TMP = [sbl(f"tmp{i}_{g}", [128, NT]) for i in range(6)]
            return sbl
        st_mod = contextlib.ExitStack()
        st_mod.__enter__()
        C.WB = [st_mod.enter_context(nc.sbuf_tensor(f"wbm{i}", [128, 8192], BF16)).ap() for i in range(3)]
        convert_w("ffn1_wgu")
        for cb in range(9 * D // 512):
            wbuf, wkey = next_wb()
            wv = wbuf.rearrange("p (k n) -> p k n", k=KC)
            S.dma_cast(wv, w_mod[:, cb * 512:(cb + 1) * 512].rearrange("(k p) n -> p k n", p=128), writes=[wkey])
            for j in range(4):
                ps, pk = next_ps()
                for k in range(KC):
                    S.pe(lambda e, ps=ps, wv=wv, k=k, j=j: e.matmul(ps[:, 0:2], lhsT=wv[:, k, j * 128:(j + 1) * 128], rhs=sTb[:, k, :], start=(k == 0), stop=(k == KC - 1)),
                         reads=[wkey, "sTb"], writes=[pk])
                col = cb * 4 + j
                S.dve(lambda e, ps=ps, col=col: e.tensor_scalar(modv[:, col, :], ps[:, 0:2], bmod[:, col:col + 1], None, op0=ALU.add),
                      reads=[pk, "bmod"], writes=["modv"])
        for j in range(3):
            for r in range(2):
                sh = modv[:, (3 * j) * KC:(3 * j + 1) * KC, r]; sc = modv[:, (3 * j + 1) * KC:(3 * j + 2) * KC, r]; gt = modv[:, (3 * j + 2) * KC:(3 * j + 3) * KC, r]
                S.dve(lambda e, sc=sc, j=j, r=r: e.scalar_tensor_tensor(Av[:, j, :, r], in0=sc, scalar=1.0, in1=ng[:, j * KC:(j + 1) * KC], op0=ALU.add, op1=ALU.mult),
                      reads=["modv", "ng"], writes=["Av"])
                S.dve(lambda e, sh=sh, j=j, r=r: e.tensor_copy(Bv[:, j, :, r], sh), reads=["modv"], writes=["Bv"])
                S.dve(lambda e, gt=gt, j=j, r=r: e.tensor_scalar(Gv[:, j, :, r], gt, (1.0 if j == 1 else 0.5), None, op0=ALU.mult), reads=["modv"], writes=["Gv"])
        C.Av = Av; C.Bv = Bv; C.Gv = Gv; C.fg = fg
        if dbg:
            dbg_outs["modv"] = nc.dram_tensor("dbg_modv", [128, 9 * KC, 2], F32, kind="ExternalOutput").ap()
            S.dma(dbg_outs["modv"], modv, reads=["modv"], writes=["dbg_modv"])
        for nm_ in ("ffn1_wd", "w_in", "ffn2_wgu", "ffn2_wd", "hy_out", "gdn_out", "w_o"):
            convert_w(nm_)
        st_mod.__exit__(None, None, None)
        S.barrier()

        def norm_mod(T, j, r, src_key="H"):
            H, XM, ACT, TMP = C.H, C.XM, C.ACT, C.TMP
            pss, psk = PS[6], "PS6"
            sq = ACT
            for k in range(KC):
                S.act(lambda e, k=k: e.activation(sq[:, k, :T], H[:, k, :T], AF.Square), reads=["H"], writes=["ACT"])
            for k in range(KC):
                S.pe(lambda e, k=k: e.matmul(pss[:, :T], lhsT=onesb, rhs=sq[:, k, :T], start=(k == 0), stop=(k == KC - 1)), reads=["ACT", "onesb"], writes=[psk])
            rs = TMP[5]
            S.dve(lambda e: e.tensor_scalar(rs[:, :T], pss[:, :T], 1.0 / D, EPS, op0=ALU.mult, op1=ALU.add), reads=[psk], writes=["tmp5"])
            S.act(lambda e: e.activation(rs[:, :T], rs[:, :T], AF.Ln), reads=["tmp5"], writes=["tmp5"])
            S.act(lambda e: e.activation(rs[:, :T], rs[:, :T], AF.Exp, scale=-0.5), reads=["tmp5"], writes=["tmp5"])
            for k in range(KC):
                t = TMP[k % 2]; tk = f"tmp{k % 2}"
                S.dve(lambda e, k=k, t=t: e.tensor_tensor(t[:, :T], H[:, k, :T], rs[:, :T], op=ALU.mult), reads=["H", "tmp5"], writes=[tk])
                S.act(lambda e, k=k, t=t: e.activation(XM[:, k, :T], t[:, :T], AF.Identity, bias=Bv[:, j, k, r:r + 1], scale=Av[:, j, k, r:r + 1]),
                      reads=[tk, "Av", "Bv"], writes=["XM"])
        C.norm_mod = norm_mod

        def linear(T, wname, kc, col0, ncols, evac, cw=512, xsrc=None, xkey="XM"):
            w_ap = Wb[wname]; wsrc_key = "Wb_" + wname
            xs = C.XM if xsrc is None else xsrc
            cw = min(cw, 8192 // kc)
            nblk = (ncols + cw - 1) // cw
            ci = 0
            pend = []
            for b in range(nblk):
                c0 = col0 + b * cw
                w = min(cw, col0 + ncols - c0)
                wbuf, wkey = next_wb()
                wv = wbuf[:, :kc * w].rearrange("p (k n) -> p k n", k=kc)
                S.dma(wv, w_ap[:, c0:c0 + w].rearrange("(k p) n -> p k n", p=128), reads=[wsrc_key], writes=[wkey])
                for jj in range((w + 127) // 128):
                    m = min(128, w - jj * 128)
                    ps, pk = next_ps()
                    for k in range(kc):
                        S.pe(lambda e, ps=ps, wv=wv, k=k, jj=jj, m=m: e.matmul(ps[:m, :T], lhsT=wv[:, k, jj * 128:jj * 128 + m], rhs=xs[:, k, :T], start=(k == 0), stop=(k == kc - 1)),
                             reads=[wkey, xkey], writes=[pk])
                    if pend:
                        evac(*pend.pop())
                    pend.append((ci, ps, pk))
                    ci += 1
            if pend:
                evac(*pend.pop())
        C.linear = linear

        def ffn(T, fi, j, r):
            H, XM, ACT, TMP = C.H, C.XM, C.ACT, C.TMP
            norm_mod(T, j, r)
            for hb in range(HC // 4):
                wg_b, wgk = next_wb(); wu_b, wuk = next_wb()
                wgv = wg_b.rearrange("p (k n) -> p k n", k=KC); wuv = wu_b.rearrange("p (k n) -> p k n", k=KC)
                gk_ = "Wb_ffn%d_wgu" % (fi + 1)
                S.dma(wgv, wgu_b[fi][:, hb * 512:(hb + 1) * 512].rearrange("(k p) n -> p k n", p=128), reads=[gk_], writes=[wgk])
                S.dma(wuv, wgu_b[fi][:, DFF + hb * 512:DFF + (hb + 1) * 512].rearrange("(k p) n -> p k n", p=128), reads=[gk_], writes=[wuk])
                for jj in range(4):
                    hc = hb * 4 + jj
                    psg, pgk = next_ps(); psu, puk = next_ps()
                    for k in range(KC):
                        S.pe(lambda e, psg=psg, wgv=wgv, k=k, jj=jj: e.matmul(psg[:, :T], lhsT=wgv[:, k, jj * 128:(jj + 1) * 128], rhs=XM[:, k, :T], start=(k == 0), stop=(k == KC - 1)), reads=[wgk, "XM"], writes=[pgk])
                    for k in range(KC):
                        S.pe(lambda e, psu=psu, wuv=wuv, k=k, jj=jj: e.matmul(psu[:, :T], lhsT=wuv[:, k, jj * 128:(jj + 1) * 128], rhs=XM[:, k, :T], start=(k == 0), stop=(k == KC - 1)), reads=[wuk, "XM"], writes=[puk])
                    t = TMP[2 + (hc % 2)]; tk = f"tmp{2 + (hc % 2)}"
                    S.act(lambda e, psg=psg, t=t: e.activation(t[:, :T], psg[:, :T], AF.Silu), reads=[pgk], writes=[tk])
                    S.dve(lambda e, psu=psu, t=t, hc=hc: e.tensor_tensor(ACT[:, hc, :T], t[:, :T], psu[:, :T], op=ALU.mult), reads=[tk, puk], writes=["ACT"])
            def ev(ci, ps, pk):
                S.dve(lambda e, ps=ps, ci=ci: e.scalar_tensor_tensor(H[:, ci, :T], in0=ps[:, :T], scalar=Gv[:, j, ci, r:r + 1], in1=H[:, ci, :T], op0=ALU.mult, op1=ALU.add),
                      reads=[pk, "H", "Gv"], writes=["H"])
            linear(T, "ffn%d_wd" % (fi + 1), HC, 0, D, ev, cw=128, xsrc=ACT, xkey="ACT")
        C.ffn = ffn

        if "A" in stages:
            st_a = contextlib.ExitStack(); st_a.__enter__()
            sb = alloc_main(st_a)
            H, XM, ACT, TMP = C.H, C.XM, C.ACT, C.TMP
            cwh = sb("cwh", [128, 24, 3]); cbh = sb("cbh", [128, 24]); cwg = sb("cwg", [128, 24, 3]); zb = sb("zb", [128, 24])
            S.dma(cwh, hy_cw, writes=["cwh"]); S.dma(cbh, hy_cb, writes=["cbh"]); S.dma(cwg, g_cw, writes=["cwg"])
            S.pool(lambda e: e.memset(zb, 0.0), writes=["zb"])
            wab = sb("wab", [128, KC, 32], BF16)
            S.dma(wab, Wb["w_in"][:, AB_OFF:AB_OFF + 32].rearrange("(k p) n -> p k n", p=128), reads=["Wb_w_in"], writes=["wab"])
            STG = sb("STG", [128, 4, 1024], BF16)
            tiles = [(0, t * NT, NT, 0) for t in range(L // NT)] + [(1, 0, LC, 1)]
            if dbg == "short":
                tiles = [tiles[0], tiles[-1]]
            def do_tile(isctx, t0, T, r):
                src = cT if isctx else xT
                S.dma(H[:, :, :T], src[:, t0:t0 + T].rearrange("(k p) t -> p k t", p=128), writes=["H"])
                ffn(T, 0, 0, r)
                own = (not isctx) and (t0 < OWN)
                if own:
                    S.dma(H1T[:, t0:t0 + T].rearrange("(k p) t -> p k t", p=128), H[:, :, :T], reads=["H"], writes=["H1T"], eng="gpsimd")
                norm_mod(T, 1, r)
                seg = 64 if not isctx else LC
                nseg = T // seg
                tcol = (L + t0) if isctx else t0

                def conv(ci, ps, pk, cw_t, cb_t, widx, outk):
                    pr = TMP[0]; y = TMP[1]
                    S.act(lambda e, ps=ps: e.copy(pr[:, :T], ps[:, :T]), reads=[pk], writes=["tmp0"])
                    S.dve(lambda e: e.tensor_scalar(y[:, :T], pr[:, :T], cw_t[:, widx, 1:2], cb_t[:, widx:widx + 1], op0=ALU.mult, op1=ALU.add), reads=["tmp0"], writes=["tmp1"])
                    prv = pr[:, :T].rearrange("p (s n) -> p s n", n=seg); yv = y[:, :T].rearrange("p (s n) -> p s n", n=seg)
                    S.dve(lambda e: e.scalar_tensor_tensor(yv[:, :, 1:], in0=prv[:, :, :seg - 1], scalar=cw_t[:, widx, 0:1], in1=yv[:, :, 1:], op0=ALU.mult, op1=ALU.add), reads=["tmp0", "tmp1"], writes=["tmp1"])
                    S.dve(lambda e: e.scalar_tensor_tensor(yv[:, :, :seg - 1], in0=prv[:, :, 1:], scalar=cw_t[:, widx, 2:3], in1=yv[:, :, :seg - 1], op0=ALU.mult, op1=ALU.add), reads=["tmp0", "tmp1"], writes=["tmp1"])
                    return y

                if not isctx:
                    def ev_hy(ci, ps, pk):
                        y = conv(ci, ps, pk, cwh, cbh, ci, None)
                        yb = TMP[2].bitcast(BF16)
                        S.act(lambda e: e.copy(yb[:, :T], y[:, :T]), reads=["tmp1"], writes=["tmp2"])
                        pt = PS[7].bitcast(BF16)
                        for bl in range(T // 128):
                            S.pe(lambda e, bl=bl: e.transpose(pt[:, bl * 128:(bl + 1) * 128], yb[:, bl * 128:(bl + 1) * 128], identb), reads=["tmp2", "identb"], writes=["PS7"])
                        c8 = ci % 8
                        S.dve(lambda e, c8=c8: e.tensor_copy(STG[:, :T // 128, c8 * 128:(c8 + 1) * 128], pt[:, :T].rearrange("p (b c) -> p b c", c=128)), reads=["PS7"], writes=["STG"])
                        if c8 == 7:
                            g8 = ci // 8
                            S.dma(HYT[t0:t0 + T, g8 * 1024:(g8 + 1) * 1024].rearrange("(b p) c -> p b c", p=128), STG[:, :T // 128, :], reads=["STG"], writes=["HYT"], eng="gpsimd")
                    linear(T, "w_in", KC, 0, HY_COLS, ev_hy)
                def ev_qkv(ci, ps, pk, base):
                    gi = base + ci
                    y = conv(ci, ps, pk, cwg, zb, gi, None)
                    a = TMP[2]
                    S.act(lambda e: e.activation(a[:, :T], y[:, :T], AF.Silu), reads=["tmp1"], writes=["tmp2"])
                    which = gi // 8; h = gi % 8
                    dst = (QT, KT, VT)[which]
                    if which < 2:
                        sq = TMP[3]
                        S.dve(lambda e: e.tensor_tensor(sq[:, :T], a[:, :T], a[:, :T], op=ALU.mult), reads=["tmp2"], writes=["tmp3"])
                        pss = PS[6]
                        S.pe(lambda e: e.matmul(pss[:, :T], lhsT=onesf, rhs=sq[:, :T], start=True, stop=True), reads=["tmp3", "onesf"], writes=["PS6"])
                        rn = TMP[4]
                        S.dve(lambda e: e.tensor_scalar(rn[:, :T], pss[:, :T], 1e-6, None, op0=ALU.add), reads=["PS6"], writes=["tmp4"])
                        S.act(lambda e: e.activation(rn[:, :T], rn[:, :T], AF.Ln), reads=["tmp4"], writes=["tmp4"])
                        S.act(lambda e: e.activation(rn[:, :T], rn[:, :T], AF.Exp, scale=-0.5), reads=["tmp4"], writes=["tmp4"])
                        scl = (128.0 ** -0.5) if which == 0 else 1.0
                        S.dve(lambda e: e.scalar_tensor_tensor(sq[:, :T], in0=a[:, :T], scalar=scl, in1=rn[:, :T], op0=ALU.mult, op1=ALU.mult), reads=["tmp2", "tmp4"], writes=["tmp3"])
                        res = sq; ak = "tmp3"
                    else:
                        res = a; ak = "tmp2"
                    S.dma(dst[h * 128:(h + 1) * 128, tcol:tcol + T], res[:, :T], reads=[ak], writes=[("QT", "KT", "VT")[which]], eng="gpsimd")
                if own:
                    linear(T, "w_in", KC, Q_OFF, 1024, lambda ci, ps, pk: ev_qkv(ci, ps, pk, 0))
                linear(T, "w_in", KC, KV_OFF, 2048, lambda ci, ps, pk: ev_qkv(ci, ps, pk, 8))
                if own:
                    def ev_z(ci, ps, pk):
                        zt = TMP[2].bitcast(BF16)
                        S.act(lambda e, ps=ps: e.activation(zt[:, :T], ps[:, :T], AF.Silu), reads=[pk], writes=["tmp2"])
                        S.dma(ZG[ci * 128:(ci + 1) * 128, t0:t0 + T], zt[:, :T], reads=["tmp2"], writes=["ZG"], eng="gpsimd")
                    linear(T, "w_in", KC, Z_OFF, 1024, ev_z)
                for bl in range(T // 128):
                    ps, pk = next_ps()
                    for k in range(KC):
                        S.pe(lambda e, ps=ps, k=k, bl=bl: e.matmul(ps[:, :32], lhsT=XM[:, k, bl * 128:(bl + 1) * 128], rhs=wab[:, k, :], start=(k == 0), stop=(k == KC - 1)), reads=["XM", "wab"], writes=[pk])
                    a = TMP[3]
                    S.dve(lambda e, ps=ps: e.tensor_copy(a[:, :32], ps[:, :32]), reads=[pk], writes=["tmp3"])
                    S.dma(ABs[tcol + bl * 128:tcol + (bl + 1) * 128, :], a[:, :32], reads=["tmp3"], writes=["ABs"], eng="gpsimd")
            for tl in tiles:
                do_tile(*tl)
            st_a.__exit__(None, None, None)
            S.barrier()


        if "F" in stages:
            st_f = contextlib.ExitStack(); st_f.__enter__()
            def sbf(name, shape, dt=F32):
                return st_f.enter_context(nc.sbuf_tensor("sF_" + name, list(shape), dt)).ap()
            TWO_PI = 2.0 * math.pi
            w3s = sbf("w3s", [64, 4096]); h2T = sbf("h2T", [64, 4096]); deltab = sbf("deltab", [128, 1024]); thn = sbf("thn", [128, 32])
            mb0 = sbf("mb0", [128, 2]); negpi = sbf("negpi", [128, 1]); frs = sbf("frs", [64, 1]); s1 = sbf("s1", [64, 1]); s2a = sbf("s2a", [64, 1]); s2b = sbf("s2b", [64, 1])
            b1s = sbf("b1s", [64, 1]); b2s = sbf("b2s", [64, 1])
            S.dma(w3s, fw3, writes=["w3s"]); S.dma(deltab, deltas_d.partition_broadcast(128), writes=["deltab"]); S.dma(thn, that_d, writes=["thn"])
            S.dma(mb0, mb0_d, writes=["mb0"]); S.dma(frs, ffr, writes=["frs"]); S.dma(b1s, fb1, writes=["b1s"]); S.dma(b2s, fb2, writes=["b2s"])
            S.pool(lambda e: e.memset(negpi, -math.pi), writes=["negpi"])
            S.dve(lambda e: e.tensor_scalar(thn, thn, -1.0, None, op0=ALU.mult), reads=["thn"], writes=["thn"])
            S.dve(lambda e: e.tensor_scalar(s1, frs, 1.0 / TWO_PI, None, op0=ALU.mult), reads=["frs"], writes=["s1"])
            S.dve(lambda e: e.tensor_tensor(s2a, frs, b1s, op=ALU.mult), reads=["frs", "b1s"], writes=["s2a"])
            S.dve(lambda e: e.tensor_scalar(s2a, s2a, 1.0 / TWO_PI, 16.5, op0=ALU.mult, op1=ALU.add), reads=["s2a"], writes=["s2a"])
            S.dve(lambda e: e.tensor_tensor(s2b, frs, b2s, op=ALU.mult), reads=["frs", "b2s"], writes=["s2b"])
            S.dve(lambda e: e.tensor_scalar(s2b, s2b, 1.0 / TWO_PI, 16.5, op0=ALU.mult, op1=ALU.add), reads=["s2b"], writes=["s2b"])
            st_f1 = contextlib.ExitStack(); st_f1.__enter__()
            def sbf1(name, shape, dt=F32):
                return st_f1.enter_context(nc.sbuf_tensor("sF1_" + name, list(shape), dt)).ap()
            zTs = sbf1("zTs", [33, 4096]); w1s = sbf1("w1s", [33, 64]); w2s = sbf1("w2s", [64, 64]); h1T = sbf1("h1T", [64, 4096])
            yt = sbf1("yt", [64, 512]); kit = sbf1("kit", [64, 512], mybir.dt.int32); kft = sbf1("kft", [64, 512])
            S.dma(zTs, zT_d, writes=["zTs"]); S.dma(w1s, fw1, writes=["w1s"]); S.dma(w2s, fw2, writes=["w2s"])
            def sin_layer(srcT, skey, kdim, wS, wkey, s2, s2key, dstT, dkey):
                for tt in range(8):
                    ps, pk = next_ps()
                    S.pe(lambda e, ps=ps, tt=tt: e.matmul(ps[:64, :512], lhsT=wS[:kdim, :], rhs=srcT[:kdim, tt * 512:(tt + 1) * 512], start=True, stop=True), reads=[skey, wkey], writes=[pk])
                    S.dve(lambda e, ps=ps: e.tensor_scalar(yt, ps[:64, :512], s1[:, 0:1], s2[:, 0:1], op0=ALU.mult, op1=ALU.add), reads=[pk, "s1", s2key], writes=["yt"])
                    S.dve(lambda e: e.tensor_copy(kit, yt), reads=["yt"], writes=["kit"])
                    S.dve(lambda e: e.tensor_copy(kft, kit), reads=["kit"], writes=["kft"])
                    S.dve(lambda e: e.tensor_tensor(yt, yt, kft, op=ALU.subtract), reads=["yt", "kft"], writes=["yt"])
                    S.dve(lambda e: e.tensor_single_scalar(kft, yt, 0.0, op=ALU.is_lt), reads=["yt"], writes=["kft"])
                    S.dve(lambda e: e.tensor_tensor(yt, yt, kft, op=ALU.add), reads=["yt", "kft"], writes=["yt"])
                    S.act(lambda e, tt=tt: e.activation(dstT[:, tt * 512:(tt + 1) * 512], yt, AF.Sin, bias=negpi[:64, 0:1], scale=TWO_PI), reads=["yt", "negpi"], writes=[dkey])
            sin_layer(zTs, "zTs", 33, w1s, "w1s", s2a, "s2a", h1T, "h1T")
            sin_layer(h1T, "h1T", 64, w2s, "w2s", s2b, "s2b", h2T, "h2T")
            st_f1.__exit__(None, None, None)
            HS = sbf("HS", [128, 32, 512], BF16); HD = sbf("HD", [128, 32, 512], BF16)
            CB = [sbf(f"CBf{i}", [128, 32, 128], BF16) for i in range(2)]; SBk = [sbf(f"SBf{i}", [128, 32, 128], BF16) for i in range(2)]
            FT = [[sbf(f"ft{i}_{j}", [128, 512]) for j in range(6)] for i in range(2)]
            RN = sbf("RN", [128, 512])
            for o in range(2):
                for hh in range(2):
                    colf = o * 2048 + hh * 512; colb = o * 2048 + 1024 + hh * 512
                    for tc in range(32):
                        tw, thf, thb, ta1, ta2, thm = FT[tc % 2]; fk = [f"ft{tc % 2}_{j}" for j in range(6)]
                        S.act(lambda e, tw=tw, tc=tc, hh=hh: e.activation(tw, deltab[:, hh * 512:(hh + 1) * 512], AF.Exp, scale=thn[:, tc:tc + 1]), reads=["deltab", "thn"], writes=[fk[0]])
                        psf, pkf = next_ps(); psb, pkb = next_ps()
                        S.pe(lambda e, psf=psf, tc=tc, colf=colf: e.matmul(psf, lhsT=h2T[:, tc * 128:(tc + 1) * 128], rhs=w3s[:, colf:colf + 512], start=True, stop=True), reads=["h2T", "w3s"], writes=[pkf])
                        S.pe(lambda e, psb=psb, tc=tc, colb=colb: e.matmul(psb, lhsT=h2T[:, tc * 128:(tc + 1) * 128], rhs=w3s[:, colb:colb + 512], start=True, stop=True), reads=["h2T", "w3s"], writes=[pkb])
                        S.dve(lambda e, psf=psf, thf=thf, tw=tw: e.tensor_tensor(thf, psf, tw, op=ALU.mult), reads=[pkf, fk[0]], writes=[fk[1]])
                        S.dve(lambda e, psb=psb, thb=thb, tw=tw: e.tensor_tensor(thb, psb, tw, op=ALU.mult), reads=[pkb, fk[0]], writes=[fk[2]])
                        S.act(lambda e, ta1=ta1, thf=thf: e.activation(ta1, thf, AF.Abs), reads=[fk[1]], writes=[fk[3]])
                        S.act(lambda e, ta2=ta2, thb=thb: e.activation(ta2, thb, AF.Abs), reads=[fk[2]], writes=[fk[4]])
                        S.pool(lambda e, ta1=ta1, ta2=ta2: e.tensor_tensor(ta1, ta1, ta2, op=ALU.add), reads=[fk[3], fk[4]], writes=[fk[3]])
                        S.pe(lambda e, ta1=ta1, tc=tc: e.matmul(PS[6], lhsT=onesf, rhs=ta1, start=(tc == 0), stop=(tc == 31)), reads=[fk[3], "onesf"], writes=["PS6"])
                        if tc == 0:
                            S.dve(lambda e, thm=thm, thb=thb: e.tensor_scalar(thm, thb, mb0[:, 1:2], None, op0=ALU.mult), reads=[fk[2], "mb0"], writes=[fk[5]])
                            S.dve(lambda e, tw=tw, thf=thf: e.tensor_scalar(tw, thf, mb0[:, 0:1], None, op0=ALU.mult), reads=[fk[1], "mb0"], writes=[fk[0]])
                            S.pool(lambda e, tw=tw, thm=thm, tc=tc: e.tensor_tensor(HS[:, tc, :], tw, thm, op=ALU.add), reads=[fk[0], fk[5]], writes=["HS"])
                        else:
                            S.pool(lambda e, thf=thf, thb=thb, tc=tc: e.tensor_tensor(HS[:, tc, :], thf, thb, op=ALU.add), reads=[fk[1], fk[2]], writes=["HS"])
                        S.dve(lambda e, thf=thf, thb=thb, tc=tc: e.tensor_tensor(HD[:, tc, :], thf, thb, op=ALU.subtract), reads=[fk[1], fk[2]], writes=["HD"])
                    S.dve(lambda e: e.reciprocal(RN, PS[6]), reads=["PS6"], writes=["RN"])
                    S.dve(lambda e: e.tensor_scalar(RN, RN, 2.0 / 8192.0, None, op0=ALU.mult), reads=["RN"], writes=["RN"])
                    S.dma(CB[0], CFt[0], writes=["CB0"]); S.dma(SBk[0], SFt[0], writes=["SBk0"])
                    for fc in range(32):
                        cb = CB[fc % 2]; sk = SBk[fc % 2]; cbk = f"CB{fc % 2}"; skk = f"SBk{fc % 2}"
                        if fc + 1 < 32:
                            S.dma(CB[(fc + 1) % 2], CFt[fc + 1], writes=[f"CB{(fc + 1) % 2}"]); S.dma(SBk[(fc + 1) % 2], SFt[fc + 1], writes=[f"SBk{(fc + 1) % 2}"])
                        psC, pkC = next_ps(); psS, pkS = next_ps()
                        for tc in range(32):
                            S.pe(lambda e, psC=psC, cb=cb, tc=tc: e.matmul(psC, lhsT=cb[:, tc, :], rhs=HS[:, tc, :], start=(tc == 0), stop=(tc == 31)), reads=[cbk, "HS"], writes=[pkC])
                        for tc in range(32):
                            S.pe(lambda e, psS=psS, sk=sk, tc=tc: e.matmul(psS, lhsT=sk[:, tc, :], rhs=HD[:, tc, :], start=(tc == 0), stop=(tc == 31)), reads=[skk, "HD"], writes=[pkS])
                        ta, tb = FT[fc % 2][0], FT[fc % 2][1]; tak, tbk = f"ft{fc % 2}_0", f"ft{fc % 2}_1"
                        S.dve(lambda e, psC=psC, ta=ta: e.tensor_tensor(ta, psC, RN, op=ALU.mult), reads=[pkC, "RN"], writes=[tak])
                        S.dve(lambda e, psS=psS, tb=tb: e.tensor_tensor(tb, psS, RN, op=ALU.mult), reads=[pkS, "RN"], writes=[tbk])
                        cc = o * 1024 + hh * 512
                        S.dma(KCs[fc * 128:(fc + 1) * 128, cc:cc + 512], ta, reads=[tak], writes=["KCs"], eng="gpsimd")
                        S.dma(KSs[fc * 128:(fc + 1) * 128, cc:cc + 512], tb, reads=[tbk], writes=["KSs"], eng="gpsimd")
            st_f.__exit__(None, None, None)
            S.barrier()

        if "H" in stages:
            st_h = contextlib.ExitStack(); st_h.__enter__()
            def sbh(name, shape, dt=F32):
                return st_h.enter_context(nc.sbuf_tensor("sH_" + name, list(shape), dt)).ap()
            ZIN = sbh("ZIN", [128, 32, 1024], BF16)
            CBh = [sbh(f"CB{i}", [128, 32, 128], BF16) for i in range(2)]; SBh = [sbh(f"SB{i}", [128, 32, 128], BF16) for i in range(2)]
            HT = [[sbh(f"ht{i}_{j}", [128, 512]) for j in range(8)] for i in range(2)]
            PQ = [[sbh(f"pq{i}_{j}", [128, 512], BF16) for j in range(4)] for i in range(2)]
            biasb = sbh("biasb", [128, 2, 1024]); ZST = sbh("ZST", [128, 4, 512], BF16)
            for o in range(2):
                S.dma(biasb[:, o, :], hyb_d[o:o + 1, :].partition_broadcast(128), writes=["biasb"])
            for o in range(2):
                src = HYT[:, 0:1024] if o == 0 else Z2T
                skey = "HYT" if o == 0 else "Z2T"
                gsrc = HYT[:, 1024:2048] if o == 0 else HYT[:, 2048:3072]
                for q4 in range(4):
                    S.dma(ZIN[:, q4 * 8:(q4 + 1) * 8, :], src[q4 * 1024:(q4 + 1) * 1024, :].rearrange("(tc p) c -> p tc c", p=128), reads=[skey], writes=["ZIN"])
                S.dma(CBh[0], CFt[0], writes=["hCB0"]); S.dma(SBh[0], SFt[0], writes=["hSB0"])
                for fc in range(32):
                    cb = CBh[fc % 2]; sk = SBh[fc % 2]; cbk = f"hCB{fc % 2}"; skk = f"hSB{fc % 2}"
                    if fc + 1 < 32:
                        S.dma(CBh[(fc + 1) % 2], CFt[fc + 1], writes=[f"hCB{(fc + 1) % 2}"]); S.dma(SBh[(fc + 1) % 2], SFt[fc + 1], writes=[f"hSB{(fc + 1) % 2}"])
                    for ct in range(2):
                        par = (fc * 2 + ct) % 2
                        ht = HT[par]; hk = [f"ht{par}_{j}" for j in range(8)]; pq = PQ[par]; pk_ = [f"pq{par}_{j}" for j in range(4)]
                        psA, pkA = next_ps(); psB, pkB = next_ps()
                        for tc in range(32):
                            S.pe(lambda e, psA=psA, cb=cb, tc=tc, ct=ct: e.matmul(psA, lhsT=cb[:, tc, :], rhs=ZIN[:, tc, ct * 512:(ct + 1) * 512], start=(tc == 0), stop=(tc == 31)), reads=[cbk, "ZIN"], writes=[pkA])
                        for tc in range(32):
                            S.pe(lambda e, psB=psB, sk=sk, tc=tc, ct=ct: e.matmul(psB, lhsT=sk[:, tc, :], rhs=ZIN[:, tc, ct * 512:(ct + 1) * 512], start=(tc == 0), stop=(tc == 31)), reads=[skk, "ZIN"], writes=[pkB])
                        kc, ks, A, B, t1, t2, t3, t4 = ht
                        cc = o * 1024 + ct * 512
                        S.dma(kc, KCs[fc * 128:(fc + 1) * 128, cc:cc + 512], reads=["KCs"], writes=[hk[0]])
                        S.dma(ks, KSs[fc * 128:(fc + 1) * 128, cc:cc + 512], reads=["KSs"], writes=[hk[1]])
                        S.act(lambda e, A=A, psA=psA: e.copy(A, psA), reads=[pkA], writes=[hk[2]])
                        S.act(lambda e, B=B, psB=psB: e.copy(B, psB), reads=[pkB], writes=[hk[3]])
                        S.dve(lambda e, t1=t1, A=A, kc=kc: e.tensor_tensor(t1, A, kc, op=ALU.mult), reads=[hk[2], hk[0]], writes=[hk[4]])
                        S.pool(lambda e, t2=t2, B=B, ks=ks: e.tensor_tensor(t2, B, ks, op=ALU.mult), reads=[hk[3], hk[1]], writes=[hk[5]])
                        S.dve(lambda e, t1=t1, t2=t2, p=pq[0]: e.tensor_tensor(p, t1, t2, op=ALU.subtract), reads=[hk[4], hk[5]], writes=[pk_[0]])
                        S.pool(lambda e, t3=t3, A=A, ks=ks: e.tensor_tensor(t3, A, ks, op=ALU.mult), reads=[hk[2], hk[1]], writes=[hk[6]])
                        S.dve(lambda e, t4=t4, B=B, kc=kc: e.tensor_tensor(t4, B, kc, op=ALU.mult), reads=[hk[3], hk[0]], writes=[hk[7]])
                        S.pool(lambda e, t3=t3, t4=t4, q=pq[1]: e.tensor_tensor(q, t3, t4, op=ALU.add), reads=[hk[6], hk[7]], writes=[pk_[1]])
                        S.dma(Ps[fc * 128:(fc + 1) * 128, ct * 512:(ct + 1) * 512], pq[0], reads=[pk_[0]], writes=["Ps"], eng="gpsimd")
                        S.dma(Qs[fc * 128:(fc + 1) * 128, ct * 512:(ct + 1) * 512], pq[1], reads=[pk_[1]], writes=["Qs"], eng="gpsimd")
                for ct in range(2):
                    PB = ZIN[:, :, 0:512]; QB = ZIN[:, :, 512:1024]
                    S.dma(PB, Ps[:, ct * 512:(ct + 1) * 512].rearrange("(fc p) c -> p fc c", p=128), reads=["Ps"], writes=["ZIN"])
                    S.dma(QB, Qs[:, ct * 512:(ct + 1) * 512].rearrange("(fc p) c -> p fc c", p=128), reads=["Qs"], writes=["ZIN"])
                    ntch = 32 if o == 0 else OWN // 128
                    S.dma(CBh[0], CIt[0], writes=["hCB0"]); S.dma(SBh[0], SIt[0], writes=["hSB0"])
                    for tch in range(ntch):
                        ci_ = CBh[tch % 2]; si_ = SBh[tch % 2]; cbk = f"hCB{tch % 2}"; skk = f"hSB{tch % 2}"
                        if tch + 1 < ntch:
                            S.dma(CBh[(tch + 1) % 2], CIt[tch + 1], writes=[f"hCB{(tch + 1) % 2}"]); S.dma(SBh[(tch + 1) % 2], SIt[tch + 1], writes=[f"hSB{(tch + 1) % 2}"])
                        par = tch % 2
                        ht = HT[par]; hk = [f"ht{par}_{j}" for j in range(8)]; pq = PQ[par]; pk_ = [f"pq{par}_{j}" for j in range(4)]
                        psY, pkY = next_ps()
                        for fc in range(32):
                            S.pe(lambda e, psY=psY, ci_=ci_, fc=fc: e.matmul(psY, lhsT=ci_[:, fc, :], rhs=PB[:, fc, :], start=(fc == 0), stop=False), reads=[cbk, "ZIN"], writes=[pkY])
                        for fc in range(32):
                            S.pe(lambda e, psY=psY, si_=si_, fc=fc: e.matmul(psY, lhsT=si_[:, fc, :], rhs=QB[:, fc, :], start=False, stop=(fc == 31)), reads=[skk, "ZIN"], writes=[pkY])
                        zin_t = pq[2]; gate_t = pq[3]; res = pq[0]
                        S.dma(zin_t, src[tch * 128:(tch + 1) * 128, ct * 512:(ct + 1) * 512], reads=[skey], writes=[pk_[2]])
                        S.dma(gate_t, gsrc[tch * 128:(tch + 1) * 128, ct * 512:(ct + 1) * 512], reads=["HYT"], writes=[pk_[3]])
                        t1, t2 = ht[4], ht[5]
                        S.dve(lambda e, t1=t1, zin_t=zin_t, o=o, ct=ct: e.tensor_tensor(t1, zin_t, biasb[:, o, ct * 512:(ct + 1) * 512], op=ALU.mult), reads=[pk_[2], "biasb"], writes=[hk[4]])
                        S.dve(lambda e, t2=t2, t1=t1, psY=psY: e.tensor_tensor(t2, psY, t1, op=ALU.add), reads=[pkY, hk[4]], writes=[hk[5]])
                        S.pool(lambda e, res=res, t2=t2, gate_t=gate_t: e.tensor_tensor(res, t2, gate_t, op=ALU.mult), reads=[hk[5], pk_[3]], writes=[pk_[0]])
                        if o == 0:
                            S.dma(Z2T[tch * 128:(tch + 1) * 128, ct * 512:(ct + 1) * 512], res, reads=[pk_[0]], writes=["Z2T"], eng="gpsimd")
                        else:
                            pt = PS[7].bitcast(BF16)
                            for j in range(4):
                                S.pe(lambda e, res=res, j=j: e.transpose(pt[:, j * 128:(j + 1) * 128], res[:, j * 128:(j + 1) * 128], identb), reads=[pk_[0], "identb"], writes=["PS7"])
                            t4_ = tch % 4
                            S.act(lambda e, t4_=t4_: e.copy(ZST[:, :, t4_ * 128:(t4_ + 1) * 128], pt[:, 0:512].rearrange("p (j t) -> p j t", j=4)), reads=["PS7"], writes=["ZST"])
                            if t4_ == 3:
                                tg = tch // 4
                                S.dma(ZH[ct * 512:(ct + 1) * 512, tg * 512:(tg + 1) * 512].rearrange("(j p) t -> p j t", p=128), ZST, reads=["ZST"], writes=["ZH"], eng="gpsimd")
            st_h.__exit__(None, None, None)
            S.barrier()

        if "G" in stages:
            st_g = contextlib.ExitStack(); st_g.__enter__()
            def sbg(name, shape, dt=F32):
                return st_g.enter_context(nc.sbuf_tensor("sG_" + name, list(shape), dt)).ap()
            NCH = LTOT // 128
            tri = sbg("tri", [128, 4, 128]); S.dma(tri, tri_d, writes=["tri"])
            TRI_LE, TRI_GE, TRI_GT, TRI_LT = tri[:, 0, :], tri[:, 1, :], tri[:, 2, :], tri[:, 3, :]
            gng = sbg("gng", [128, 1]); S.dma(gng, gng_d, writes=["gng"])
            alb = sbg("alb", [128, 16]); dtbb = sbg("dtbb", [128, 16]); negea = sbg("negea", [128, 16])
            S.dma(alb, alog_d.partition_broadcast(128), writes=["alb"]); S.dma(dtbb, dtb_d.partition_broadcast(128), writes=["dtbb"])
            S.act(lambda e: e.activation(negea, alb, AF.Exp), reads=["alb"], writes=["negea"])
            S.dve(lambda e: e.tensor_scalar(negea, negea, -1.0, None, op0=ALU.mult), reads=["negea"], writes=["negea"])
            ABt = sbg("ABt", [128, NCH, 32]); S.dma(ABt, ABs.rearrange("(c p) n -> p c n", p=128), reads=["ABs"], writes=["ABt"])
            X = sbg("X", [128, NCH, 16])
            for col in range(16):
                S.dve(lambda e, col=col: e.tensor_scalar(X[:, :, col], ABt[:, :, col], dtbb[:, col:col + 1], None, op0=ALU.add), reads=["ABt", "dtbb"], writes=["X"])
            S.act(lambda e: e.activation(X, X, AF.Exp), reads=["X"], writes=["X"])
            S.act(lambda e: e.activation(X, X, AF.Ln, bias=1.0), reads=["X"], writes=["X"])
            Gg = sbg("Gg", [128, 2, NCH, 8]); BETA = sbg("BETA", [128, 2, NCH, 8]); GC = sbg("GC", [128, 2, NCH, 8]); GLt = sbg("GLt", [128, 2, NCH, 8])
            EG = sbg("EG", [128, 2, NCH, 8]); NEG = sbg("NEG", [128, 2, NCH, 8]); EKD = sbg("EKD", [128, 2, NCH, 8]); EGL = sbg("EGL", [128, 2, NCH, 8])
            for col in range(16):
                d_, h_ = col // 8, col % 8
                S.dve(lambda e, col=col, d_=d_, h_=h_: e.tensor_scalar(Gg[:, d_, :, h_], X[:, :, col], negea[:, col:col + 1], None, op0=ALU.mult), reads=["X", "negea"], writes=["Gg"])
            for d_ in range(2):
                S.act(lambda e, d_=d_: e.activation(BETA[:, d_, :, :], ABt[:, :, 16 + 8 * d_:24 + 8 * d_], AF.Sigmoid), reads=["ABt"], writes=["BETA"])
                gflat = Gg[:, d_, :, :].rearrange("p c h -> p (c h)")
                S.pe(lambda e, d_=d_, gflat=gflat: e.matmul(PS[0][:, :NCH * 8], lhsT=(TRI_LE if d_ == 0 else TRI_GE), rhs=gflat, start=True, stop=True), reads=["Gg", "tri"], writes=["PS0"])
                S.dve(lambda e, d_=d_: e.tensor_copy(GC[:, d_, :, :].rearrange("p c h -> p (c h)"), PS[0][:, :NCH * 8]), reads=["PS0"], writes=["GC"])
                S.pe(lambda e, d_=d_, gflat=gflat: e.matmul(PS[1][:, :NCH * 8], lhsT=onesf, rhs=gflat, start=True, stop=True), reads=["Gg", "onesf"], writes=["PS1"])
                S.dve(lambda e, d_=d_: e.tensor_copy(GLt[:, d_, :, :].rearrange("p c h -> p (c h)"), PS[1][:, :NCH * 8]), reads=["PS1"], writes=["GLt"])
            S.act(lambda e: e.activation(EG, GC, AF.Exp), reads=["GC"], writes=["EG"])
            S.dve(lambda e: e.tensor_scalar(NEG, EG, -1.0, None, op0=ALU.mult), reads=["EG"], writes=["NEG"])
            S.dve(lambda e: e.tensor_tensor(EKD, GLt, GC, op=ALU.subtract), reads=["GLt", "GC"], writes=["EKD"])
            S.act(lambda e: e.activation(EKD, EKD, AF.Exp), reads=["EKD"], writes=["EKD"])
            S.act(lambda e: e.activation(EGL, GLt, AF.Exp), reads=["GLt"], writes=["EGL"])
            if dbg:
                for nm, t_ in (("Gg", Gg), ("BETA", BETA)):
                    dbg_outs[nm] = nc.dram_tensor("dbg_" + nm, [128, 2, NCH, 8], F32, kind="ExternalOutput").ap()
                    S.dma(dbg_outs[nm], t_, reads=[nm], writes=["dbg_" + nm])
            S8 = sbg("S8", [128, 8, 128])
            KB = [sbg(f"kT8_{i}", [128, 8, 128]) for i in range(2)]; VB = [sbg(f"vT8_{i}", [128, 8, 128]) for i in range(2)]; QB_ = [sbg(f"qT8_{i}", [128, 8, 128]) for i in range(2)]
            O8 = [sbg(f"O8_{i}", [128, 8, 128]) for i in range(2)]; OFt = [sbg(f"OFt_{i}", [128, 8, 128]) for i in range(2)]
            ZGt = [sbg(f"ZGt_{i}", [128, 8, 128], BF16) for i in range(2)]; OGc = [sbg(f"OGc_{i}", [128, 8, 128], BF16) for i in range(2)]
            ONn = sbg("ONn", [128, 8, 128]); SQn = sbg("SQn", [128, 8, 128]); ssn = sbg("ssn", [128, 8])
            NSET = 8
            names = ["gU", "ET", "ETs", "ETi", "Pa", "Pb", "PTa", "PTb", "TT", "kd", "vtok", "R", "vnew", "qkT", "o2s"]
            TS = [{n: sbg(f"{n}_{i}", [128, 128]) for n in names} for i in range(NSET)]
            C.pg_rr = 0
            def next_pg():
                i = C.pg_rr % 8
                C.pg_rr += 1
                return PS[i][:, 0:128], f"PS{i}"

            GSTOP = 99; GPROB = 10 ** 9
            def prob(d, c, h, pi, par):
                isctx = (c >= 32) or (d == 1 and c * 128 >= OWN)
                ts = TS[pi % NSET]; tk = {n: f"{n}_{pi % NSET}" for n in names}
                gcol = Gg[:, d, c, h:h + 1]; bcol = BETA[:, d, c, h:h + 1]; negeg = NEG[:, d, c, h:h + 1]; eg = EG[:, d, c, h:h + 1]
                ekd = EKD[:, d, c, h:h + 1]; egl = EGL[:, d, c, h:h + 1]
                U, Lm, MsT, MiT = (TRI_LE, TRI_GT, TRI_LT, TRI_LE) if d == 0 else (TRI_GE, TRI_LT, TRI_GT, TRI_GE)
                kT = KB[par][:, h, :]; vT = VB[par][:, h, :]; qT = QB_[par][:, h, :]
                kk, vk, qk_ = f"kT8_{par}", f"vT8_{par}", f"qT8_{par}"
                gU, ET, ETs, ETi, TT = ts["gU"], ts["ET"], ts["ETs"], ts["ETi"], ts["TT"]
                pA = PS[h][:, 0:128]; pB = PS[h][:, 128:256]; pk = f"PS{h}"
                S.dve(lambda e: e.tensor_scalar(gU, U, gcol, None, op0=ALU.mult), reads=["tri", "Gg"], writes=[tk["gU"]])
                S.pe(lambda e: e.matmul(pA, lhsT=Lm, rhs=gU, start=True, stop=True), reads=["tri", tk["gU"]], writes=[pk])
                S.act(lambda e: e.activation(ET, pA, AF.Exp), reads=[pk], writes=[tk["ET"]])
                S.pool(lambda e: e.tensor_tensor(ETs, ET, MsT, op=ALU.mult), reads=[tk["ET"], "tri"], writes=[tk["ETs"]])
                yield
                P0 = ts["Pa"]; PT0 = ts["PTa"]
                S.pe(lambda e: e.matmul(pA, lhsT=kT, rhs=kT, start=True, stop=True), reads=[kk], writes=[pk])
                S.dve(lambda e: e.scalar_tensor_tensor(P0, in0=pA, scalar=bcol, in1=ETs, op0=ALU.mult, op1=ALU.mult), reads=[pk, "BETA", tk["ETs"]], writes=[tk["Pa"]])
                yield
                if not isctx:
                    qkT = ts["qkT"]
                    S.pool(lambda e: e.tensor_tensor(ETi, ET, MiT, op=ALU.mult), reads=[tk["ET"], "tri"], writes=[tk["ETi"]])
                    S.pe(lambda e: e.matmul(pA, lhsT=kT, rhs=qT, start=True, stop=True), reads=[kk, qk_], writes=[pk])
                    S.dve(lambda e: e.tensor_tensor(qkT, pA, ETi, op=ALU.mult), reads=[pk, tk["ETi"]], writes=[tk["qkT"]])
                    yield
                S.pe(lambda e: e.transpose(pA, P0, ident), reads=[tk["Pa"], "ident"], writes=[pk])
                S.act(lambda e: e.copy(PT0, pA), reads=[pk], writes=[tk["PTa"]])
                S.pool(lambda e: e.tensor_tensor(TT, ident, P0, op=ALU.subtract), reads=["ident", tk["Pa"]], writes=[tk["TT"]])
                yield
                Pc, PTc, Pck, PTck = P0, PT0, tk["Pa"], tk["PTa"]
                for l in range(1, 7):
                    Pn, PTn = (ts["Pb"], ts["PTb"]) if l % 2 == 1 else (ts["Pa"], ts["PTa"])
                    Pnk, PTnk = (tk["Pb"], tk["PTb"]) if l % 2 == 1 else (tk["Pa"], tk["PTa"])
                    if l < 6:
                        S.pe(lambda e, PTc=PTc, Pc=Pc: e.matmul(pA, lhsT=PTc, rhs=Pc, start=True, stop=True), reads=[PTck, Pck], writes=[pk])
                        S.act(lambda e, Pn=Pn: e.copy(Pn, pA), reads=[pk], writes=[Pnk])
                        yield
                    S.pe(lambda e, PTc=PTc, Pc=Pc: e.matmul(pA, lhsT=Pc, rhs=PTc, start=True, stop=True), reads=[PTck, Pck], writes=[pk])
                    S.dve(lambda e, PTn=PTn: e.tensor_copy(PTn, pA), reads=[pk], writes=[PTnk])
                    yield
                    S.pe(lambda e, PTn=PTn: e.matmul(pA, lhsT=PTn, rhs=TT, start=True, stop=True), reads=[PTnk, tk["TT"]], writes=[pk])
                    S.dve(lambda e: e.tensor_tensor(TT, TT, pA, op=ALU.add), reads=[tk["TT"], pk], writes=[tk["TT"]])
                    yield
                    Pc, PTc, Pck, PTck = Pn, PTn, Pnk, PTnk
                kd, vtok, R_, vnew = ts["kd"], ts["vtok"], ts["R"], ts["vnew"]
                S.pe(lambda e: e.transpose(pA, kT, ident), reads=[kk, "ident"], writes=[pk])
                S.dve(lambda e: e.tensor_scalar(kd, pA, ekd, None, op0=ALU.mult), reads=[pk, "EKD"], writes=[tk["kd"]])
                yield
                S.pe(lambda e: e.transpose(pA, vT, ident), reads=[vk, "ident"], writes=[pk])
                S.act(lambda e: e.copy(vtok, pA), reads=[pk], writes=[tk["vtok"]])
                yield
                Sh = S8[:, h, :]; sk_ = f"S8_{h}"
                S.pe(lambda e: e.matmul(pA, lhsT=kT, rhs=Sh, start=True, stop=True), reads=[kk, sk_], writes=[pk])
                S.dve(lambda e: e.scalar_tensor_tensor(R_, in0=pA, scalar=negeg, in1=vtok, op0=ALU.mult, op1=ALU.add), reads=[pk, "NEG", tk["vtok"]], writes=[tk["R"]])
                yield
                S.pe(lambda e: e.matmul(pA, lhsT=TT, rhs=R_, start=True, stop=True), reads=[tk["TT"], tk["R"]], writes=[pk])
                S.act(lambda e: e.activation(vnew, pA, AF.Identity, scale=bcol), reads=[pk, "BETA"], writes=[tk["vnew"]])
                yield
                if not isctx:
                    o2s = ts["o2s"]
                    S.pe(lambda e: e.matmul(pA, lhsT=qT, rhs=Sh, start=True, stop=True), reads=[qk_, sk_], writes=[pk])
                    S.pe(lambda e: e.matmul(pB, lhsT=qkT, rhs=vnew, start=True, stop=True), reads=[tk["qkT"], tk["vnew"]], writes=[pk])
                    S.act(lambda e: e.copy(o2s, pB), reads=[pk], writes=[tk["o2s"], pk])
                    S.dve(lambda e: e.scalar_tensor_tensor(O8[par][:, h, :], in0=pA, scalar=eg, in1=o2s, op0=ALU.mult, op1=ALU.add), reads=[pk, "EG", tk["o2s"]], writes=[f"O8_{par}", pk])
                    yield
                S.pe(lambda e: e.matmul(pA, lhsT=kd, rhs=vnew, start=True, stop=True), reads=[tk["kd"], tk["vnew"]], writes=[pk])
                S.dve(lambda e: e.scalar_tensor_tensor(Sh, in0=Sh, scalar=egl, in1=pA, op0=ALU.mult, op1=ALU.add), reads=[sk_, "EGL", pk], writes=[sk_])
                yield

            pi = 0
            lat = list(range(32))
            if dbg == "short":
                lat = [0, 1]

            for d in range(2 if GSTOP > 0 else 0):
                S.pool(lambda e: e.memset(S8, 0.0), reads=[f"S8_{h}" for h in range(8)], writes=[f"S8_{h}" for h in range(8)])
                order = ([32, 33] + [c_ for c_ in lat if c_ * 128 < OWN]) if d == 0 else ([33, 32] + lat[::-1])
                for n_, c in enumerate(order):
                    par = n_ % 2
                    isctx = (c >= 32) or (d == 1 and c * 128 >= OWN)
                    S.dma(KB[par], KT[:, c * 128:(c + 1) * 128].rearrange("(h p) t -> p h t", p=128), reads=["KT"], writes=[f"kT8_{par}"])
                    S.dma(VB[par], VT[:, c * 128:(c + 1) * 128].rearrange("(h p) t -> p h t", p=128), reads=["VT"], writes=[f"vT8_{par}"])
                    if not isctx:
                        S.dma(QB_[par], QT[:, c * 128:(c + 1) * 128].rearrange("(h p) t -> p h t", p=128), reads=["QT"], writes=[f"qT8_{par}"])
                    gens = []
                    for h in range(8):
                        gens.append(prob(d, c, h, pi, par))
                        pi += 1
                    while gens:
                        alive = []
                        for g_ in gens:
                            try:
                                next(g_)
                                alive.append(g_)
                            except StopIteration:
                                pass
                        gens = alive
                    if isctx:
                        continue
                    if GSTOP < 4: continue
                    if d == 0:
                        S.dma(OFs[c * 128:(c + 1) * 128], O8[par], reads=[f"O8_{par}"], writes=["OFs"])
                    else:
                        def epilogue(c=c, par=par):
                            oft = OFt[par]; zg = ZGt[par]; ogc = OGc[par]; o8 = O8[par]
                            S.dma(oft, OFs[c * 128:(c + 1) * 128], reads=["OFs"], writes=[f"OFt_{par}"])
                            S.dma(zg, ZG[:, c * 128:(c + 1) * 128].rearrange("(h p) t -> p h t", p=128), reads=["ZG"], writes=[f"ZGt_{par}"])
                            S.pool(lambda e: e.tensor_tensor(o8, o8, oft, op=ALU.add), reads=[f"O8_{par}", f"OFt_{par}"], writes=[f"O8_{par}"])
                            S.dve(lambda e: e.tensor_tensor(SQn, o8, o8, op=ALU.mult), reads=[f"O8_{par}"], writes=["SQn"])
                            S.dve(lambda e: e.tensor_reduce(ssn, SQn, axis=AX.X, op=ALU.add), reads=["SQn"], writes=["ssn"])
                            S.dve(lambda e: e.tensor_scalar(ssn, ssn, 1.0 / 128.0, EPS, op0=ALU.mult, op1=ALU.add), reads=["ssn"], writes=["ssn"])
                            S.act(lambda e: e.activation(ssn, ssn, AF.Ln), reads=["ssn"], writes=["ssn"])
                            S.act(lambda e: e.activation(ssn, ssn, AF.Exp, scale=-0.5), reads=["ssn"], writes=["ssn"])
                            for h in range(8):
                                S.dve(lambda e, h=h: e.tensor_scalar(ONn[:, h, :], o8[:, h, :], ssn[:, h:h + 1], None, op0=ALU.mult), reads=[f"O8_{par}", "ssn"], writes=["ONn"])
                                pt_, kt_ = next_pg()
                                S.pe(lambda e, h=h, pt_=pt_: e.transpose(pt_, ONn[:, h, :], ident), reads=["ONn", "ident"], writes=[kt_])
                                S.dve(lambda e, h=h, pt_=pt_: e.scalar_tensor_tensor(ogc[:, h, :], in0=pt_, scalar=gng[:, 0:1], in1=zg[:, h, :], op0=ALU.mult, op1=ALU.mult), reads=[kt_, "gng", f"ZGt_{par}"], writes=[f"OGc_{par}"])
                            S.dma(OG[:, c * 128:(c + 1) * 128].rearrange("(h p) t -> p h t", p=128), ogc, reads=[f"OGc_{par}"], writes=["OG"])
                        if GSTOP >= 5: epilogue()
            st_g.__exit__(None, None, None)
            S.barrier()

        if "T" in stages:
            st_t = contextlib.ExitStack(); st_t.__enter__()
            sb = alloc_main(st_t)
            H, XM, ACT, TMP = C.H, C.XM, C.ACT, C.TMP
            ZHt = sb("ZHt", [128, 8, NT], BF16); OGt = sb("OGt", [128, 8, NT], BF16); MRG = sb("MRG", [128, KC, NT], BF16)
            GATES = ACT
            ttiles = [t * NT for t in range(OWN // NT)]
            if dbg == "short":
                ttiles = ttiles[:1]
            def do_tail(t0):
                T = NT
                S.dma(H, H1T[:, t0:t0 + T].rearrange("(k p) t -> p k t", p=128), reads=["H1T"], writes=["H"])
                S.dma(ZHt, ZH[:, t0:t0 + T].rearrange("(k p) t -> p k t", p=128), reads=["ZH"], writes=["ZHt"])
                S.dma(OGt, OG[:, t0:t0 + T].rearrange("(k p) t -> p k t", p=128), reads=["OG"], writes=["OGt"])
                norm_mod(T, 1, 0)
                def ev_gate(ci, ps, pk):
                    S.act(lambda e, ps=ps, ci=ci: e.activation(GATES[:, ci, :T], ps[:, :T], AF.Sigmoid), reads=[pk], writes=["ACT"])
                linear(T, "w_in", KC, GATE_OFF, 2 * D, ev_gate)
                for fb in range(4):
                    wbuf, wkey = next_wb()
                    wa = wbuf[:, 0:4096].rearrange("p (k n) -> p k n", k=8); wb_ = wbuf[:, 4096:8192].rearrange("p (k n) -> p k n", k=8)
                    S.dma(wa, Wb["hy_out"][:, fb * 512:(fb + 1) * 512].rearrange("(k p) n -> p k n", p=128), reads=["Wb_hy_out"], writes=[wkey])
                    S.dma(wb_, Wb["gdn_out"][:, fb * 512:(fb + 1) * 512].rearrange("(k p) n -> p k n", p=128), reads=["Wb_gdn_out"], writes=[wkey])
                    for jj in range(4):
                        ci = fb * 4 + jj
                        psA, pkA = next_ps(); psB, pkB = next_ps()
                        for k in range(8):
                            S.pe(lambda e, psA=psA, wa=wa, k=k, jj=jj: e.matmul(psA[:, :T], lhsT=wa[:, k, jj * 128:(jj + 1) * 128], rhs=ZHt[:, k, :T], start=(k == 0), stop=(k == 7)), reads=[wkey, "ZHt"], writes=[pkA])
                        for k in range(8):
                            S.pe(lambda e, psB=psB, wb_=wb_, k=k, jj=jj: e.matmul(psB[:, :T], lhsT=wb_[:, k, jj * 128:(jj + 1) * 128], rhs=OGt[:, k, :T], start=(k == 0), stop=(k == 7)), reads=[wkey, "OGt"], writes=[pkB])
                        t1 = TMP[0]; t2 = TMP[1]
                        S.dve(lambda e, psA=psA, ci=ci: e.tensor_tensor(t1[:, :T], GATES[:, ci, :T], psA[:, :T], op=ALU.mult), reads=["ACT", pkA], writes=["tmp0"])
                        S.dve(lambda e, psB=psB, ci=ci: e.tensor_tensor(t2[:, :T], GATES[:, KC + ci, :T], psB[:, :T], op=ALU.mult), reads=["ACT", pkB], writes=["tmp1"])
                        S.pool(lambda e, ci=ci: e.tensor_tensor(MRG[:, ci, :T], t1[:, :T], t2[:, :T], op=ALU.add), reads=["tmp0", "tmp1"], writes=["MRG"])
                def ev_o(ci, ps, pk):
                    S.dve(lambda e, ps=ps, ci=ci: e.scalar_tensor_tensor(H[:, ci, :T], in0=ps[:, :T], scalar=Gv[:, 1, ci, 0:1], in1=H[:, ci, :T], op0=ALU.mult, op1=ALU.add),
                          reads=[pk, "H", "Gv"], writes=["H"])
                linear(T, "w_o", KC, 0, D, ev_o, xsrc=MRG, xkey="MRG")
                if dbg:
                    S.dma(dbg_outs["h2"][:, t0:t0 + T].rearrange("(k p) t -> p k t", p=128), H, reads=["H"], writes=["dbg_h2"])
                ffn(T, 1, 2, 0)
                pss = PS[6]
                for k in range(KC):
                    S.act(lambda e, k=k: e.activation(ACT[:, k, :T], H[:, k, :T], AF.Square), reads=["H"], writes=["ACT"])
                for k in range(KC):
                    S.pe(lambda e, k=k: e.matmul(pss[:, :T], lhsT=onesb, rhs=ACT[:, k, :T], start=(k == 0), stop=(k == KC - 1)), reads=["ACT", "onesb"], writes=["PS6"])
                rs = TMP[5]
                S.dve(lambda e: e.tensor_scalar(rs[:, :T], pss[:, :T], 1.0 / D, EPS, op0=ALU.mult, op1=ALU.add), reads=["PS6"], writes=["tmp5"])
                S.act(lambda e: e.activation(rs[:, :T], rs[:, :T], AF.Ln), reads=["tmp5"], writes=["tmp5"])
                S.act(lambda e: e.activation(rs[:, :T], rs[:, :T], AF.Exp, scale=-0.5), reads=["tmp5"], writes=["tmp5"])
                for k in range(KC):
                    S.dve(lambda e, k=k: e.scalar_tensor_tensor(H[:, k, :T], in0=H[:, k, :T], scalar=fg[:, k:k + 1], in1=rs[:, :T], op0=ALU.mult, op1=ALU.mult), reads=["H", "tmp5", "fg"], writes=["H"])
                S.dma(out_d[:, t0:t0 + T].rearrange("(k p) t -> p k t", p=128), H, reads=["H"], writes=["outT"])
            if dbg:
                dbg_outs["h2"] = nc.dram_tensor("dbg_h2", [D, L], F32, kind="ExternalOutput").ap()
            for t0 in ttiles:
                do_tail(t0)
            st_t.__exit__(None, None, None)
            S.barrier()
        C.final_keys = []
        if dbg and "G" in stages:
            dbg_outs["OG"] = nc.dram_tensor("dbg_OG", [GW, L], BF16, kind="ExternalOutput").ap()
            S.dma(dbg_outs["OG"], OG, reads=["OG"], writes=["dbg_OG"])
            dbg_outs["OFs"] = nc.dram_tensor("dbg_OFs", [L, 8, 128], F32, kind="ExternalOutput").ap()
            S.dma(dbg_outs["OFs"], OFs, reads=["OFs"], writes=["dbg_OFs"])
        if dbg and "H" in stages:
            for nm in ("Z2T", "ZH"):
                a = C.scr[nm]
                dbg_outs[nm] = nc.dram_tensor("dbg_" + nm, list(a.shape), BF16, kind="ExternalOutput").ap()
                S.dma(dbg_outs[nm], a, reads=[nm], writes=["dbg_" + nm])
        if dbg and "F" in stages:
            for nm in ("KCs", "KSs"):
                a = C.scr[nm]
                dbg_outs[nm] = nc.dram_tensor("dbg_" + nm, list(a.shape), F32, kind="ExternalOutput").ap()
                S.dma(dbg_outs[nm], a, reads=[nm], writes=["dbg_" + nm])
        if dbg and "A" in stages:
            for nm in ("H1T", "QT", "KT", "VT", "ABs"):
                a = C.scr[nm]
                dbg_outs[nm] = nc.dram_tensor("dbg_" + nm, list(a.shape), F32, kind="ExternalOutput").ap()
                S.dma(dbg_outs[nm], a, reads=[nm], writes=["dbg_" + nm])
            for nm in ("HYT", "ZG"):
                a = C.scr[nm]
                if nm in ext: continue
                dbg_outs[nm] = nc.dram_tensor("dbg_" + nm, list(a.shape), BF16, kind="ExternalOutput").ap()
                S.dma(dbg_outs[nm], a, reads=[nm], writes=["dbg_" + nm])
        S.emit(final_keys=[k for k in S.last_w if isinstance(k, str) and (k.startswith("dbg_") or k == "outT")])
    return nc, C


def _prep_core(inputs, b, rev=False):
    f = np.float32
    g = lambda k: np.asarray(inputs[k])
    m = {}
    xs = g("x")[b]; cs = g("ctx")[b]
    if rev:
        xs = xs[::-1]; cs = cs[::-1]
    m["xT"] = np.ascontiguousarray(xs.T, dtype=f)
    m["ctxT"] = np.ascontiguousarray(cs.T, dtype=f)
    s = np.stack([g("c")[b], g("c_ctx")], 0)
    m["sT"] = np.ascontiguousarray(s.reshape(2, KC, 128).transpose(2, 1, 0), dtype=f)
    m["w_mod"] = np.ascontiguousarray(g("w_mod")[0], dtype=f)
    m["b_mod"] = np.ascontiguousarray(g("b_mod")[0].reshape(9 * KC, 128).T, dtype=f)
    m["norm_g"] = np.ascontiguousarray(g("norm_g")[0].reshape(3 * KC, 128).T, dtype=f)
    m["fin_g"] = np.ascontiguousarray(g("final_norm_g").reshape(KC, 128).T, dtype=f)
    for k in ("ffn1_wgu", "ffn1_wd", "ffn2_wgu", "ffn2_wd", "w_in", "hy_out", "gdn_out", "w_o"):
        m[k] = np.ascontiguousarray(g(k)[0], dtype=f)
    hcw = g("hy_conv_w")[0]; gcw = g("gdn_conv_w")[0]
    if rev:
        hcw = hcw[::-1]; gcw = gcw[::-1]
        wi = m["w_in"].copy()
        ab = wi[:, AB_OFF:AB_OFF + 32].reshape(-1, 2, 2, 8)[:, :, ::-1, :].reshape(-1, 32)
        wi[:, AB_OFF:AB_OFF + 32] = ab
        m["w_in"] = wi
    m["hy_cw"] = np.ascontiguousarray(hcw.reshape(3, 24, 128).transpose(2, 1, 0), dtype=f)
    m["hy_cb"] = np.ascontiguousarray(g("hy_conv_b")[0].reshape(24, 128).T, dtype=f)
    m["g_cw"] = np.ascontiguousarray(gcw.reshape(3, 24, 128).transpose(2, 1, 0), dtype=f)
    return m


_CONST = {}
def _consts():
    if _CONST:
        return _CONST
    import ml_dtypes
    f = np.float32
    Lq = 4096; N = 8192
    pos = np.arange(Lq, dtype=f)
    t = (pos / f(Lq))[:, None].astype(f)
    fb = np.linspace(1e-4, 15, 16, dtype=f)
    ang = (f(2 * math.pi) * t * fb).astype(f)
    z = np.concatenate([t, np.cos(ang), -np.sin(ang)], -1).astype(f)
    _CONST["zT"] = np.ascontiguousarray(z.T)
    _CONST["deltas"] = np.abs(np.linspace(math.log(1e-2) / 1.5, math.log(1e-2) / 0.3, 1024, dtype=f)).reshape(1, 1024).astype(f)
    _CONST["that"] = np.ascontiguousarray((pos / f(Lq)).reshape(32, 128).T)
    tt = np.arange(Lq, dtype=np.int64)[:, None]; ff = np.arange(Lq, dtype=np.int64)[None, :]
    m = ((2 * ff + 1) * tt) % (2 * N)
    th = (2.0 * np.pi / (2 * N)) * m.astype(np.float64)
    Cm = np.cos(th).astype(ml_dtypes.bfloat16); Sm = np.sin(th).astype(ml_dtypes.bfloat16)
    del th, m
    def fwd_tile(M):
        return np.ascontiguousarray(M.reshape(32, 128, 32, 128).transpose(2, 1, 0, 3))
    def inv_tile(M):
        return np.ascontiguousarray(M.reshape(32, 128, 32, 128).transpose(0, 3, 2, 1))
    _CONST["CFt"] = fwd_tile(Cm); _CONST["SFt"] = fwd_tile(Sm); _CONST["CIt"] = inv_tile(Cm); _CONST["SIt"] = inv_tile(Sm)
    return _CONST


def _prep_core2(inputs, b, m, rev=False):
    f = np.float32
    g = lambda k: np.asarray(inputs[k])
    m.update(_consts())
    mb0 = np.ones((128, 2), f); mb0[0, 0 if rev else 1] = 0.0
    m["mb0"] = mb0
    m["f_w1"] = np.ascontiguousarray(g("hy_f_w1")[0], dtype=f); m["f_w2"] = np.ascontiguousarray(g("hy_f_w2")[0], dtype=f)
    w3 = g("hy_f_w3")[0]
    if rev:
        w3 = w3.reshape(64, 2, 2, 1024)[:, :, ::-1, :].reshape(64, 4096)
    m["f_w3"] = np.ascontiguousarray(w3, dtype=f)
    m["f_b1"] = np.ascontiguousarray(g("hy_f_b1")[0].reshape(64, 1), dtype=f); m["f_b2"] = np.ascontiguousarray(g("hy_f_b2")[0].reshape(64, 1), dtype=f)
    m["f_freq"] = np.ascontiguousarray(g("hy_sin_freq")[0].reshape(64, 1), dtype=f)
    m["hy_bias"] = np.ascontiguousarray(g("hy_bias")[0], dtype=f)
    a = np.arange(128)
    tri = np.stack([a[:, None] <= a[None, :], a[:, None] >= a[None, :], a[:, None] > a[None, :], a[:, None] < a[None, :]], 1).astype(f)
    m["tri"] = np.ascontiguousarray(tri)
    al = g("gdn_a_log")[0]; db = g("gdn_dt_bias")[0]
    if rev:
        al = al[::-1]; db = db[::-1]
    m["a_log"] = np.ascontiguousarray(al.reshape(1, 16), dtype=f)
    m["dt_bias"] = np.ascontiguousarray(db.reshape(1, 16), dtype=f)
    m["gdn_ng"] = np.ascontiguousarray(g("gdn_norm_g")[0].reshape(128, 1), dtype=f)
    return m


_PROG = {}

def kernel(**inputs):
    if "nc" not in _PROG:
        nc = bass.Bass("TRN2", target_bir_lowering=False)
        nc, C = build_program(nc, dbg=False)
        _PROG["nc"] = nc; _PROG["din"] = set(C.din)
    nc = _PROG["nc"]
    B = np.asarray(inputs["x"]).shape[0]
    in_maps = []
    percore = {}
    shared = {}
    for r in range(8):
        b, rev = r // 2, bool(r % 2)
        m = _prep_core2(inputs, b, _prep_core(inputs, b, rev), rev)
        m = {k: v for k, v in m.items() if k in _PROG["din"]}
        if not rev:
            for k in ("w_mod", "ffn1_wgu", "ffn1_wd", "ffn2_wgu", "ffn2_wd", "hy_out", "gdn_out", "w_o"):
                m[k] = shared.setdefault(k, m[k])
        else:
            for k in shared:
                m[k] = shared[k]
        in_maps.append(m)
    res = run_bass_kernel_spmd(nc, in_maps, core_ids=list(range(8)))
    out = np.empty((B, L, D), np.float32)
    for b in range(B):
        out[b, :OWN] = res.results[2 * b]["outT"].T
        out[b, OWN:] = res.results[2 * b + 1]["outT"].T[::-1]
    return out
```
